# Optimizing a Trainium2 kernel written in Bass

```python
import math
import jax, jax.numpy as jnp
from jax import lax
import numpy as np

D_MODEL = 1024
BATCH = 16
SEQ = 2048
DEPTH = 4

CTX_LEN = 256
GRID_W = 64
ATTN_HEADS = 4
QK_DIM = 64
V_DIM = 2 * QK_DIM
ATTN_WIDTH = ATTN_HEADS * V_DIM
QK_WIDTH = ATTN_HEADS * 2 * QK_DIM
CHUNK = 128
SG_GROUPS = 4
SG_GROUP_DIM = 128
SG_WIDTH = SG_GROUPS * SG_GROUP_DIM
D_FF = ((8 * D_MODEL // 3 + 255) // 256) * 256
Q_BLOCK = 128
ROPE_THETA = 10000.0
EPS = 1e-6

K_OFF = 0
V_OFF = K_OFF + QK_WIDTH
Q_OFF = V_OFF + ATTN_WIDTH
U_OFF = Q_OFF + QK_WIDTH
SGV_OFF = U_OFF + SG_WIDTH
GATE_OFF = SGV_OFF + SG_WIDTH
IN_WIDTH = GATE_OFF + 2 * D_MODEL

kernel_name = 'hybrid_diffattn_spatialgate_dit_block'


def rms_norm(x, g):
    xf = x.astype(jnp.float32)
    y = xf * lax.rsqrt(jnp.mean(xf * xf, axis=-1, keepdims=True) + EPS)
    return (y * g.astype(jnp.float32)).astype(x.dtype)


def layer_norm(x, g):
    xf = x.astype(jnp.float32)
    mu = jnp.mean(xf, axis=-1, keepdims=True)
    var = jnp.mean(jnp.square(xf - mu), axis=-1, keepdims=True)
    return ((xf - mu) * lax.rsqrt(var + EPS) * g.astype(jnp.float32)).astype(x.dtype)


def modulate(h, shift, scale):
    return h * (1 + scale) + shift


def axial_rope_tables(n_tokens, dtype):
    rows = n_tokens // GRID_W
    row = jnp.broadcast_to(jnp.arange(rows)[:, None], (rows, GRID_W)).reshape(-1).astype(jnp.float32)
    col = jnp.broadcast_to(jnp.arange(GRID_W)[None, :], (rows, GRID_W)).reshape(-1).astype(jnp.float32)
    half = QK_DIM // 2
    inv = 1.0 / (ROPE_THETA ** (jnp.arange(0, half, 2, dtype=jnp.float32) / half))
    ang_r = row[:, None] * inv[None, :]
    ang_c = col[:, None] * inv[None, :]
    return (jnp.cos(ang_r).astype(dtype), jnp.sin(ang_r).astype(dtype),
            jnp.cos(ang_c).astype(dtype), jnp.sin(ang_c).astype(dtype))


def rotate(x, cos, sin):
    x1, x2 = jnp.split(x, 2, axis=-1)
    return jnp.concatenate([x1 * cos - x2 * sin, x2 * cos + x1 * sin], axis=-1)


def apply_rope2d(x, tables):
    cr, sr, cc, sc = [t[None, :, None, None, :] for t in tables]
    half = QK_DIM // 2
    return jnp.concatenate([rotate(x[..., :half], cr, sr), rotate(x[..., half:], cc, sc)], axis=-1)


def diff_softmax_attend(q, k, v, lam):
    s = jnp.einsum('bqhmd,bkhmd->bhmqk', q, k).astype(jnp.float32) * (QK_DIM ** -0.5)
    p = jax.nn.softmax(s, axis=-1)
    p = p[:, :, 0] - lam * p[:, :, 1]
    return jnp.einsum('bhqk,bkhd->bqhd', p.astype(v.dtype), v)


def latent_diff_attention(q, k_all, v_all, lam):
    B, S = q.shape[0], q.shape[1]
    nb = S // Q_BLOCK
    qb = q.reshape(B, nb, Q_BLOCK, ATTN_HEADS, 2, QK_DIM).swapaxes(0, 1)
    out = lax.map(lambda blk: diff_softmax_attend(blk, k_all, v_all, lam), qb)
    return out.swapaxes(0, 1).reshape(B, S, ATTN_HEADS, V_DIM)


def diff_head_out(o, subln_g, lam_init):
    B, T = o.shape[0], o.shape[1]
    return (rms_norm(o, subln_g) * (1.0 - lam_init)).reshape(B, T, ATTN_WIDTH)


def spatial_gating(u, v, norm_g, w_s, b_s):
    B, T = v.shape[0], v.shape[1]
    vc = layer_norm(v, norm_g).reshape(B, T // CHUNK, CHUNK, SG_GROUPS, SG_GROUP_DIM)
    mixed = jnp.einsum('gpq,bnqgd->bnpgd', w_s, vc) + b_s.T[None, None, :, :, None]
    return u * mixed.reshape(B, T, SG_WIDTH)


def merge_branches(attn, sg, gates, w_a, w_sg, w_o):
    ga, gb = jnp.split(jax.nn.sigmoid(gates), 2, axis=-1)
    return (ga * (attn @ w_a) + gb * (sg @ w_sg)) @ w_o


def swiglu(h, w_in, w_out):
    a, b = jnp.split(h @ w_in, 2, axis=-1)
    return (jax.nn.silu(a) * b) @ w_out


def qk_heads(p):
    return p.reshape(p.shape[0], p.shape[1], ATTN_HEADS, 2, QK_DIM)


def v_heads(p):
    return p.reshape(p.shape[0], p.shape[1], ATTN_HEADS, V_DIM)


def setup_inputs(seed: int = 0) -> dict:
    key = jax.random.key(seed)
    ks = jax.random.split(key, 24)
    f32 = jnp.float32
    nrm = lambda k, shape, s: jax.random.normal(k, shape, f32) * s
    return {
        'x': nrm(ks[0], (BATCH, SEQ, D_MODEL), 1.0),
        'c': nrm(ks[1], (BATCH, D_MODEL), 1.0),
        'ctx': nrm(ks[2], (BATCH, CTX_LEN, D_MODEL), 1.0),
        'c_ctx': nrm(ks[3], (D_MODEL,), 1.0),
        'ada_w': nrm(ks[4], (DEPTH, D_MODEL, 6 * D_MODEL), 0.5 * D_MODEL ** -0.5),
        'ada_b': nrm(ks[5], (DEPTH, 6 * D_MODEL), 0.02),
        'norm1_g': 1.0 + nrm(ks[6], (DEPTH, D_MODEL), 0.02),
        'w_in': nrm(ks[7], (DEPTH, D_MODEL, IN_WIDTH), D_MODEL ** -0.5),
        'lambda_q1': nrm(ks[8], (DEPTH, QK_DIM), 0.1),
        'lambda_k1': nrm(ks[9], (DEPTH, QK_DIM), 0.1),
        'lambda_q2': nrm(ks[10], (DEPTH, QK_DIM), 0.1),
        'lambda_k2': nrm(ks[11], (DEPTH, QK_DIM), 0.1),
        'subln_g': 1.0 + nrm(ks[12], (DEPTH, V_DIM), 0.02),
        'sg_norm_g': 1.0 + nrm(ks[13], (DEPTH, SG_WIDTH), 0.02),
        'sg_w': nrm(ks[14], (DEPTH, SG_GROUPS, CHUNK, CHUNK), CHUNK ** -0.5),
        'sg_b': 1.0 + nrm(ks[15], (DEPTH, SG_GROUPS, CHUNK), 0.01),
        'w_branch_attn': nrm(ks[16], (DEPTH, ATTN_WIDTH, D_MODEL), ATTN_WIDTH ** -0.5),
        'w_branch_sg': nrm(ks[17], (DEPTH, SG_WIDTH, D_MODEL), SG_WIDTH ** -0.5),
        'w_out': nrm(ks[18], (DEPTH, D_MODEL, D_MODEL), D_MODEL ** -0.5),
        'norm2_g': 1.0 + nrm(ks[19], (DEPTH, D_MODEL), 0.02),
        'w_ffn_in': nrm(ks[20], (DEPTH, D_MODEL, 2 * D_FF), D_MODEL ** -0.5),
        'w_ffn_out': nrm(ks[21], (DEPTH, D_FF, D_MODEL), D_FF ** -0.5),
        'final_g': 1.0 + nrm(ks[22], (D_MODEL,), 0.02),
    }


def reference(x, c, ctx, c_ctx, ada_w, ada_b, norm1_g, w_in, lambda_q1, lambda_k1,
              lambda_q2, lambda_k2, subln_g, sg_norm_g, sg_w, sg_b, w_branch_attn,
              w_branch_sg, w_out, norm2_g, w_ffn_in, w_ffn_out, final_g):
    S = x.shape[1]
    tables = axial_rope_tables(S, x.dtype)
    silu_c = jax.nn.silu(c)
    silu_cc = jax.nn.silu(c_ctx)
    x_l, x_c = x, ctx
    for i in range(DEPTH):
        last = i == DEPTH - 1
        mod_l = (silu_c @ ada_w[i] + ada_b[i])[:, None, :]
        mod_c = silu_cc @ ada_w[i] + ada_b[i]
        sh1_l, sc1_l, g1_l, sh2_l, sc2_l, g2_l = jnp.split(mod_l, 6, axis=-1)
        sh1_c, sc1_c, g1_c, sh2_c, sc2_c, g2_c = jnp.split(mod_c, 6, axis=-1)
        lam_init = 0.8 - 0.6 * math.exp(-0.3 * i)
        lam = (jnp.exp(jnp.sum(lambda_q1[i].astype(jnp.float32) * lambda_k1[i].astype(jnp.float32)))
               - jnp.exp(jnp.sum(lambda_q2[i].astype(jnp.float32) * lambda_k2[i].astype(jnp.float32)))
               + lam_init)

        h_l = modulate(rms_norm(x_l, norm1_g[i]), sh1_l, sc1_l)
        h_c = modulate(rms_norm(x_c, norm1_g[i]), sh1_c, sc1_c)
        p_l = h_l @ w_in[i]
        p_c = h_c @ (w_in[i][:, :Q_OFF] if last else w_in[i])

        k_c = qk_heads(p_c[..., K_OFF:V_OFF])
        v_c = v_heads(p_c[..., V_OFF:Q_OFF])
        k_l = apply_rope2d(qk_heads(p_l[..., K_OFF:V_OFF]), tables)
        v_l = v_heads(p_l[..., V_OFF:Q_OFF])
        q_l = apply_rope2d(qk_heads(p_l[..., Q_OFF:U_OFF]), tables)
        k_all = jnp.concatenate([k_c, k_l], axis=1)
        v_all = jnp.concatenate([v_c, v_l], axis=1)

        attn_l = diff_head_out(latent_diff_attention(q_l, k_all, v_all, lam), subln_g[i], lam_init)
        sg_l = spatial_gating(jax.nn.gelu(p_l[..., U_OFF:SGV_OFF]), jax.nn.gelu(p_l[..., SGV_OFF:GATE_OFF]),
                              sg_norm_g[i], sg_w[i], sg_b[i])
        mix_l = merge_branches(attn_l, sg_l, p_l[..., GATE_OFF:], w_branch_attn[i], w_branch_sg[i], w_out[i])
        x_l = x_l + g1_l * mix_l

        if not last:
            q_c = qk_heads(p_c[..., Q_OFF:U_OFF])
            attn_c = diff_head_out(diff_softmax_attend(q_c, k_c, v_c, lam), subln_g[i], lam_init)
            sg_c = spatial_gating(jax.nn.gelu(p_c[..., U_OFF:SGV_OFF]), jax.nn.gelu(p_c[..., SGV_OFF:GATE_OFF]),
                                  sg_norm_g[i], sg_w[i], sg_b[i])
            mix_c = merge_branches(attn_c, sg_c, p_c[..., GATE_OFF:], w_branch_attn[i], w_branch_sg[i], w_out[i])
            x_c = x_c + g1_c * mix_c

        f_l = swiglu(modulate(rms_norm(x_l, norm2_g[i]), sh2_l, sc2_l), w_ffn_in[i], w_ffn_out[i])
        x_l = x_l + g2_l * f_l
        if not last:
            f_c = swiglu(modulate(rms_norm(x_c, norm2_g[i]), sh2_c, sc2_c), w_ffn_in[i], w_ffn_out[i])
            x_c = x_c + g2_c * f_c

    return rms_norm(x_l, final_g)
```

```python
import math
import numpy as np
import concourse.bass as bass
import concourse.mybir as mybir
from concourse.bass_utils import run_bass_kernel_spmd

F32 = mybir.dt.float32
BF16 = mybir.dt.bfloat16
AF = mybir.ActivationFunctionType
ALU = mybir.AluOpType
AX = mybir.AxisListType

D = 1024
SEQ = 2048
CTX = 256
T = SEQ + CTX
DFF = 2816
NCORES = 8
EPS = 1e-6
BLK = [(0, 256), (256, 512), (768, 512), (1280, 512), (1792, 512)]
NTILE = T // 128
GELU_C = math.sqrt(2.0 / math.pi)
FFN_GROUPS = [(0, 8), (8, 8), (16, 6)]
WSLOT = 2048
NWS = 4


def tile_blk(t):
    return 0 if t < 2 else 1 + (t - 2) // 4


def lam_init(i):
    return 0.8 - 0.6 * math.exp(-0.3 * i)


def layer_tile_specs():
    tl = []
    K_OFF, V_OFF, Q_OFF, U_OFF, SGV_OFF, GATE_OFF = 0, 512, 1024, 1536, 2048, 2560

    def kv(h):
        return ("KV%d" % h, [("w_in", 0, 8, K_OFF + h * 128, 128), ("w_in", 0, 8, V_OFF + h * 128, 128)])

    def q(p):
        return ("Q%d" % p, [("w_in", 0, 8, Q_OFF + p * 256, 256)])

    tl += [kv(0), q(0), kv(1), q(1), kv(2), kv(3)]
    tl += [("U0", [("w_in", 0, 8, U_OFF, 256)]), ("U1", [("w_in", 0, 8, U_OFF + 256, 256)]),
           ("SGV0", [("w_in", 0, 8, SGV_OFF, 256)]), ("SGV1", [("w_in", 0, 8, SGV_OFF + 256, 256)])]
    for jg in range(2):
        for jp in range(2):
            j0 = jg * 4 + jp * 2
            tl.append(("AS%d" % j0, [("w_branch_attn", 0, 4, j0 * 128, 128), ("w_branch_sg", 0, 4, j0 * 128, 128),
                                      ("w_branch_attn", 0, 4, (j0 + 1) * 128, 128),
                                      ("w_branch_sg", 0, 4, (j0 + 1) * 128, 128)]))
            for j in (j0, j0 + 1):
                tl.append(("G%d" % j, [("w_in", 0, 8, GATE_OFF + j * 128, 128),
                                       ("w_in", 0, 8, GATE_OFF + 1024 + j * 128, 128)]))
        for half in range(2):
            tl.append(("WO%d_%d" % (jg, half), [("w_out", jg * 4, 4, half * 512, 512)]))
    for (c0, ng) in FFN_GROUPS:
        for c in range(c0, c0 + ng):
            tl.append(("F%d" % c, [("w_ffn_in", 0, 8, c * 128, 128), ("w_ffn_in", 0, 8, DFF + c * 128, 128)]))
        for qd in range(4):
            tl.append(("FO%d_%d" % (c0, qd), [("w_ffn_out", c0, ng, qd * 256, 256)]))
    return tl


def pack_part(W, k0, nk, col0, ncols):
    blk = W[k0 * 128:(k0 + nk) * 128, col0:col0 + ncols]
    return blk.reshape(nk, 128, ncols).transpose(1, 0, 2).reshape(128, nk * ncols)


def rope_tables():
    t = np.arange(SEQ)
    row = (t // 64).astype(np.float32)
    col = (t % 64).astype(np.float32)
    half = 32
    inv = (1.0 / (np.float32(10000.0) ** (np.arange(0, half, 2, dtype=np.float32) / np.float32(half)))).astype(np.float32)
    ang_r = row[:, None] * inv[None, :]
    ang_c = col[:, None] * inv[None, :]
    C = np.zeros((64, SEQ), np.float32)
    S_ = np.zeros((64, SEQ), np.float32)
    for d in range(64):
        j = d % 16
        a = ang_r[:, j] if d < 32 else ang_c[:, j]
        C[d] = np.cos(a.astype(np.float32))
        S_[d] = np.sin(a.astype(np.float32))
    C = np.concatenate([C, C], 0)
    S_ = np.concatenate([S_, S_], 0)
    return np.stack([C, S_], 0).astype(np.float32)


def perm_matrix():
    P = np.zeros((128, 128), np.float32)
    for fp in range(128):
        d = fp % 32
        if d < 16:
            P[fp + 16, fp] = -1.0
        else:
            P[fp - 16, fp] = 1.0
    return P


class Trk:
    __slots__ = ("w", "r", "excl")

    def __init__(self, fence=None):
        self.w = None
        self.r = dict(fence) if fence else {}
        self.excl = False


class Ring:
    def __init__(self, items):
        self.items = list(items)
        self.i = 0

    def next(self):
        it = self.items[self.i % len(self.items)]
        self.i += 1
        return it


class Prog:
    ENG = ("pe", "act", "dve", "pool", "sp")

    def __init__(self, nc):
        self.nc = nc
        self.q = {e: [] for e in self.ENG}
        self.cnt = {e: 0 for e in self.ENG}
        self.seen = {e: {} for e in self.ENG}
        self.semh = {}
        self.dcnt = {}
        self.trk = {}
        self.region_trk = {}
        self.region_fence = {}

    def sem(self, name):
        if name not in self.semh:
            self.semh[name] = self.nc.alloc_semaphore("s_" + name)
        return self.semh[name]

    def t(self, *key):
        if key not in self.trk:
            self.trk[key] = Trk()
        return self.trk[key]

    def rt(self, region, *key):
        k = (region,) + key
        if k not in self.trk:
            self.trk[k] = Trk(self.region_fence.get(region))
            self.region_trk.setdefault(region, []).append(k)
        return self.trk[k]

    def release(self, region):
        fence = dict(self.region_fence.get(region, {}))
        for k in self.region_trk.get(region, []):
            tr = self.trk.pop(k)
            if tr.w is not None and fence.get(tr.w[0], 0) < tr.w[1]:
                fence[tr.w[0]] = tr.w[1]
            for s, v in tr.r.items():
                if fence.get(s, 0) < v:
                    fence[s] = v
        self.region_trk[region] = []
        self.region_fence[region] = fence

    def merge_fences(self, regions):
        u = {}
        for r in regions:
            for s, v in self.region_fence.get(r, {}).items():
                if u.get(s, 0) < v:
                    u[s] = v
        for r in regions:
            self.region_fence[r] = dict(u)

    def _deps(self, eng, reads, writes):
        need = {}

        def add(s, v):
            if need.get(s, 0) < v:
                need[s] = v

        for t in reads:
            if t.w is not None:
                add(*t.w)
            if t.excl:
                for s, v in t.r.items():
                    if s != eng:
                        add(s, v)
        for t in writes:
            if t.w is not None:
                add(*t.w)
            for s, v in t.r.items():
                add(s, v)
        waits = []
        for s, v in need.items():
            if s == eng and eng == "pe":
                continue
            if self.seen[eng].get(s, 0) >= v:
                continue
            self.seen[eng][s] = v
            waits.append((s, v))
        return waits

    def _mark(self, tok, reads, writes):
        s, v = tok
        for t in reads:
            if t.r.get(s, 0) < v:
                t.r[s] = v
        for t in writes:
            t.w = tok
            t.r = {}

    def op(self, eng, fn, reads=(), writes=()):
        waits = self._deps(eng, reads, writes)
        self.cnt[eng] += 1
        tok = (eng, self.cnt[eng])
        self.q[eng].append(("op", waits, fn, None))
        self._mark(tok, reads, writes)
        return tok

    def dma(self, eng, fn, reads, writes, sem):
        waits = self._deps(eng, reads, writes)
        self.dcnt[sem] = self.dcnt.get(sem, 0) + 16
        tok = (sem, self.dcnt[sem])
        self.q[eng].append(("dma", waits, fn, sem))
        self._mark(tok, reads, writes)
        return tok

    def wait_all(self, eng, trks):
        waits = self._deps(eng, trks, ())
        self.q[eng].append(("wait", waits, None, None))

    def emit(self):
        nc = self.nc
        for e in ("pe", "act", "dve", "pool"):
            self.sem(e)
        for s in list(self.dcnt.keys()):
            self.sem(s)
        with nc.Block() as block:
            engmap = {"pe": block.tensor, "act": block.scalar, "dve": block.vector,
                      "pool": block.gpsimd, "sp": block.sync}
            for name in self.ENG:
                items = self.q[name]
                if not items:
                    continue

                def body(e, items=items, name=name):
                    for kind, waits, fn, sem in items:
                        for (s, v) in waits:
                            e.wait_ge(self.semh[s], v)
                        if kind == "wait":
                            continue
                        if isinstance(fn, tuple):
                            ins = getattr(e, fn[0])(*fn[1], **fn[2])
                        else:
                            ins = fn(e)
                        if kind == "op":
                            ins.then_inc(self.semh[name], 1)
                        else:
                            ins.then_inc(self.semh[sem], 16)

                engmap[name](body)


class Deferred:
    def __init__(self):
        self.chains = []
        self.busy = False

    def add(self, key, stages):
        stages = list(stages)
        self.chains.append(dict(key=key, stages=stages, idx=0, wait=stages[0][0]))

    def tick(self):
        if self.busy or not self.chains:
            return
        self.busy = True
        ch = self.chains[0]
        if ch["wait"] > 0:
            ch["wait"] -= 1
        else:
            while True:
                ch["stages"][ch["idx"]][1]()
                ch["idx"] += 1
                if ch["idx"] >= len(ch["stages"]):
                    self.chains.pop(0)
                    break
                ch["wait"] = ch["stages"][ch["idx"]][0]
                if ch["wait"] > 0:
                    break
        self.busy = False

    def pending(self, key):
        return any(c["key"] == key for c in self.chains)

    def ensure(self, key):
        while self.pending(key):
            self.tick()

    def flush(self):
        while self.chains:
            self.tick()


def I(name, *a, **kw):
    return (name, a, kw)


def mmgroup(out_ap, pairs):
    pairs = list(pairs)

    def fn(e):
        n = len(pairs)
        ins = None
        for i, (l, r) in enumerate(pairs):
            ins = e.matmul(out_ap, lhsT=l, rhs=r, start=(i == 0), stop=(i == n - 1))
        return ins

    return fn


def build_program(depth=4, nb=2, debug=False):
    nc = bass.Bass("TRN2", target_bir_lowering=False)
    specs = layer_tile_specs()
    tile_sizes = [sum(nk * ncols for (_, _, nk, _, ncols) in parts) for (_, parts) in specs]
    tile_offs = np.concatenate([[0], np.cumsum(tile_sizes)]).astype(int)
    E = int(tile_offs[-1])
    NSM = 24 + 32 + 32 + 8 + 192 + 4
    SM_C, SM_N1, SM_N2, SM_FG, SM_AB, SM_SUB = 0, 24, 56, 88, 96, 288

    xin = nc.dram_tensor("xin", [nb, 128, 8, T], F32, kind="ExternalInput").ap()
    yout = nc.dram_tensor("yout", [nb, 128, 8, SEQ], F32, kind="ExternalOutput").ap()
    smalls_d = nc.dram_tensor("smalls", [128, NSM], F32, kind="ExternalInput").ap()
    lams_d = nc.dram_tensor("lams", [1024], F32, kind="ExternalInput").ap()
    sgw_d = nc.dram_tensor("sgw", [depth, 128, 512], F32, kind="ExternalInput").ap()
    sgb_d = nc.dram_tensor("sgb", [depth, 512], F32, kind="ExternalInput").ap()
    sgg_d = nc.dram_tensor("sgg", [depth, 512], F32, kind="ExternalInput").ap()
    rope_d = nc.dram_tensor("rope", [2, 128, SEQ], F32, kind="ExternalInput").ap()
    perm_d = nc.dram_tensor("perm", [128, 128], F32, kind="ExternalInput").ap()
    ident_d = nc.dram_tensor("ident", [128, 128], F32, kind="ExternalInput").ap()
    wada_d = nc.dram_tensor("wada", [depth, 24, 128, 2048], F32, kind="ExternalInput").ap()
    wst_d = nc.dram_tensor("wst", [depth, 128, E], F32, kind="ExternalInput").ap()

    xT = nc.alloc_sbuf_tensor("xT", [128, 8, T], F32)
    hT = nc.alloc_sbuf_tensor("hT", [128, 8, T], BF16)
    arena = nc.alloc_sbuf_tensor("arena", [128, 27648], BF16)
    wring = nc.alloc_sbuf_tensor("wring", [128, NWS, WSLOT], BF16)
    btr = nc.alloc_sbuf_tensor("btr", [128, 4, 512], BF16)
    ftr = nc.alloc_sbuf_tensor("ftr", [128, 4, 512], F32)
    smalls = nc.alloc_sbuf_tensor("smalls_sb", [128, NSM], F32)
    mod = nc.alloc_sbuf_tensor("mod", [128, depth, 48, 3], F32)
    lamt = nc.alloc_sbuf_tensor("lamt", [128, 16], F32)
    scT = nc.alloc_sbuf_tensor("scT", [128, 8, 3], BF16)
    wsT = nc.alloc_sbuf_tensor("wsT", [128, 512], BF16)
    sgbr = nc.alloc_sbuf_tensor("sgbr", [1, 1024], BF16)
    sgbf = nc.alloc_sbuf_tensor("sgbf", [1, 1024], F32)
    sggt = nc.alloc_sbuf_tensor("sggt", [128, 512], F32)
    perm = nc.alloc_sbuf_tensor("perm_sb", [128, 128], BF16)
    ones = nc.alloc_sbuf_tensor("ones_sb", [128, 128], BF16)
    nhalf = nc.alloc_sbuf_tensor("nhalf", [128, 512], BF16)
    stat = nc.alloc_sbuf_tensor("stat", [128, 4, 16], F32)
    identf = nc.alloc_sbuf_tensor("identf", [128, 128], F32)
    onesf = nc.alloc_sbuf_tensor("onesf", [128, 128], F32)
    diag = nc.alloc_sbuf_tensor("diag", [128, 4, 128], F32)
    rtiny = nc.alloc_sbuf_tensor("rtiny", [128, 8], F32)
    epsc = nc.alloc_sbuf_tensor("epsc", [128, 2], BF16)
    banks = [nc.alloc_psum_tensor("bank%d" % i, [128, 512], F32) for i in range(8)]

    P = Prog(nc)

    def KT(s):
        return arena[:, s * 4608: s * 4608 + 2304]

    def VV(s):
        return arena[:, s * 4608 + 2304:(s + 1) * 4608].rearrange("p (t d) -> p t d", d=128)

    QA = arena[:, 9216:18432].rearrange("p (h t) -> p h t", t=T)
    UB = arena[:, 18432:27648].rearrange("p (g t) -> p g t", t=T)
    ropeC = arena[:, 18432:18432 + 2048]
    ropeS = arena[:, 18432 + 2048:18432 + 4096]
    PT = [arena[:, 18432 + 4096 + i * 512:18432 + 4096 + (i + 1) * 512] for i in range(4)]
    AT = [arena[:, 18432 + 6144 + i * 1024:18432 + 6144 + (i + 1) * 1024].bitcast(F32) for i in range(2)]
    QP = [arena[:, 18432 + 6144 + 2048 + m * 512:18432 + 6144 + 2048 + (m + 1) * 512] for m in range(2)]
    SGV = arena[:, 0:9216].rearrange("p (t f) -> p t f", f=512)
    MG = arena[:, 0:9216].rearrange("p (j t) -> p j t", t=T)
    ACTF = arena[:, 0:18432].rearrange("p (c t) -> p c t", t=T)
    OST = arena[:, 18432:18432 + 8192].bitcast(F32).rearrange("p (c t) -> p c t", t=512)

    bt_ring = Ring(range(4))
    ft_ring = Ring(range(4))
    stat_ring = Ring(range(4))
    rs_ring = Ring(range(2))
    bank_all = Ring(range(6))
    DEF = Deferred()
    bank_S = Ring([4, 5])
    bank_B = Ring([6, 7])

    def bk(i):
        tr = P.t("bank", i)
        tr.excl = True
        return tr

    dbg_n = [0]

    def dump(name, ap, shape, dtype, trks):
        if not debug:
            return
        d = nc.dram_tensor(name, list(shape), dtype, kind="ExternalOutput").ap()
        dbg_n[0] += 1
        P.dma("sp", I("dma_start", out=d, in_=ap), tuple(trks), (P.t("dbg", name),), "dbg%d" % dbg_n[0])
        P.wait_all("sp", [P.t("dbg", name)])

    TILES = []
    NFFN = sum(ng + 4 for (_, ng) in FFN_GROUPS)
    NPRE = len(specs) - NFFN
    for jt in range(24):
        TILES.append((wada_d[0, jt], 2048))
    for n in range(nb):
        for l in range(depth):
            side = (n == 0 and l + 1 < depth)
            for ti in range(len(specs)):
                TILES.append((wst_d[l, :, int(tile_offs[ti]):int(tile_offs[ti + 1])], tile_sizes[ti]))
                fi_ = ti - NPRE
                if side and 0 <= fi_ < 24:
                    TILES.append((wada_d[l + 1, fi_], 2048))
    wstate = {"issued": 0, "cur": 0}

    def w_prefetch(upto):
        while wstate["issued"] < min(upto, len(TILES)):
            i = wstate["issued"]
            slot = i % NWS
            src, size = TILES[i]
            P.dma("pool", I("dma_start", out=wring[:, slot, 0:size], in_=src),
                  reads=(), writes=(P.t("W", slot),), sem="w%d" % slot)
            wstate["issued"] += 1

    def w_next(expect_size=None, hold=0):
        i = wstate["cur"]
        w_prefetch(i - hold + NWS)
        wstate["cur"] += 1
        slot = i % NWS
        if expect_size is not None:
            assert TILES[i][1] == expect_size, (i, TILES[i][1], expect_size)
        return wring[:, slot, :], P.t("W", slot)

    P.dma("sp", I("dma_start", out=smalls[:], in_=smalls_d), (), (P.t("smalls"),), "misc0")
    lam_raw = ftr[:, 0:2, :].rearrange("p a b -> p (a b)")
    P.dma("sp", I("dma_start", out=lam_raw, in_=lams_d.partition_broadcast(128)), (),
          (P.t("ft", 0), P.t("ft", 1)), "misc2")
    P.dma("pool", I("dma_start", out=perm[:], in_=perm_d), (), (P.t("perm"),), "misc3")
    P.dma("sp", I("dma_start", out=identf[:], in_=ident_d), (), (P.t("identf"),), "misc4")
    P.op("dve", I("memset", onesf[:], 1.0), (), (P.t("onesf"),))
    P.op("dve", I("memset", epsc[:], EPS), (), (P.t("epsc"),))
    P.op("dve", I("memset", ones[:], 1.0), (), (P.t("ones"),))
    P.op("dve", I("memset", nhalf[:], -0.5), (), (P.t("nhalf"),))
    w_prefetch(NWS)

    lr = lam_raw.rearrange("p (a l d) -> p a l d", a=4, l=4)
    P.op("dve", I("tensor_tensor", out=lr[:, 0], in0=lr[:, 0], in1=lr[:, 1], op=ALU.mult),
         (P.t("ft", 0), P.t("ft", 1)), (P.t("ft", 0),))
    P.op("dve", I("tensor_tensor", out=lr[:, 2], in0=lr[:, 2], in1=lr[:, 3], op=ALU.mult),
         (P.t("ft", 0), P.t("ft", 1)), (P.t("ft", 1),))
    P.op("dve", I("reduce_sum", out=lamt[:, 0:4], in_=lr[:, 0], axis=AX.X), (P.t("ft", 0),), (P.t("lamt"),))
    P.op("dve", I("reduce_sum", out=lamt[:, 4:8], in_=lr[:, 2], axis=AX.X), (P.t("ft", 1),), (P.t("lamt"),))
    P.op("act", I("activation", out=lamt[:, 0:8], in_=lamt[:, 0:8], func=AF.Exp), (P.t("lamt"),), (P.t("lamt"),))
    P.op("dve", I("tensor_tensor", out=lamt[:, 8:12], in0=lamt[:, 4:8], in1=lamt[:, 0:4], op=ALU.subtract),
         (P.t("lamt"),), (P.t("lamt"),))
    for l in range(depth):
        P.op("dve", I("tensor_scalar", out=lamt[:, 8 + l:9 + l], in0=lamt[:, 8 + l:9 + l],
                                                     scalar1=-lam_init(l), scalar2=None, op0=ALU.add),
             (P.t("lamt"),), (P.t("lamt"),))
        P.op("dve", I("tensor_scalar", out=lamt[:, 12 + l:13 + l], in0=smalls[:, SM_SUB + l:SM_SUB + l + 1],
                                                     scalar1=1.0 - lam_init(l), scalar2=None, op0=ALU.mult),
             (P.t("smalls"), P.t("lamt")), (P.t("lamt"),))

    cT = smalls[:, SM_C:SM_C + 24]
    sct = ftr[:, 2, 0:24]
    P.op("act", I("activation", out=sct, in_=cT, func=AF.Tanh, scale=0.5), (P.t("smalls"),), (P.t("ft", 2),))
    P.op("dve", I("tensor_scalar", out=sct, in0=sct, scalar1=0.5, scalar2=0.5, op0=ALU.mult, op1=ALU.add),
         (P.t("ft", 2),), (P.t("ft", 2),))
    for n in range(3):
        P.op("dve", I("tensor_tensor", out=scT[:, :, n], in0=sct[:, n * 8:(n + 1) * 8],
                                                     in1=cT[:, n * 8:(n + 1) * 8], op=ALU.mult),
             (P.t("ft", 2), P.t("smalls")), (P.t("scT"),))

    def ada_items(l, b):
        def tile_item(jt):
            def f():
                wv, wt = w_next(2048)
                wv3 = wv.rearrange("p (k c) -> p k c", c=256)
                for jj in range(2):
                    j = jt * 2 + jj
                    P.op("pe", mmgroup(banks[b][:, j * 3:(j + 1) * 3],
                                       [(wv3[:, kc, jj * 128:(jj + 1) * 128], scT[:, kc, :]) for kc in range(8)]),
                         (wt, P.t("scT")), (bk(b),))
                if jt == 23:
                    post()
            return f

        def post():
            psv = banks[b][:, 0:144].rearrange("p (j n) -> p j n", n=3)
            for n in range(3):
                P.op("dve", I("tensor_tensor", out=mod[:, l, :, n], in0=psv[:, :, n],
                              in1=smalls[:, SM_AB + l * 48:SM_AB + (l + 1) * 48], op=ALU.add),
                     (bk(b), P.t("smalls")), (P.t("mod", l),))
            for n in range(3):
                for (seg, gsrc) in ((1, SM_N1), (4, SM_N2)):
                    P.op("dve", I("scalar_tensor_tensor", out=mod[:, l, seg * 8:(seg + 1) * 8, n],
                                  in0=mod[:, l, seg * 8:(seg + 1) * 8, n], scalar=1.0,
                                  in1=smalls[:, gsrc + l * 8:gsrc + (l + 1) * 8], op0=ALU.add, op1=ALU.mult),
                         (P.t("mod", l), P.t("smalls")), (P.t("mod", l),))
                for seg in (2, 5):
                    P.op("dve", I("tensor_scalar", out=mod[:, l, seg * 8:(seg + 1) * 8, n],
                                  in0=mod[:, l, seg * 8:(seg + 1) * 8, n], scalar1=0.5, scalar2=None, op0=ALU.mult),
                         (P.t("mod", l),), (P.t("mod", l),))
        return [tile_item(jt) for jt in range(24)]

    for f in ada_items(0, bank_all.next()):
        f()

    def mcol(l, seg, fc, v):
        return mod[:, l, seg * 8 + fc, v:v + 1]

    def rstd_stages(src_fn, src_trks, sz, nchunks, inv_n, d0=1, extra=None, eps_add=EPS):
        nt = sz // 128
        b1, b2 = 6, 7
        groups = [list(range(c, min(c + 4, nchunks))) for c in range(0, nchunks, 4)]
        slots = {}

        def sq(grp):
            for c in grp:
                bi = bt_ring.next()
                slots[c] = bi
                P.op("act", I("activation", out=btr[:, bi, 0:sz], in_=src_fn(c), func=AF.Square),
                     (src_trks[c],), (P.t("bt", bi),))

        def mm(grp):
            for c in grp:
                bi = slots[c]

                def fn(e, c=c, bi=bi):
                    ins = None
                    for tt in range(nt):
                        ins = e.matmul(banks[b1][:, tt:tt + 1], lhsT=btr[:, bi, tt * 128:(tt + 1) * 128], rhs=ones[:, 0:1],
                                       start=(c == 0 and tt == 0),
                                       stop=(c == nchunks - 1 and tt == nt - 1 and extra is None),
                                       skip_group_check=True)
                    return ins
                P.op("pe", fn, (P.t("bt", bi), P.t("ones")), (bk(b1),))
            if extra is not None and grp[-1] == nchunks - 1:
                ei, erhs = extra

                def fne(e):
                    ins = None
                    for tt in range(nt):
                        ins = e.matmul(banks[b1][:, tt:tt + 1], lhsT=btr[:, ei, tt * 128:(tt + 1) * 128], rhs=erhs,
                                       start=False, stop=(tt == nt - 1), skip_group_check=True)
                    return ins
                P.op("pe", fne, (P.t("bt", ei), P.t("epsc")), (bk(b1),))

        def fin():
            P.op("dve", I("tensor_scalar", out=rtiny[:, 0:nt], in0=banks[b1][:, 0:nt], scalar1=inv_n, scalar2=eps_add,
                          op0=ALU.mult, op1=ALU.add), (bk(b1),), (P.t("rtiny"),))
            P.op("pool", I("tensor_tensor", out=rtiny[:, 0:nt], in0=rtiny[:, 0:nt], in1=nhalf[:, 0:nt], op=ALU.pow),
                 (P.t("rtiny"), P.t("nhalf")), (P.t("rtiny"),))

        def dg():
            for tt in range(nt):
                P.op("dve", I("tensor_scalar", out=diag[:, tt, :], in0=identf[:], scalar1=rtiny[:, tt:tt + 1], scalar2=None,
                              op0=ALU.mult), (P.t("rtiny"), P.t("identf")), (P.t("diag", tt),))

        def bc():
            def fn2(e):
                ins = None
                for tt in range(nt):
                    ins = e.matmul(banks[b2][:, tt * 128:(tt + 1) * 128], lhsT=onesf[:], rhs=diag[:, tt, :],
                                   start=True, stop=True)
                return ins
            P.op("pe", fn2, tuple(P.t("diag", tt) for tt in range(nt)) + (P.t("onesf"),), (bk(b2),))

        stages = [(d0, lambda: sq(groups[0]))]
        for gi in range(len(groups)):
            last_g = gi == len(groups) - 1

            def st(gi=gi, last_g=last_g):
                mm(groups[gi])
                if not last_g:
                    sq(groups[gi + 1])
                else:
                    fin()
            stages.append((2, st))
        stages.append((2, dg))
        stages.append((1, bc))
        return stages

    def norm_chain(l, n, seg_sh, seg_a, bi_):
        t0, sz = BLK[bi_]
        v = 2 if bi_ == 0 else n
        stages = rstd_stages(lambda c: xT[:, c, t0:t0 + sz], [P.t("x", c, bi_) for c in range(8)], sz, 8, 1.0 / D, d0=0)

        def apply():
            for fc in range(8):
                f2 = ft_ring.next()
                P.op("dve", I("tensor_tensor", out=ftr[:, f2, 0:sz], in0=xT[:, fc, t0:t0 + sz], in1=banks[7][:, 0:sz],
                              op=ALU.mult), (P.t("x", fc, bi_), bk(7)), (P.t("ft", f2),))
                P.op("act", I("activation", out=hT[:, fc, t0:t0 + sz], in_=ftr[:, f2, 0:sz], func=AF.Identity,
                              bias=mcol(l, seg_sh, fc, v), scale=mcol(l, seg_a, fc, v)),
                     (P.t("ft", f2), P.t("mod", l)), (P.t("h", fc, bi_),))
        stages.append((1, apply))
        return stages

    def final_chain(n, bi_):
        t0, sz = BLK[bi_]
        stages = rstd_stages(lambda c: xT[:, c, t0:t0 + sz], [P.t("x", c, bi_) for c in range(8)], sz, 8, 1.0 / D, d0=0)

        def apply():
            otr = P.rt("R2", "OST")
            for fc in range(8):
                P.op("dve", I("scalar_tensor_tensor", out=OST[:, fc, :], in0=xT[:, fc, t0:t0 + sz],
                              scalar=smalls[:, SM_FG + fc:SM_FG + fc + 1], in1=banks[7][:, 0:sz], op0=ALU.mult, op1=ALU.mult),
                     (P.t("x", fc, bi_), P.t("smalls"), bk(7)), (otr,))
            P.dma("sp", I("dma_start", out=yout[n, :, :, t0 - CTX:t0 - CTX + sz], in_=OST), (otr,), (P.t("yout"),), "out")
        stages.append((1, apply))
        return stages

    def proj_fm(b, wcols, wt, src_fn, src_trks, sz, nk):
        P.op("pe", mmgroup(banks[b][:, 0:sz], [(wcols(kc), src_fn(kc)) for kc in range(nk)]),
             (wt,) + tuple(src_trks), (bk(b),))
        DEF.tick()

    def h_src(bi_):
        DEF.ensure(("h", bi_))
        t0, sz = BLK[bi_]
        return (lambda kc: hT[:, kc, t0:t0 + sz]), [P.t("h", kc, bi_) for kc in range(8)]

    def gelu_chain(b, sz, out_ap, out_trks):
        X = banks[b][:, 0:sz]
        f1 = ft_ring.next()
        T1 = ftr[:, f1, 0:sz]
        P.op("act", I("activation", out=T1, in_=X, func=AF.Identity, scale=0.5), (bk(b),), (P.t("ft", f1),))
        P.op("act", I("activation", out=X, in_=X, func=AF.Square), (bk(b),), (bk(b),))
        P.op("dve", I("tensor_scalar", out=X, in0=X, scalar1=0.044715, scalar2=1.0, op0=ALU.mult, op1=ALU.add),
             (bk(b),), (bk(b),))
        P.op("dve", I("tensor_tensor", out=X, in0=X, in1=T1, op=ALU.mult), (bk(b), P.t("ft", f1)), (bk(b),))
        P.op("act", I("activation", out=X, in_=X, func=AF.Tanh, scale=2.0 * GELU_C), (bk(b),), (bk(b),))
        P.op("dve", I("scalar_tensor_tensor", out=out_ap, in0=X, scalar=1.0, in1=T1, op0=ALU.add, op1=ALU.mult),
             (bk(b), P.t("ft", f1)), tuple(out_trks))

    def proj_kv_items(l, hd, slot, ring):
        items = []
        holder = {}

        def get_w():
            if "w" not in holder:
                wv, wt = w_next(2048)
                holder["w"] = (wv.rearrange("p (s k c) -> p s k c", s=2, k=8), wt)
            return holder["w"]

        def k_item(bi_):
            def f():
                wv3, wt = get_w()
                t0, sz = BLK[bi_]
                src_fn, src_trks = h_src(bi_)
                b = ring.next()
                proj_fm(b, lambda kc: wv3[:, 0, kc, :], wt, src_fn, src_trks, sz, 8)
                dst = KT(slot)[:, t0:t0 + sz]
                dtrk = P.rt("R0", "KT", slot, bi_)
                if bi_ == 0:
                    P.op("act", I("activation", out=dst, in_=banks[b][:, 0:sz], func=AF.Copy), (bk(b),), (dtrk,))
                else:
                    rope_evac(b, sz, t0 - CTX, dst, dtrk, ring)
            return f

        def v_item(tiles):
            def f():
                wv3, wt = get_w()
                for t in tiles:
                    DEF.ensure(("h", tile_blk(t)))
                b = ring.next()
                for i, t in enumerate(tiles):
                    P.op("pe", mmgroup(banks[b][:, i * 128:(i + 1) * 128],
                                       [(hT[:, kc, t * 128:(t + 1) * 128], wv3[:, 1, kc, :]) for kc in range(8)]),
                         (wt,) + tuple(P.t("h", kc, tile_blk(t)) for kc in range(8)), (bk(b),))
                nt = len(tiles)
                dst = VV(slot)[:, tiles[0]:tiles[0] + nt, :]
                src = banks[b][:, 0:nt * 128].rearrange("p (t d) -> p t d", d=128)
                P.op("dve", I("tensor_copy", out=dst, in_=src), (bk(b),),
                     (P.rt("R0", "VV", slot, tile_blk(tiles[0])),))
            return f

        for bi_ in range(5):
            items.append(k_item(bi_))
        items.append(v_item([0, 1]))
        for g in range(4):
            items.append(v_item([2 + 4 * g + i for i in range(4)]))
        return items

    def rope_evac(b, sz, lt0, dst, dtrk, ring):
        X = banks[b][:, 0:sz]
        qi = PT_ring.next()
        qtk = P.rt("R2", "PT", qi)
        P.op("act", I("activation", out=PT[qi][:, 0:sz], in_=X, func=AF.Copy), (bk(b),), (qtk,))
        b2 = ring.next()
        P.op("pe", I("matmul", banks[b2][:, 0:sz], lhsT=perm[:], rhs=PT[qi][:, 0:sz], start=True, stop=True),
             (qtk, P.t("perm")), (bk(b2),))
        rtk = P.rt("R2", "ROPE")
        P.op("dve", I("tensor_tensor", out=X, in0=X, in1=ropeC[:, lt0:lt0 + sz], op=ALU.mult), (bk(b), rtk), (bk(b),))
        f1 = ft_ring.next()
        P.op("dve", I("tensor_tensor", out=ftr[:, f1, 0:sz], in0=banks[b2][:, 0:sz], in1=ropeS[:, lt0:lt0 + sz],
                                              op=ALU.mult), (bk(b2), rtk), (P.t("ft", f1),))
        P.op("dve", I("tensor_tensor", out=dst, in0=ftr[:, f1, 0:sz], in1=X, op=ALU.add),
             (P.t("ft", f1), bk(b)), (dtrk,))

    def proj_q_items(l, pair, ring, last):
        items = []
        holder = {}

        def get_w():
            if "w" not in holder:
                wv, wt = w_next(2048)
                holder["w"] = (wv.rearrange("p (k c) -> p k c", c=256), wt)
            return holder["w"]

        def q_item(hh, bi_):
            def f():
                wv3, wt = get_w()
                hd = pair * 2 + hh
                t0, sz = BLK[bi_]
                src_fn, src_trks = h_src(bi_)
                b = ring.next()
                proj_fm(b, lambda kc: wv3[:, kc, hh * 128:(hh + 1) * 128], wt, src_fn, src_trks, sz, 8)
                dst = QA[:, hd, t0:t0 + sz]
                dtrk = P.rt("R1", "QA", hd, bi_)
                if bi_ == 0:
                    P.op("act", I("activation", out=dst, in_=banks[b][:, 0:sz], func=AF.Copy), (bk(b),), (dtrk,))
                else:
                    rope_evac(b, sz, t0 - CTX, dst, dtrk, ring)
            return f

        for hh in range(2):
            for bi_ in range(5):
                if bi_ == 0 and last:
                    continue
                items.append(q_item(hh, bi_))
        return items

    def attention(l, n, hd, slot, last, background):
        qblocks = [1, 2, 3, 4] if last else [0, 1, 2, 3, 4]
        nlam = lamt[:, 8 + l:9 + l]
        subcol = lamt[:, 12 + l:13 + l]
        O = [0, 1]
        R = [2, 3]
        bg = list(background)
        nbound = [len(qblocks)]

        def bg_boundary():
            k = -(-len(bg) // max(1, nbound[0]))
            nbound[0] -= 1
            for _ in range(min(k, len(bg))):
                bg.pop(0)()

        def prep_q(qb_):
            q0_, qs_ = BLK[qb_]
            tr = P.rt("R1", "QA", hd, qb_)
            P.op("pool", I("tensor_copy", out=QP[0][0:64, 0:qs_], in_=QA[0:64, hd, q0_:q0_ + qs_]), (tr,), (P.rt("R2", "QP", 0),))
            P.op("pool", I("tensor_copy", out=QP[1][64:128, 0:qs_], in_=QA[64:128, hd, q0_:q0_ + qs_]), (tr,), (P.rt("R2", "QP", 1),))

        prep_q(qblocks[0])
        for qi_, qb in enumerate(qblocks):
            q0, qs = BLK[qb]
            kchunks = [0, 1] if qb == 0 else list(range(NTILE))
            qtrk = P.rt("R1", "QA", hd, qb)

            def scores(kc):
                res = []
                for m in range(2):
                    b = bank_S.next()
                    P.op("pe", I("matmul", banks[b][:, 0:qs], lhsT=KT(slot)[:, kc * 128:(kc + 1) * 128],
                                 rhs=QP[m][:, 0:qs], start=True, stop=True),
                         (P.rt("R0", "KT", slot, tile_blk(kc)), P.rt("R2", "QP", m)), (bk(b),))
                    pi = PT_ring.next()
                    P.op("act", I("activation", out=PT[pi][:, 0:qs], in_=banks[b][:, 0:qs], func=AF.Exp, scale=0.125, bias=-8.0),
                         (bk(b),), (P.rt("R2", "PT", pi),))
                    res.append(pi)
                return res

            def pv(kc, pis, first, lastk):
                vtrk = P.rt("R0", "VV", slot, tile_blk(kc))
                for m in range(2):
                    pi = pis[m]
                    P.op("pe", I("matmul", banks[O[m]][:, 0:qs], lhsT=VV(slot)[:, kc, :],
                                                               rhs=PT[pi][:, 0:qs], start=first, stop=lastk),
                         (vtrk, P.rt("R2", "PT", pi)), (bk(O[m]),))
                    P.op("pe", I("matmul", banks[R[m]][:, 0:qs], lhsT=ones[:],
                                                               rhs=PT[pi][:, 0:qs], start=first, stop=lastk),
                         (P.t("ones"), P.rt("R2", "PT", pi)), (bk(R[m]),))

            prev = scores(kchunks[0])
            for i, kc in enumerate(kchunks):
                nxt = scores(kchunks[i + 1]) if i + 1 < len(kchunks) else None
                if i + 2 == len(kchunks) and qi_ + 1 < len(qblocks):
                    prep_q(qblocks[qi_ + 1])
                pv(kc, prev, i == 0, i == len(kchunks) - 1)
                prev = nxt
                DEF.tick()

            def tail_T1(qs=qs):
                a1, a2 = AT[0][:, 0:qs], AT[1][:, 0:qs]
                t1, t2 = P.rt("R2", "AT", 0), P.rt("R2", "AT", 1)
                di = bt_ring.next()
                dt = P.t("bt", di)
                dd = btr[:, di, 0:qs]
                P.op("act", I("activation", out=a1, in_=banks[R[1]][:, 0:qs], func=AF.Copy), (bk(R[1]),), (t1,))
                P.op("act", I("activation", out=a2, in_=banks[R[0]][:, 0:qs], func=AF.Copy), (bk(R[0]),), (t2,))
                P.op("dve", I("tensor_tensor", out=dd, in0=a2, in1=a1, op=ALU.mult), (t2, t1), (dt,))
                P.op("dve", I("tensor_tensor", out=a1, in0=banks[O[0]][:, 0:qs], in1=a1, op=ALU.mult), (bk(O[0]), t1), (t1,))
                P.op("dve", I("scalar_tensor_tensor", out=a2, in0=banks[O[1]][:, 0:qs], scalar=nlam, in1=a2,
                              op0=ALU.mult, op1=ALU.mult), (bk(O[1]), t2, P.t("lamt")), (t2,))
                P.op("pool", I("tensor_tensor", out=a1, in0=a1, in1=a2, op=ALU.add), (t1, t2), (t1,))
                P.op("pool", I("tensor_tensor", out=dd, in0=dd, in1=dd, op=ALU.mult), (dt,), (dt,))
                return di

            def tail_fin(qs=qs, q0=q0, qtrk=qtrk):
                a1 = AT[0][:, 0:qs]
                P.op("dve", I("scalar_tensor_tensor", out=QA[:, hd, q0:q0 + qs], in0=a1, scalar=subcol, in1=banks[7][:, 0:qs],
                              op0=ALU.mult, op1=ALU.mult), (P.rt("R2", "AT", 0), bk(7), P.t("lamt")), (qtrk,))

            DEF.flush()
            di_ = tail_T1()
            a1_ = AT[0][:, 0:qs]
            DEF.add(("tail", hd, qb), rstd_stages(lambda c, a1_=a1_: a1_, [P.rt("R2", "AT", 0)], qs, 1, 1.0 / 128, d0=2,
                                                  extra=(di_, epsc[:, 0:1]), eps_add=0.0) + [(1, tail_fin)])
            bg_boundary()
        while bg:
            bg.pop(0)()

    PT_ring = Ring(range(4))

    def phase_A(l, n, last):
        for r in ("R0", "R1", "R2"):
            P.release(r)
        P.op("pool", I("memset", QP[0][64:128, :], 0.0), (), (P.rt("R2", "QP", 0),))
        P.op("pool", I("memset", QP[1][0:64, :], 0.0), (), (P.rt("R2", "QP", 1),))
        rtk = P.rt("R2", "ROPE")
        P.dma("pool", I("dma_start", out=ropeC, in_=rope_d[0]), (), (rtk,), "rope")
        P.dma("pool", I("dma_start", out=ropeS, in_=rope_d[1]), (), (rtk,), "rope")
        for f in proj_kv_items(l, 0, 0, bank_all):
            f()
        for f in proj_q_items(l, 0, bank_all, last):
            f()
        attention(l, n, 0, 0, last, proj_kv_items(l, 1, 1, bank_B))
        attention(l, n, 1, 1, last, proj_q_items(l, 1, bank_B, last) + proj_kv_items(l, 2, 0, bank_B))
        attention(l, n, 2, 0, last, proj_kv_items(l, 3, 1, bank_B))
        attention(l, n, 3, 1, last, [])

    def phase_B(l, n, blocks):
        P.release("R0")
        P.release("R2")
        P.dma("pool", I("dma_start", out=wsT[:], in_=sgw_d[l]), (), (P.t("wsT"),), "sgp")
        P.dma("sp", I("dma_start", out=sgbf[:, 0:512], in_=sgb_d[l:l + 1, :]), (), (P.t("sgbf"),), "sgp2")
        P.dma("sp", I("dma_start", out=sggt[:], in_=sgg_d[l].partition_broadcast(128)), (), (P.t("sggt"),), "sgp3")
        P.op("dve", I("tensor_copy", out=sgbr[:, 0:512], in_=sgbf[:, 0:512]), (P.t("sgbf"),), (P.t("sgbr"),))
        P.op("dve", I("tensor_tensor", out=sgbf[:, 512:1024], in0=sgbf[:, 0:512], in1=sgbr[:, 0:512],
                                              op=ALU.subtract), (P.t("sgbf"), P.t("sgbr")), (P.t("sgbf"),))
        P.op("dve", I("tensor_copy", out=sgbr[:, 512:1024], in_=sgbf[:, 512:1024]), (P.t("sgbf"),), (P.t("sgbr"),))
        tiles = [t for t in range(NTILE) if tile_blk(t) in blocks]
        SQK = math.sqrt(0.044715)

        def run_skew(items, sched):
            n_ = len(items)
            maxlag = max(lg for _, lg in sched)
            for step in range(n_ + maxlag):
                for fn_, lg in sched:
                    i_ = step - lg
                    if 0 <= i_ < n_:
                        fn_(items[i_])

        def g_s1(it):
            f1 = ft_ring.next()
            it["f1"] = f1
            X, T1 = banks[it["b"]][:, 0:it["sz"]], ftr[:, f1, 0:it["sz"]]
            P.op("act", I("activation", out=T1, in_=X, func=AF.Identity, scale=0.5), (bk(it["b"]),), (P.t("ft", f1),))
            P.op("act", I("activation", out=X, in_=X, func=AF.Square, scale=SQK), (bk(it["b"]),), (bk(it["b"]),))

        def g_s2a(it):
            X, T1 = banks[it["b"]][:, 0:it["sz"]], ftr[:, it["f1"], 0:it["sz"]]
            P.op("dve", I("scalar_tensor_tensor", out=X, in0=X, scalar=1.0, in1=T1, op0=ALU.add, op1=ALU.mult),
                 (bk(it["b"]), P.t("ft", it["f1"])), (bk(it["b"]),))

        def g_s2b(it):
            X = banks[it["b"]][:, 0:it["sz"]]
            P.op("act", I("activation", out=X, in_=X, func=AF.Tanh, scale=2.0 * GELU_C), (bk(it["b"]),), (bk(it["b"]),))

        uw = {}

        def u_s0(it):
            if it["half"] not in uw:
                wv, wt = w_next(2048)
                uw[it["half"]] = (wv.rearrange("p (k c) -> p k c", c=256), wt)
            wv3, wt = uw[it["half"]]
            t0, sz = BLK[it["bi"]]
            src_fn, src_trks = h_src(it["bi"])
            b = bank_all.next()
            it["b"], it["sz"] = b, sz
            gg = it["gg"]
            proj_fm(b, lambda kc: wv3[:, kc, gg * 128:(gg + 1) * 128], wt, src_fn, src_trks, sz, 8)

        def u_s3(it):
            t0, sz = BLK[it["bi"]]
            X, T1 = banks[it["b"]][:, 0:sz], ftr[:, it["f1"], 0:sz]
            P.op("dve", I("scalar_tensor_tensor", out=UB[:, it["g"], t0:t0 + sz], in0=X, scalar=1.0, in1=T1,
                          op0=ALU.add, op1=ALU.mult), (bk(it["b"]), P.t("ft", it["f1"])),
                 tuple(P.rt("R2", "UB", it["g"], t) for t in range(t0 // 128, (t0 + sz) // 128)))

        uitems = [dict(half=half, gg=gg, g=half * 2 + gg, bi=bi_) for half in range(2) for gg in range(2) for bi_ in blocks]
        run_skew(uitems, [(u_s3, 3), (g_s2a, 2), (g_s1, 1), (g_s2b, 2), (u_s0, 0)])

        sw = {}

        def v_s0(it):
            if "w" not in sw:
                wv0, wt0 = w_next(2048)
                wv1, wt1 = w_next(2048, hold=1)
                sw["w"] = ([wv0.rearrange("p (k c) -> p k c", c=256), wv1.rearrange("p (k c) -> p k c", c=256)], [wt0, wt1])
            wvs, wts = sw["w"]
            t = it["t"]
            DEF.ensure(("h", tile_blk(t)))
            b = bank_all.next()
            it["b"], it["sz"] = b, 512
            for half in range(2):
                P.op("pe", mmgroup(banks[b][:, half * 256:(half + 1) * 256],
                                   [(hT[:, kc, t * 128:(t + 1) * 128], wvs[half][:, kc, :]) for kc in range(8)]),
                     (wts[half],) + tuple(P.t("h", kc, tile_blk(t)) for kc in range(8)), (bk(b),))

        def v_s3(it):
            X, G = banks[it["b"]][:, :], ftr[:, it["f1"], :]
            si = stat_ring.next()
            it["si"] = si
            st = P.t("stat", si)
            gt = P.t("ft", it["f1"])
            P.op("dve", I("scalar_tensor_tensor", out=G, in0=X, scalar=1.0, in1=G, op0=ALU.add, op1=ALU.mult),
                 (bk(it["b"]), gt), (gt,))
            P.op("dve", I("bn_stats", out=stat[:, si, 0:6], in_=G), (gt,), (st,))
            P.op("dve", I("bn_aggr", out=stat[:, si, 8:10], in_=stat[:, si, 0:6]), (st,), (st,))
            P.op("dve", I("tensor_scalar", out=stat[:, si, 10:11], in0=stat[:, si, 9:10], scalar1=EPS, scalar2=None,
                          op0=ALU.add), (st,), (st,))
            P.op("pool", I("tensor_tensor", out=stat[:, si, 10:11], in0=stat[:, si, 10:11], in1=nhalf[:, 0:1], op=ALU.pow),
                 (st, P.t("nhalf")), (st,))

        def v_s4a(it):
            si = it["si"]
            st = P.t("stat", si)
            P.op("dve", I("scalar_tensor_tensor", out=stat[:, si, 11:12], in0=stat[:, si, 8:9], scalar=-1.0,
                          in1=stat[:, si, 10:11], op0=ALU.mult, op1=ALU.mult), (st,), (st,))

        def v_s4b(it):
            si = it["si"]
            G = ftr[:, it["f1"], :]
            gt = P.t("ft", it["f1"])
            P.op("act", I("activation", out=G, in_=G, func=AF.Identity, bias=stat[:, si, 11:12], scale=stat[:, si, 10:11]),
                 (gt, P.t("stat", si)), (gt,))

        def v_s4c(it):
            G = ftr[:, it["f1"], :]
            P.op("pool", I("tensor_tensor", out=SGV[:, it["t"], :], in0=G, in1=sggt[:], op=ALU.mult),
                 (P.t("ft", it["f1"]), P.t("sggt")), (P.rt("R0", "SGV", it["t"]),))

        vitems = [dict(t=t) for t in tiles]
        run_skew(vitems, [(v_s4a, 4), (v_s4b, 4), (v_s3, 3), (v_s4c, 4), (g_s2a, 2), (g_s1, 1), (g_s2b, 2), (v_s0, 0)])
        for t in tiles:
            b = bank_all.next()
            for g in range(4):
                def fn(e, g=g, t=t, b=b):
                    o = banks[b][:, g * 128:(g + 1) * 128]
                    e.matmul(o, lhsT=SGV[:, t, g * 128:(g + 1) * 128], rhs=wsT[:, g * 128:(g + 1) * 128],
                             start=True, stop=False)
                    e.matmul(o, lhsT=ones[0:1, :], rhs=sgbr[0:1, g * 128:(g + 1) * 128], start=False, stop=False)
                    return e.matmul(o, lhsT=ones[0:1, :], rhs=sgbr[0:1, 512 + g * 128:512 + (g + 1) * 128],
                                    start=False, stop=True)
                P.op("pe", fn, (P.rt("R0", "SGV", t), P.t("wsT"), P.t("ones"), P.t("sgbr")), (bk(b),))
            utr = [P.rt("R2", "UB", g, t) for g in range(4)]
            P.op("dve", I("tensor_tensor",
                out=UB[:, :, t * 128:(t + 1) * 128], in0=UB[:, :, t * 128:(t + 1) * 128],
                in1=banks[b][:, :].rearrange("p (g q) -> p g q", q=128), op=ALU.mult), tuple(utr) + (bk(b),), tuple(utr))

    def phase_C(l, n, blocks, on_final):
        P.release("R0")
        for jg in range(2):
            for jp in range(2):
                j0 = jg * 4 + jp * 2
                wva, wta = w_next(2048)
                was = wva.rearrange("p (j s k c) -> p j s k c", j=2, s=2, k=4)
                for jj in range(2):
                    j = j0 + jj
                    wvg, wtg = w_next(2048, hold=1 + jj)
                    wg = wvg.rearrange("p (s k c) -> p s k c", s=2, k=8)
                    mj = jp * 2 + jj
                    for bi_ in blocks:
                        t0, sz = BLK[bi_]
                        src_fn, src_trks = h_src(bi_)
                        tl = list(range(t0 // 128, (t0 + sz) // 128))
                        bga, ba, bgb, bs = bank_all.next(), bank_all.next(), bank_all.next(), bank_all.next()
                        proj_fm(bga, lambda kc: wg[:, 0, kc, :], wtg, src_fn, src_trks, sz, 8)
                        proj_fm(ba, lambda kc: was[:, jj, 0, kc, :], wta, lambda kc: QA[:, kc, t0:t0 + sz],
                                [P.rt("R1", "QA", kc, bi_) for kc in range(4)], sz, 4)
                        proj_fm(bgb, lambda kc: wg[:, 1, kc, :], wtg, src_fn, src_trks, sz, 8)
                        proj_fm(bs, lambda kc: was[:, jj, 1, kc, :], wta, lambda kc: UB[:, kc, t0:t0 + sz],
                                [P.rt("R2", "UB", kc, t) for kc in range(4) for t in tl], sz, 4)
                        f1, f2 = ft_ring.next(), ft_ring.next()
                        T1, T2 = ftr[:, f1, 0:sz], ftr[:, f2, 0:sz]
                        P.op("act", I("activation", out=T1, in_=banks[bga][:, 0:sz], func=AF.Tanh,
                                                                           scale=0.5), (bk(bga),), (P.t("ft", f1),))
                        P.op("act", I("activation", out=T2, in_=banks[bgb][:, 0:sz], func=AF.Tanh,
                                                                           scale=0.5), (bk(bgb),), (P.t("ft", f2),))
                        P.op("dve", I("scalar_tensor_tensor",
                            out=T1, in0=T1, scalar=1.0, in1=banks[ba][:, 0:sz], op0=ALU.add, op1=ALU.mult),
                             (P.t("ft", f1), bk(ba)), (P.t("ft", f1),))
                        P.op("dve", I("scalar_tensor_tensor",
                            out=T2, in0=T2, scalar=1.0, in1=banks[bs][:, 0:sz], op0=ALU.add, op1=ALU.mult),
                             (P.t("ft", f2), bk(bs)), (P.t("ft", f2),))
                        P.op("pool", I("tensor_tensor",
                            out=MG[:, mj, t0:t0 + sz], in0=T1, in1=T2, op=ALU.add),
                             (P.t("ft", f1), P.t("ft", f2)), (P.rt("R0", "MG", mj, bi_),))
            for half in range(2):
                wvo, wto = w_next(2048)
                wo = wvo.rearrange("p (k c) -> p k c", c=512)
                for bi_ in blocks:
                    t0, sz = BLK[bi_]
                    v = 2 if bi_ == 0 else n
                    for ff in range(4):
                        fo = half * 4 + ff
                        b = bank_all.next()
                        proj_fm(b, lambda kc: wo[:, kc, ff * 128:(ff + 1) * 128], wto, lambda kc: MG[:, kc, t0:t0 + sz],
                                [P.rt("R0", "MG", kc, bi_) for kc in range(4)], sz, 4)
                        P.op("dve", I("scalar_tensor_tensor",
                            out=xT[:, fo, t0:t0 + sz], in0=banks[b][:, 0:sz], scalar=mcol(l, 2, fo, v),
                            in1=xT[:, fo, t0:t0 + sz], op0=ALU.mult, op1=ALU.add),
                             (bk(b), P.t("mod", l), P.t("x", fo, bi_)), (P.t("x", fo, bi_),))
                    if jg == 1 and half == 1:
                        on_final(bi_)
            if jg == 0:
                P.release("R0")

    def phase_F(l, n, blocks, on_final, side=()):
        P.release("R0")
        P.release("R1")
        P.release("R2")
        side = list(side)

        def side_step():
            if side:
                side.pop(0)()
        for gi, (c0, ng) in enumerate(FFN_GROUPS):
            if gi > 0:
                P.release("R0")
                P.release("R1")
            for cc in range(ng):
                wv, wt = w_next(2048)
                wf = wv.rearrange("p (s k c) -> p s k c", s=2, k=8)
                reg = "R0" if cc < 4 else "R1"
                for bi_ in blocks:
                    t0, sz = BLK[bi_]
                    src_fn, src_trks = h_src(bi_)
                    ba, bb = bank_all.next(), bank_all.next()
                    proj_fm(ba, lambda kc: wf[:, 0, kc, :], wt, src_fn, src_trks, sz, 8)
                    proj_fm(bb, lambda kc: wf[:, 1, kc, :], wt, src_fn, src_trks, sz, 8)
                    f1 = ft_ring.next()
                    T1 = ftr[:, f1, 0:sz]
                    P.op("act", I("activation", out=T1, in_=banks[ba][:, 0:sz], func=AF.Tanh, scale=0.5),
                         (bk(ba),), (P.t("ft", f1),))
                    P.op("dve", I("scalar_tensor_tensor",
                        out=T1, in0=T1, scalar=1.0, in1=banks[ba][:, 0:sz], op0=ALU.add, op1=ALU.mult),
                         (P.t("ft", f1), bk(ba)), (P.t("ft", f1),))
                    P.op("dve", I("tensor_tensor",
                        out=ACTF[:, cc, t0:t0 + sz], in0=T1, in1=banks[bb][:, 0:sz], op=ALU.mult),
                         (P.t("ft", f1), bk(bb)), (P.rt(reg, "ACTF", cc, bi_),))
                side_step()
            for qd in range(4):
                wv, wt = w_next(ng * 256)
                wo = wv[:, 0:ng * 256].rearrange("p (k c) -> p k c", c=256)
                for bi_ in blocks:
                    t0, sz = BLK[bi_]
                    v = 2 if bi_ == 0 else n
                    for ff in range(2):
                        fo = qd * 2 + ff
                        b = bank_all.next()
                        proj_fm(b, lambda kc: wo[:, kc, ff * 128:(ff + 1) * 128], wt, lambda kc: ACTF[:, kc, t0:t0 + sz],
                                [P.rt("R0" if kc < 4 else "R1", "ACTF", kc, bi_) for kc in range(ng)], sz, ng)
                        P.op("dve", I("scalar_tensor_tensor",
                            out=xT[:, fo, t0:t0 + sz], in0=banks[b][:, 0:sz], scalar=mcol(l, 5, fo, v),
                            in1=xT[:, fo, t0:t0 + sz], op0=ALU.mult, op1=ALU.add),
                             (bk(b), P.t("mod", l), P.t("x", fo, bi_)), (P.t("x", fo, bi_),))
                    if gi == len(FFN_GROUPS) - 1 and qd == 3:
                        on_final(bi_)
                side_step()

    for n in range(nb):
        for fc in range(8):
            P.dma("sp", I("dma_start", out=xT[:, fc, :], in_=xin[n, :, fc, :]), (),
                  tuple(P.t("x", fc, bi_) for bi_ in range(5)), "xl%d" % fc)
        for bi_ in range(5):
            DEF.add(("h", bi_), norm_chain(0, n, 0, 1, bi_))
        for l in range(depth):
            last = (l == depth - 1)
            blocks = [1, 2, 3, 4] if last else [0, 1, 2, 3, 4]
            dbg = debug and n == 0 and l == depth - 1
            phase_A(l, n, last)
            DEF.flush()
            if dbg:
                dump("d_h1", hT[:], [128, 8, T], BF16, [P.t("h", kc, b_) for kc in range(8) for b_ in range(5)])
                dump("d_qa", arena[:, 9216:18432], [128, 9216], BF16, [P.rt("R1", "QA", h_, b_) for h_ in range(4) for b_ in blocks])
                dump("d_kv", arena[:, 0:9216], [128, 9216], BF16, [P.rt("R0", "KT", s_, b_) for s_ in range(2) for b_ in range(5)] + [P.rt("R0", "VV", s_, b_) for s_ in range(2) for b_ in range(5)])
            phase_B(l, n, blocks)
            if dbg:
                dump("d_ub", arena[:, 18432:27648], [128, 9216], BF16, [P.rt("R2", "UB", g_, t_) for g_ in range(4) for t_ in range(NTILE) if tile_blk(t_) in blocks])
                dump("d_sgv", arena[:, 0:9216], [128, 9216], BF16, [P.rt("R0", "SGV", t_) for t_ in range(NTILE) if tile_blk(t_) in blocks])
            phase_C(l, n, blocks, lambda b_, l=l, n=n: DEF.add(("h", b_), norm_chain(l, n, 3, 4, b_)))
            if dbg:
                DEF.flush()
                dump("d_x1", xT[:], [128, 8, T], F32, [P.t("x", fc, b_) for fc in range(8) for b_ in range(5)])
                dump("d_h2", hT[:], [128, 8, T], BF16, [P.t("h", kc, b_) for kc in range(8) for b_ in range(5)])
            if last:
                onf = lambda b_, n=n: DEF.add(("fin", b_), final_chain(n, b_))
            else:
                onf = lambda b_, l=l, n=n: DEF.add(("h", b_), norm_chain(l + 1, n, 0, 1, b_))
            if n == 0 and l + 1 < depth:
                bank_all.items = list(range(5))
                phase_F(l, n, blocks, onf, ada_items(l + 1, 5))
                bank_all.items = list(range(6))
            else:
                phase_F(l, n, blocks, onf)
            if dbg:
                dump("d_x2", xT[:], [128, 8, T], F32, [P.t("x", fc, b_) for fc in range(8) for b_ in range(5)])
                dump("d_mod", mod[:], [128, depth, 48, 3], F32, [P.t("mod", l_) for l_ in range(depth)])
                dump("d_lamt", lamt[:], [128, 16], F32, [P.t("lamt")])
        DEF.flush()
    P.wait_all("sp", [P.t("yout")])
    assert wstate["cur"] == len(TILES), (wstate["cur"], len(TILES))
    P.emit()
    print("sbuf bytes remaining/partition:", nc.sbuf_bytes_remaining, "ops:", {k: len(v) for k, v in P.q.items()})
    return nc


def prepare_shared(inp, depth):
    specs = layer_tile_specs()
    wst = []
    for l in range(depth):
        parts = []
        for (_, ps) in specs:
            for (mname, k0, nk, c0, ncols) in ps:
                parts.append(pack_part(inp[mname][l], k0, nk, c0, ncols))
        wst.append(np.concatenate(parts, axis=1))
    wst = np.ascontiguousarray(np.stack(wst, 0), dtype=np.float32)
    wada = np.stack([np.stack([pack_part(inp["ada_w"][l], 0, 8, jt * 256, 256) for jt in range(24)], 0)
                     for l in range(depth)], 0).astype(np.float32)
    lams = np.stack([inp["lambda_q1"][:depth], inp["lambda_k1"][:depth], inp["lambda_q2"][:depth],
                     inp["lambda_k2"][:depth]], 0)
    if depth < 4:
        lams = np.concatenate([lams, np.zeros((4, 4 - depth, 64), np.float32)], 1)
    lams = np.ascontiguousarray(lams.reshape(1024), dtype=np.float32)
    sgw = np.ascontiguousarray(inp["sg_w"][:depth].transpose(0, 3, 1, 2).reshape(depth, 128, 512), dtype=np.float32)
    sgb = np.ascontiguousarray(inp["sg_b"][:depth].reshape(depth, 512), dtype=np.float32)
    sgg = np.ascontiguousarray(inp["sg_norm_g"][:depth], dtype=np.float32)
    return dict(wst=wst, wada=np.ascontiguousarray(wada), lams=lams, sgw=sgw, sgb=sgb, sgg=sgg,
                rope=rope_tables(), perm=perm_matrix(), ident=np.eye(128, dtype=np.float32))


def cols(v):
    R = v.shape[0]
    k = v.shape[1] // 128
    return v.reshape(R, k, 128).transpose(2, 0, 1).reshape(128, R * k)


def prepare_core(inp, depth, bsel):
    x = inp["x"][bsel]
    ctx = inp["ctx"][bsel]
    nb = len(bsel)
    xc = np.concatenate([ctx, x], axis=1)
    xin = np.ascontiguousarray(xc.reshape(nb, T, 8, 128).transpose(0, 3, 2, 1), dtype=np.float32)
    cs = [inp["c"][b] for b in bsel]
    while len(cs) < 2:
        cs.append(np.zeros(D, np.float32))
    cmat = np.stack(cs + [inp["c_ctx"]], 0)

    def padl(a):
        a = a[:depth]
        if depth < 4:
            a = np.concatenate([a, np.zeros((4 - depth,) + a.shape[1:], a.dtype)], 0)
        return a

    sm = np.concatenate([cols(cmat), cols(padl(inp["norm1_g"])), cols(padl(inp["norm2_g"])),
                         cols(inp["final_g"][None, :]), cols(padl(inp["ada_b"])),
                         cols(padl(inp["subln_g"]))], axis=1)
    return dict(xin=xin, smalls=np.ascontiguousarray(sm, dtype=np.float32))


_PROG_CACHE = {}


def run(inp, depth=4, ncores=NCORES, nb=2, trace=False):
    inp = {k: np.asarray(v, dtype=np.float32) for k, v in inp.items()}
    key = (depth, nb)
    if key not in _PROG_CACHE:
        _PROG_CACHE[key] = build_program(depth, nb)
    nc = _PROG_CACHE[key]
    shared = prepare_shared(inp, depth)
    in_maps = []
    for c in range(ncores):
        m = dict(shared)
        m.update(prepare_core(inp, depth, list(range(c * nb, (c + 1) * nb))))
        in_maps.append(m)
    res = run_bass_kernel_spmd(nc, in_maps, core_ids=list(range(ncores)), trace=trace)
    outs = []
    for c in range(ncores):
        y = res.results[c]["yout"]
        outs.append(np.asarray(y).transpose(0, 3, 2, 1).reshape(nb, SEQ, D))
    return np.ascontiguousarray(np.concatenate(outs, 0), dtype=np.float32), res


def kernel(**inputs):
    out, _ = run(inputs, depth=4, ncores=NCORES, nb=2)
    return out
```

```python
import math
import numpy as np
import concourse.bass as bass
import concourse.mybir as mybir
from concourse.bass_utils import run_bass_kernel_spmd

F32 = mybir.dt.float32
BF16 = mybir.dt.bfloat16
AF = mybir.ActivationFunctionType
ALU = mybir.AluOpType
AX = mybir.AxisListType

D = 1024
SEQ = 2048
CTX = 256
T = SEQ + CTX
DFF = 2816
NCORES = 8
EPS = 1e-6
BLK = [(0, 256), (256, 512), (768, 512), (1280, 512), (1792, 512)]
NTILE = T // 128
GELU_C = math.sqrt(2.0 / math.pi)
FFN_GROUPS = [(0, 8), (8, 8), (16, 6)]
WSLOT = 2048
NWS = 4


def tile_blk(t):
    return 0 if t < 2 else 1 + (t - 2) // 4


def lam_init(i):
    return 0.8 - 0.6 * math.exp(-0.3 * i)


def layer_tile_specs():
    tl = []
    K_OFF, V_OFF, Q_OFF, U_OFF, SGV_OFF, GATE_OFF = 0, 512, 1024, 1536, 2048, 2560

    def kv(h):
        return ("KV%d" % h, [("w_in", 0, 8, K_OFF + h * 128, 128), ("w_in", 0, 8, V_OFF + h * 128, 128)])

    def q(p):
        return ("Q%d" % p, [("w_in", 0, 8, Q_OFF + p * 256, 256)])

    tl += [kv(0), q(0), kv(1), q(1), kv(2), kv(3)]
    tl += [("U0", [("w_in", 0, 8, U_OFF, 256)]), ("U1", [("w_in", 0, 8, U_OFF + 256, 256)]),
           ("SGV0", [("w_in", 0, 8, SGV_OFF, 256)]), ("SGV1", [("w_in", 0, 8, SGV_OFF + 256, 256)])]
    for jg in range(2):
        for jp in range(2):
            j0 = jg * 4 + jp * 2
            tl.append(("AS%d" % j0, [("w_branch_attn", 0, 4, j0 * 128, 128), ("w_branch_sg", 0, 4, j0 * 128, 128),
                                      ("w_branch_attn", 0, 4, (j0 + 1) * 128, 128),
                                      ("w_branch_sg", 0, 4, (j0 + 1) * 128, 128)]))
            for j in (j0, j0 + 1):
                tl.append(("G%d" % j, [("w_in", 0, 8, GATE_OFF + j * 128, 128),
                                       ("w_in", 0, 8, GATE_OFF + 1024 + j * 128, 128)]))
        for half in range(2):
            tl.append(("WO%d_%d" % (jg, half), [("w_out", jg * 4, 4, half * 512, 512)]))
    for (c0, ng) in FFN_GROUPS:
        for c in range(c0, c0 + ng):
            tl.append(("F%d" % c, [("w_ffn_in", 0, 8, c * 128, 128), ("w_ffn_in", 0, 8, DFF + c * 128, 128)]))
        for qd in range(4):
            tl.append(("FO%d_%d" % (c0, qd), [("w_ffn_out", c0, ng, qd * 256, 256)]))
    return tl


def pack_part(W, k0, nk, col0, ncols):
    blk = W[k0 * 128:(k0 + nk) * 128, col0:col0 + ncols]
    return blk.reshape(nk, 128, ncols).transpose(1, 0, 2).reshape(128, nk * ncols)


def rope_tables():
    t = np.arange(SEQ)
    row = (t // 64).astype(np.float32)
    col = (t % 64).astype(np.float32)
    half = 32
    inv = (1.0 / (np.float32(10000.0) ** (np.arange(0, half, 2, dtype=np.float32) / np.float32(half)))).astype(np.float32)
    ang_r = row[:, None] * inv[None, :]
    ang_c = col[:, None] * inv[None, :]
    C = np.zeros((64, SEQ), np.float32)
    S_ = np.zeros((64, SEQ), np.float32)
    for d in range(64):
        j = d % 16
        a = ang_r[:, j] if d < 32 else ang_c[:, j]
        C[d] = np.cos(a.astype(np.float32))
        S_[d] = np.sin(a.astype(np.float32))
    C = np.concatenate([C, C], 0)
    S_ = np.concatenate([S_, S_], 0)
    return np.stack([C, S_], 0).astype(np.float32)


def perm_matrix():
    P = np.zeros((128, 128), np.float32)
    for fp in range(128):
        d = fp % 32
        if d < 16:
            P[fp + 16, fp] = -1.0
        else:
            P[fp - 16, fp] = 1.0
    return P


class Trk:
    __slots__ = ("w", "r", "excl")

    def __init__(self, fence=None):
        self.w = None
        self.r = dict(fence) if fence else {}
        self.excl = False


class Ring:
    def __init__(self, items):
        self.items = list(items)
        self.i = 0

    def next(self):
        it = self.items[self.i % len(self.items)]
        self.i += 1
        return it


class Prog:
    ENG = ("pe", "act", "dve", "pool", "sp")

    def __init__(self, nc):
        self.nc = nc
        self.q = {e: [] for e in self.ENG}
        self.cnt = {e: 0 for e in self.ENG}
        self.seen = {e: {} for e in self.ENG}
        self.semh = {}
        self.dcnt = {}
        self.trk = {}
        self.region_trk = {}
        self.region_fence = {}

    def sem(self, name):
        if name not in self.semh:
            self.semh[name] = self.nc.alloc_semaphore("s_" + name)
        return self.semh[name]

    def t(self, *key):
        if key not in self.trk:
            self.trk[key] = Trk()
        return self.trk[key]

    def rt(self, region, *key):
        k = (region,) + key
        if k not in self.trk:
            self.trk[k] = Trk(self.region_fence.get(region))
            self.region_trk.setdefault(region, []).append(k)
        return self.trk[k]

    def release(self, region):
        fence = dict(self.region_fence.get(region, {}))
        for k in self.region_trk.get(region, []):
            tr = self.trk.pop(k)
            if tr.w is not None and fence.get(tr.w[0], 0) < tr.w[1]:
                fence[tr.w[0]] = tr.w[1]
            for s, v in tr.r.items():
                if fence.get(s, 0) < v:
                    fence[s] = v
        self.region_trk[region] = []
        self.region_fence[region] = fence

    def merge_fences(self, regions):
        u = {}
        for r in regions:
            for s, v in self.region_fence.get(r, {}).items():
                if u.get(s, 0) < v:
                    u[s] = v
        for r in regions:
            self.region_fence[r] = dict(u)

    def _deps(self, eng, reads, writes):
        need = {}

        def add(s, v):
            if need.get(s, 0) < v:
                need[s] = v

        for t in reads:
            if t.w is not None:
                add(*t.w)
            if t.excl:
                for s, v in t.r.items():
                    if s != eng:
                        add(s, v)
        for t in writes:
            if t.w is not None:
                add(*t.w)
            for s, v in t.r.items():
                add(s, v)
        waits = []
        for s, v in need.items():
            if s == eng and eng == "pe":
                continue
            if self.seen[eng].get(s, 0) >= v:
                continue
            self.seen[eng][s] = v
            waits.append((s, v))
        return waits

    def _mark(self, tok, reads, writes):
        s, v = tok
        for t in reads:
            if t.r.get(s, 0) < v:
                t.r[s] = v
        for t in writes:
            t.w = tok
            t.r = {}

    def op(self, eng, fn, reads=(), writes=()):
        waits = self._deps(eng, reads, writes)
        self.cnt[eng] += 1
        tok = (eng, self.cnt[eng])
        self.q[eng].append(("op", waits, fn, None))
        self._mark(tok, reads, writes)
        return tok

    def dma(self, eng, fn, reads, writes, sem):
        waits = self._deps(eng, reads, writes)
        self.dcnt[sem] = self.dcnt.get(sem, 0) + 16
        tok = (sem, self.dcnt[sem])
        self.q[eng].append(("dma", waits, fn, sem))
        self._mark(tok, reads, writes)
        return tok

    def wait_all(self, eng, trks):
        waits = self._deps(eng, trks, ())
        self.q[eng].append(("wait", waits, None, None))

    def emit(self):
        nc = self.nc
        for e in ("pe", "act", "dve", "pool"):
            self.sem(e)
        for s in list(self.dcnt.keys()):
            self.sem(s)
        with nc.Block() as block:
            engmap = {"pe": block.tensor, "act": block.scalar, "dve": block.vector,
                      "pool": block.gpsimd, "sp": block.sync}
            for name in self.ENG:
                items = self.q[name]
                if not items:
                    continue

                def body(e, items=items, name=name):
                    for kind, waits, fn, sem in items:
                        for (s, v) in waits:
                            e.wait_ge(self.semh[s], v)
                        if kind == "wait":
                            continue
                        if isinstance(fn, tuple):
                            ins = getattr(e, fn[0])(*fn[1], **fn[2])
                        else:
                            ins = fn(e)
                        if kind == "op":
                            ins.then_inc(self.semh[name], 1)
                        else:
                            ins.then_inc(self.semh[sem], 16)

                engmap[name](body)


class Deferred:
    def __init__(self):
        self.chains = []
        self.busy = False

    def add(self, key, stages):
        stages = list(stages)
        self.chains.append(dict(key=key, stages=stages, idx=0, wait=stages[0][0]))

    def _step(self, ch, only_a=False):
        if ch["wait"] > 0:
            ch["wait"] -= 1
            return False
        while True:
            st = ch["stages"][ch["idx"]]
            if only_a and not (len(st) > 2 and st[2] == "A"):
                return False
            st[1]()
            ch["idx"] += 1
            if ch["idx"] >= len(ch["stages"]):
                return True
            ch["wait"] = ch["stages"][ch["idx"]][0]
            if ch["wait"] > 0:
                return False

    def _in_b(self, ch):
        st = ch["stages"][ch["idx"]]
        return not (len(st) > 2 and st[2] == "A")

    def tick(self):
        if self.busy or not self.chains:
            return
        self.busy = True
        head = self.chains[0]
        overlap = len(self.chains) > 1 and self._in_b(head)
        if self._step(head):
            self.chains.pop(0)
        elif overlap:
            self._step(self.chains[1], only_a=True)
        self.busy = False

    def pending(self, key):
        return any(c["key"] == key for c in self.chains)

    def ensure(self, key):
        while self.pending(key):
            self.tick()

    def flush(self):
        while self.chains:
            self.tick()


def I(name, *a, **kw):
    return (name, a, kw)


def mmgroup(out_ap, pairs):
    pairs = list(pairs)

    def fn(e):
        n = len(pairs)
        ins = None
        for i, (l, r) in enumerate(pairs):
            ins = e.matmul(out_ap, lhsT=l, rhs=r, start=(i == 0), stop=(i == n - 1))
        return ins

    return fn


def build_program(depth=4, nb=2, debug=False):
    nc = bass.Bass("TRN2", target_bir_lowering=False)
    specs = layer_tile_specs()
    tile_sizes = [sum(nk * ncols for (_, _, nk, _, ncols) in parts) for (_, parts) in specs]
    tile_offs = np.concatenate([[0], np.cumsum(tile_sizes)]).astype(int)
    E = int(tile_offs[-1])
    NSM = 24 + 32 + 32 + 8 + 192 + 4
    SM_C, SM_N1, SM_N2, SM_FG, SM_AB, SM_SUB = 0, 24, 56, 88, 96, 288

    xin = nc.dram_tensor("xin", [nb, 128, 8, T], F32, kind="ExternalInput").ap()
    yout = nc.dram_tensor("yout", [nb, 128, 8, SEQ], F32, kind="ExternalOutput").ap()
    smalls_d = nc.dram_tensor("smalls", [128, NSM], F32, kind="ExternalInput").ap()
    lams_d = nc.dram_tensor("lams", [1024], F32, kind="ExternalInput").ap()
    sgw_d = nc.dram_tensor("sgw", [depth, 128, 512], F32, kind="ExternalInput").ap()
    sgb_d = nc.dram_tensor("sgb", [depth, 512], F32, kind="ExternalInput").ap()
    sgg_d = nc.dram_tensor("sgg", [depth, 512], F32, kind="ExternalInput").ap()
    rope_d = nc.dram_tensor("rope", [2, 128, SEQ], F32, kind="ExternalInput").ap()
    perm_d = nc.dram_tensor("perm", [128, 128], F32, kind="ExternalInput").ap()
    ident_d = nc.dram_tensor("ident", [128, 128], F32, kind="ExternalInput").ap()
    wada_d = nc.dram_tensor("wada", [depth, 24, 128, 2048], F32, kind="ExternalInput").ap()
    wst_d = nc.dram_tensor("wst", [depth, 128, E], F32, kind="ExternalInput").ap()

    xT = nc.alloc_sbuf_tensor("xT", [128, 8, T], F32)
    hT = nc.alloc_sbuf_tensor("hT", [128, 8, T], BF16)
    arena = nc.alloc_sbuf_tensor("arena", [128, 27648], BF16)
    wring = nc.alloc_sbuf_tensor("wring", [128, NWS, WSLOT], BF16)
    btr = nc.alloc_sbuf_tensor("btr", [128, 4, 512], BF16)
    ftr = nc.alloc_sbuf_tensor("ftr", [128, 4, 512], F32)
    smalls = nc.alloc_sbuf_tensor("smalls_sb", [128, NSM], F32)
    mod = nc.alloc_sbuf_tensor("mod", [128, depth, 48, 3], F32)
    lamt = nc.alloc_sbuf_tensor("lamt", [128, 16], F32)
    scT = nc.alloc_sbuf_tensor("scT", [128, 8, 3], BF16)
    wsT = nc.alloc_sbuf_tensor("wsT", [128, 512], BF16)
    sgbr = nc.alloc_sbuf_tensor("sgbr", [1, 1024], BF16)
    sgbf = nc.alloc_sbuf_tensor("sgbf", [1, 1024], F32)
    sggt = nc.alloc_sbuf_tensor("sggt", [128, 512], F32)
    perm = nc.alloc_sbuf_tensor("perm_sb", [128, 128], BF16)
    ones = nc.alloc_sbuf_tensor("ones_sb", [128, 128], BF16)
    nhalf = nc.alloc_sbuf_tensor("nhalf", [128, 512], BF16)
    stat = nc.alloc_sbuf_tensor("stat", [128, 4, 16], F32)
    identf = nc.alloc_sbuf_tensor("identf", [128, 128], F32)
    onesf = nc.alloc_sbuf_tensor("onesf", [128, 128], F32)
    diag = nc.alloc_sbuf_tensor("diag", [128, 4, 128], F32)
    rtiny = nc.alloc_sbuf_tensor("rtiny", [128, 24], F32)
    epsc = nc.alloc_sbuf_tensor("epsc", [128, 2], BF16)
    banks = [nc.alloc_psum_tensor("bank%d" % i, [128, 512], F32) for i in range(8)]

    P = Prog(nc)

    def KT(s):
        return arena[:, s * 4608: s * 4608 + 2304]

    def VV(s):
        return arena[:, s * 4608 + 2304:(s + 1) * 4608].rearrange("p (t d) -> p t d", d=128)

    QA = arena[:, 9216:18432].rearrange("p (h t) -> p h t", t=T)
    UB = arena[:, 18432:27648].rearrange("p (g t) -> p g t", t=T)
    ropeC = arena[:, 18432:18432 + 2048]
    ropeS = arena[:, 18432 + 2048:18432 + 4096]
    PT = [arena[:, 18432 + 4096 + i * 512:18432 + 4096 + (i + 1) * 512] for i in range(4)]
    AT = [arena[:, 18432 + 6144 + i * 1024:18432 + 6144 + (i + 1) * 1024].bitcast(F32) for i in range(2)]
    QP = [arena[:, 18432 + 6144 + 2048 + m * 512:18432 + 6144 + 2048 + (m + 1) * 512] for m in range(2)]
    SGV = arena[:, 0:9216].rearrange("p (t f) -> p t f", f=512)
    MG = arena[:, 0:9216].rearrange("p (j t) -> p j t", t=T)
    ACTF = arena[:, 0:18432].rearrange("p (c t) -> p c t", t=T)
    OST = arena[:, 18432:18432 + 8192].bitcast(F32).rearrange("p (c t) -> p c t", t=512)

    bt_ring = Ring(range(4))
    ft_ring = Ring(range(4))
    stat_ring = Ring(range(4))
    rs_ring = Ring(range(2))
    bank_all = Ring(range(6))
    DEF = Deferred()
    bank_S = Ring([4, 5])
    bank_B = Ring([6, 7])

    def bk(i):
        tr = P.t("bank", i)
        tr.excl = True
        return tr

    dbg_n = [0]

    def dump(name, ap, shape, dtype, trks):
        if not debug:
            return
        d = nc.dram_tensor(name, list(shape), dtype, kind="ExternalOutput").ap()
        dbg_n[0] += 1
        P.dma("sp", I("dma_start", out=d, in_=ap), tuple(trks), (P.t("dbg", name),), "dbg%d" % dbg_n[0])
        P.wait_all("sp", [P.t("dbg", name)])

    TILES = []
    NFFN = sum(ng + 4 for (_, ng) in FFN_GROUPS)
    NPRE = len(specs) - NFFN
    for jt in range(24):
        TILES.append((wada_d[0, jt], 2048))
    for n in range(nb):
        for l in range(depth):
            side = (n == 0 and l + 1 < depth)
            for ti in range(len(specs)):
                TILES.append((wst_d[l, :, int(tile_offs[ti]):int(tile_offs[ti + 1])], tile_sizes[ti]))
                fi_ = ti - NPRE
                if side and 0 <= fi_ < 24:
                    TILES.append((wada_d[l + 1, fi_], 2048))
    wstate = {"issued": 0, "cur": 0}

    def w_prefetch(upto):
        while wstate["issued"] < min(upto, len(TILES)):
            i = wstate["issued"]
            slot = i % NWS
            src, size = TILES[i]
            P.dma("pool", I("dma_start", out=wring[:, slot, 0:size], in_=src),
                  reads=(), writes=(P.t("W", slot),), sem="w%d" % slot)
            wstate["issued"] += 1

    def w_next(expect_size=None, hold=0):
        i = wstate["cur"]
        w_prefetch(i - hold + NWS)
        wstate["cur"] += 1
        slot = i % NWS
        if expect_size is not None:
            assert TILES[i][1] == expect_size, (i, TILES[i][1], expect_size)
        return wring[:, slot, :], P.t("W", slot)

    P.dma("sp", I("dma_start", out=smalls[:], in_=smalls_d), (), (P.t("smalls"),), "misc0")
    lam_raw = ftr[:, 0:2, :].rearrange("p a b -> p (a b)")
    P.dma("sp", I("dma_start", out=lam_raw, in_=lams_d.partition_broadcast(128)), (),
          (P.t("ft", 0), P.t("ft", 1)), "misc2")
    P.dma("pool", I("dma_start", out=perm[:], in_=perm_d), (), (P.t("perm"),), "misc3")
    P.dma("sp", I("dma_start", out=identf[:], in_=ident_d), (), (P.t("identf"),), "misc4")
    P.op("dve", I("memset", onesf[:], 1.0), (), (P.t("onesf"),))
    P.op("dve", I("memset", epsc[:], EPS), (), (P.t("epsc"),))
    P.op("dve", I("memset", ones[:], 1.0), (), (P.t("ones"),))
    P.op("dve", I("memset", nhalf[:], -0.5), (), (P.t("nhalf"),))
    w_prefetch(NWS)

    lr = lam_raw.rearrange("p (a l d) -> p a l d", a=4, l=4)
    P.op("dve", I("tensor_tensor", out=lr[:, 0], in0=lr[:, 0], in1=lr[:, 1], op=ALU.mult),
         (P.t("ft", 0), P.t("ft", 1)), (P.t("ft", 0),))
    P.op("dve", I("tensor_tensor", out=lr[:, 2], in0=lr[:, 2], in1=lr[:, 3], op=ALU.mult),
         (P.t("ft", 0), P.t("ft", 1)), (P.t("ft", 1),))
    P.op("dve", I("reduce_sum", out=lamt[:, 0:4], in_=lr[:, 0], axis=AX.X), (P.t("ft", 0),), (P.t("lamt"),))
    P.op("dve", I("reduce_sum", out=lamt[:, 4:8], in_=lr[:, 2], axis=AX.X), (P.t("ft", 1),), (P.t("lamt"),))
    P.op("act", I("activation", out=lamt[:, 0:8], in_=lamt[:, 0:8], func=AF.Exp), (P.t("lamt"),), (P.t("lamt"),))
    P.op("dve", I("tensor_tensor", out=lamt[:, 8:12], in0=lamt[:, 4:8], in1=lamt[:, 0:4], op=ALU.subtract),
         (P.t("lamt"),), (P.t("lamt"),))
    for l in range(depth):
        P.op("dve", I("tensor_scalar", out=lamt[:, 8 + l:9 + l], in0=lamt[:, 8 + l:9 + l],
                                                     scalar1=-lam_init(l), scalar2=None, op0=ALU.add),
             (P.t("lamt"),), (P.t("lamt"),))
        P.op("dve", I("tensor_scalar", out=lamt[:, 12 + l:13 + l], in0=smalls[:, SM_SUB + l:SM_SUB + l + 1],
                                                     scalar1=1.0 - lam_init(l), scalar2=None, op0=ALU.mult),
             (P.t("smalls"), P.t("lamt")), (P.t("lamt"),))

    cT = smalls[:, SM_C:SM_C + 24]
    sct = ftr[:, 2, 0:24]
    P.op("act", I("activation", out=sct, in_=cT, func=AF.Tanh, scale=0.5), (P.t("smalls"),), (P.t("ft", 2),))
    P.op("dve", I("tensor_scalar", out=sct, in0=sct, scalar1=0.5, scalar2=0.5, op0=ALU.mult, op1=ALU.add),
         (P.t("ft", 2),), (P.t("ft", 2),))
    for n in range(3):
        P.op("dve", I("tensor_tensor", out=scT[:, :, n], in0=sct[:, n * 8:(n + 1) * 8],
                                                     in1=cT[:, n * 8:(n + 1) * 8], op=ALU.mult),
             (P.t("ft", 2), P.t("smalls")), (P.t("scT"),))

    def ada_items(l, b):
        def tile_item(jt):
            def f():
                wv, wt = w_next(2048)
                wv3 = wv.rearrange("p (k c) -> p k c", c=256)
                for jj in range(2):
                    j = jt * 2 + jj
                    P.op("pe", mmgroup(banks[b][:, j * 3:(j + 1) * 3],
                                       [(wv3[:, kc, jj * 128:(jj + 1) * 128], scT[:, kc, :]) for kc in range(8)]),
                         (wt, P.t("scT")), (bk(b),))
                if jt == 23:
                    post()
            return f

        def post():
            psv = banks[b][:, 0:144].rearrange("p (j n) -> p j n", n=3)
            for n in range(3):
                P.op("dve", I("tensor_tensor", out=mod[:, l, :, n], in0=psv[:, :, n],
                              in1=smalls[:, SM_AB + l * 48:SM_AB + (l + 1) * 48], op=ALU.add),
                     (bk(b), P.t("smalls")), (P.t("mod", l),))
            for n in range(3):
                for (seg, gsrc) in ((1, SM_N1), (4, SM_N2)):
                    P.op("dve", I("scalar_tensor_tensor", out=mod[:, l, seg * 8:(seg + 1) * 8, n],
                                  in0=mod[:, l, seg * 8:(seg + 1) * 8, n], scalar=1.0,
                                  in1=smalls[:, gsrc + l * 8:gsrc + (l + 1) * 8], op0=ALU.add, op1=ALU.mult),
                         (P.t("mod", l), P.t("smalls")), (P.t("mod", l),))
                for seg in (2, 5):
                    P.op("dve", I("tensor_scalar", out=mod[:, l, seg * 8:(seg + 1) * 8, n],
                                  in0=mod[:, l, seg * 8:(seg + 1) * 8, n], scalar1=0.5, scalar2=None, op0=ALU.mult),
                         (P.t("mod", l),), (P.t("mod", l),))
        return [tile_item(jt) for jt in range(24)]

    for f in ada_items(0, bank_all.next()):
        f()

    def mcol(l, seg, fc, v):
        return mod[:, l, seg * 8 + fc, v:v + 1]

    def rstd_stages(src_fn, src_trks, sz, nchunks, inv_n, d0=1, extra=None, eps_add=EPS, co=0):
        nt = sz // 128
        b1, b2 = 6, 7
        groups = [list(range(c, min(c + 4, nchunks))) for c in range(0, nchunks, 4)]
        slots = {}

        def sq(grp):
            for c in grp:
                bi = bt_ring.next()
                slots[c] = bi
                P.op("act", I("activation", out=btr[:, bi, 0:sz], in_=src_fn(c), func=AF.Square),
                     (src_trks[c],), (P.t("bt", bi),))

        def mm(grp):
            for c in grp:
                bi = slots[c]

                def fn(e, c=c, bi=bi):
                    ins = None
                    for tt in range(nt):
                        ins = e.matmul(banks[b1][:, co + tt:co + tt + 1], lhsT=btr[:, bi, tt * 128:(tt + 1) * 128], rhs=ones[:, 0:1],
                                       start=(c == 0 and tt == 0),
                                       stop=(c == nchunks - 1 and tt == nt - 1 and extra is None),
                                       skip_group_check=True)
                    return ins
                P.op("pe", fn, (P.t("bt", bi), P.t("ones")), (bk(b1),))
            if extra is not None and grp[-1] == nchunks - 1:
                ei, erhs = extra

                def fne(e):
                    ins = None
                    for tt in range(nt):
                        ins = e.matmul(banks[b1][:, co + tt:co + tt + 1], lhsT=btr[:, ei, tt * 128:(tt + 1) * 128], rhs=erhs,
                                       start=False, stop=(tt == nt - 1), skip_group_check=True)
                    return ins
                P.op("pe", fne, (P.t("bt", ei), P.t("epsc")), (bk(b1),))

        def fin():
            P.op("dve", I("tensor_scalar", out=rtiny[:, co:co + nt], in0=banks[b1][:, co:co + nt], scalar1=inv_n,
                          scalar2=eps_add, op0=ALU.mult, op1=ALU.add), (bk(b1),), (P.t("rtiny", co),))
            P.op("pool", I("tensor_tensor", out=rtiny[:, co:co + nt], in0=rtiny[:, co:co + nt], in1=nhalf[:, 0:nt], op=ALU.pow),
                 (P.t("rtiny", co), P.t("nhalf")), (P.t("rtiny", co),))

        def dg():
            for tt in range(nt):
                P.op("dve", I("tensor_scalar", out=diag[:, tt, :], in0=identf[:], scalar1=rtiny[:, co + tt:co + tt + 1],
                              scalar2=None, op0=ALU.mult), (P.t("rtiny", co), P.t("identf")), (P.t("diag", tt),))

        def bc():
            def fn2(e):
                ins = None
                for tt in range(nt):
                    ins = e.matmul(banks[b2][:, tt * 128:(tt + 1) * 128], lhsT=onesf[:], rhs=diag[:, tt, :],
                                   start=True, stop=True)
                return ins
            P.op("pe", fn2, tuple(P.t("diag", tt) for tt in range(nt)) + (P.t("onesf"),), (bk(b2),))

        stages = [(d0, lambda: sq(groups[0]), "A")]
        for gi in range(len(groups)):
            last_g = gi == len(groups) - 1

            def st(gi=gi, last_g=last_g):
                mm(groups[gi])
                if not last_g:
                    sq(groups[gi + 1])
                else:
                    fin()
            stages.append((2, st, "A"))
        stages.append((2, dg))
        stages.append((1, bc))
        return stages

    def norm_chain(l, n, seg_sh, seg_a, bi_):
        t0, sz = BLK[bi_]
        v = 2 if bi_ == 0 else n
        stages = rstd_stages(lambda c: xT[:, c, t0:t0 + sz], [P.t("x", c, bi_) for c in range(8)], sz, 8, 1.0 / D, d0=0,
                             co=4 * bi_)

        def apply():
            for fc in range(8):
                f2 = ft_ring.next()
                P.op("dve", I("tensor_tensor", out=ftr[:, f2, 0:sz], in0=xT[:, fc, t0:t0 + sz], in1=banks[7][:, 0:sz],
                              op=ALU.mult), (P.t("x", fc, bi_), bk(7)), (P.t("ft", f2),))
                P.op("act", I("activation", out=hT[:, fc, t0:t0 + sz], in_=ftr[:, f2, 0:sz], func=AF.Identity,
                              bias=mcol(l, seg_sh, fc, v), scale=mcol(l, seg_a, fc, v)),
                     (P.t("ft", f2), P.t("mod", l)), (P.t("h", fc, bi_),))
        stages.append((1, apply))
        return stages

    def final_chain(n, bi_):
        t0, sz = BLK[bi_]
        stages = rstd_stages(lambda c: xT[:, c, t0:t0 + sz], [P.t("x", c, bi_) for c in range(8)], sz, 8, 1.0 / D, d0=0,
                             co=4 * bi_)

        def apply():
            otr = P.rt("R2", "OST")
            for fc in range(8):
                P.op("dve", I("scalar_tensor_tensor", out=OST[:, fc, :], in0=xT[:, fc, t0:t0 + sz],
                              scalar=smalls[:, SM_FG + fc:SM_FG + fc + 1], in1=banks[7][:, 0:sz], op0=ALU.mult, op1=ALU.mult),
                     (P.t("x", fc, bi_), P.t("smalls"), bk(7)), (otr,))
            P.dma("sp", I("dma_start", out=yout[n, :, :, t0 - CTX:t0 - CTX + sz], in_=OST), (otr,), (P.t("yout"),), "out")
        stages.append((1, apply))
        return stages

    def proj_fm(b, wcols, wt, src_fn, src_trks, sz, nk):
        P.op("pe", mmgroup(banks[b][:, 0:sz], [(wcols(kc), src_fn(kc)) for kc in range(nk)]),
             (wt,) + tuple(src_trks), (bk(b),))
        DEF.tick()

    def h_src(bi_):
        DEF.ensure(("h", bi_))
        t0, sz = BLK[bi_]
        return (lambda kc: hT[:, kc, t0:t0 + sz]), [P.t("h", kc, bi_) for kc in range(8)]

    def gelu_chain(b, sz, out_ap, out_trks):
        X = banks[b][:, 0:sz]
        f1 = ft_ring.next()
        T1 = ftr[:, f1, 0:sz]
        P.op("act", I("activation", out=T1, in_=X, func=AF.Identity, scale=0.5), (bk(b),), (P.t("ft", f1),))
        P.op("act", I("activation", out=X, in_=X, func=AF.Square), (bk(b),), (bk(b),))
        P.op("dve", I("tensor_scalar", out=X, in0=X, scalar1=0.044715, scalar2=1.0, op0=ALU.mult, op1=ALU.add),
             (bk(b),), (bk(b),))
        P.op("dve", I("tensor_tensor", out=X, in0=X, in1=T1, op=ALU.mult), (bk(b), P.t("ft", f1)), (bk(b),))
        P.op("act", I("activation", out=X, in_=X, func=AF.Tanh, scale=2.0 * GELU_C), (bk(b),), (bk(b),))
        P.op("dve", I("scalar_tensor_tensor", out=out_ap, in0=X, scalar=1.0, in1=T1, op0=ALU.add, op1=ALU.mult),
             (bk(b), P.t("ft", f1)), tuple(out_trks))

    def proj_kv_items(l, hd, slot, ring):
        items = []
        holder = {}

        def get_w():
            if "w" not in holder:
                wv, wt = w_next(2048)
                holder["w"] = (wv.rearrange("p (s k c) -> p s k c", s=2, k=8), wt)
            return holder["w"]

        def k_item(bi_):
            def f():
                wv3, wt = get_w()
                t0, sz = BLK[bi_]
                src_fn, src_trks = h_src(bi_)
                b = ring.next()
                proj_fm(b, lambda kc: wv3[:, 0, kc, :], wt, src_fn, src_trks, sz, 8)
                dst = KT(slot)[:, t0:t0 + sz]
                dtrk = P.rt("R0", "KT", slot, bi_)
                if bi_ == 0:
                    P.op("act", I("activation", out=dst, in_=banks[b][:, 0:sz], func=AF.Copy), (bk(b),), (dtrk,))
                else:
                    rope_evac(b, sz, t0 - CTX, dst, dtrk, ring)
            return f

        def v_item(tiles):
            def f():
                wv3, wt = get_w()
                for t in tiles:
                    DEF.ensure(("h", tile_blk(t)))
                b = ring.next()
                for i, t in enumerate(tiles):
                    P.op("pe", mmgroup(banks[b][:, i * 128:(i + 1) * 128],
                                       [(hT[:, kc, t * 128:(t + 1) * 128], wv3[:, 1, kc, :]) for kc in range(8)]),
                         (wt,) + tuple(P.t("h", kc, tile_blk(t)) for kc in range(8)), (bk(b),))
                nt = len(tiles)
                dst = VV(slot)[:, tiles[0]:tiles[0] + nt, :]
                src = banks[b][:, 0:nt * 128].rearrange("p (t d) -> p t d", d=128)
                P.op("dve", I("tensor_copy", out=dst, in_=src), (bk(b),),
                     (P.rt("R0", "VV", slot, tile_blk(tiles[0])),))
            return f

        for bi_ in range(5):
            items.append(k_item(bi_))
        items.append(v_item([0, 1]))
        for g in range(4):
            items.append(v_item([2 + 4 * g + i for i in range(4)]))
        return items

    def rope_evac(b, sz, lt0, dst, dtrk, ring):
        X = banks[b][:, 0:sz]
        qi = PT_ring.next()
        qtk = P.rt("R2", "PT", qi)
        P.op("act", I("activation", out=PT[qi][:, 0:sz], in_=X, func=AF.Copy), (bk(b),), (qtk,))
        b2 = ring.next()
        P.op("pe", I("matmul", banks[b2][:, 0:sz], lhsT=perm[:], rhs=PT[qi][:, 0:sz], start=True, stop=True),
             (qtk, P.t("perm")), (bk(b2),))
        rtk = P.rt("R2", "ROPE")
        P.op("dve", I("tensor_tensor", out=X, in0=X, in1=ropeC[:, lt0:lt0 + sz], op=ALU.mult), (bk(b), rtk), (bk(b),))
        f1 = ft_ring.next()
        P.op("dve", I("tensor_tensor", out=ftr[:, f1, 0:sz], in0=banks[b2][:, 0:sz], in1=ropeS[:, lt0:lt0 + sz],
                                              op=ALU.mult), (bk(b2), rtk), (P.t("ft", f1),))
        P.op("dve", I("tensor_tensor", out=dst, in0=ftr[:, f1, 0:sz], in1=X, op=ALU.add),
             (P.t("ft", f1), bk(b)), (dtrk,))

    def proj_q_items(l, pair, ring, last):
        items = []
        holder = {}

        def get_w():
            if "w" not in holder:
                wv, wt = w_next(2048)
                holder["w"] = (wv.rearrange("p (k c) -> p k c", c=256), wt)
            return holder["w"]

        def q_item(hh, bi_):
            def f():
                wv3, wt = get_w()
                hd = pair * 2 + hh
                t0, sz = BLK[bi_]
                src_fn, src_trks = h_src(bi_)
                b = ring.next()
                proj_fm(b, lambda kc: wv3[:, kc, hh * 128:(hh + 1) * 128], wt, src_fn, src_trks, sz, 8)
                dst = QA[:, hd, t0:t0 + sz]
                dtrk = P.rt("R1", "QA", hd, bi_)
                if bi_ == 0:
                    P.op("act", I("activation", out=dst, in_=banks[b][:, 0:sz], func=AF.Copy), (bk(b),), (dtrk,))
                else:
                    rope_evac(b, sz, t0 - CTX, dst, dtrk, ring)
            return f

        for hh in range(2):
            for bi_ in range(5):
                if bi_ == 0 and last:
                    continue
                items.append(q_item(hh, bi_))
        return items

    def attention(l, n, hd, slot, last, background):
        qblocks = [1, 2, 3, 4] if last else [0, 1, 2, 3, 4]
        nlam = lamt[:, 8 + l:9 + l]
        subcol = lamt[:, 12 + l:13 + l]
        O = [0, 1]
        R = [2, 3]
        bg = list(background)
        nbound = [len(qblocks)]

        def bg_boundary():
            k = -(-len(bg) // max(1, nbound[0]))
            nbound[0] -= 1
            for _ in range(min(k, len(bg))):
                bg.pop(0)()

        def prep_q(qb_):
            q0_, qs_ = BLK[qb_]
            tr = P.rt("R1", "QA", hd, qb_)
            P.op("pool", I("tensor_copy", out=QP[0][0:64, 0:qs_], in_=QA[0:64, hd, q0_:q0_ + qs_]), (tr,), (P.rt("R2", "QP", 0),))
            P.op("pool", I("tensor_copy", out=QP[1][64:128, 0:qs_], in_=QA[64:128, hd, q0_:q0_ + qs_]), (tr,), (P.rt("R2", "QP", 1),))

        prep_q(qblocks[0])
        for qi_, qb in enumerate(qblocks):
            q0, qs = BLK[qb]
            kchunks = [0, 1] if qb == 0 else list(range(NTILE))
            qtrk = P.rt("R1", "QA", hd, qb)

            def scores(kc):
                res = []
                for m in range(2):
                    b = bank_S.next()
                    P.op("pe", I("matmul", banks[b][:, 0:qs], lhsT=KT(slot)[:, kc * 128:(kc + 1) * 128],
                                 rhs=QP[m][:, 0:qs], start=True, stop=True),
                         (P.rt("R0", "KT", slot, tile_blk(kc)), P.rt("R2", "QP", m)), (bk(b),))
                    pi = PT_ring.next()
                    P.op("act", I("activation", out=PT[pi][:, 0:qs], in_=banks[b][:, 0:qs], func=AF.Exp, scale=0.125, bias=-8.0),
                         (bk(b),), (P.rt("R2", "PT", pi),))
                    res.append(pi)
                return res

            def pv(kc, pis, first, lastk):
                vtrk = P.rt("R0", "VV", slot, tile_blk(kc))
                for m in range(2):
                    pi = pis[m]
                    P.op("pe", I("matmul", banks[O[m]][:, 0:qs], lhsT=VV(slot)[:, kc, :],
                                                               rhs=PT[pi][:, 0:qs], start=first, stop=lastk),
                         (vtrk, P.rt("R2", "PT", pi)), (bk(O[m]),))
                    P.op("pe", I("matmul", banks[R[m]][:, 0:qs], lhsT=ones[:],
                                                               rhs=PT[pi][:, 0:qs], start=first, stop=lastk),
                         (P.t("ones"), P.rt("R2", "PT", pi)), (bk(R[m]),))

            prev = scores(kchunks[0])
            for i, kc in enumerate(kchunks):
                nxt = scores(kchunks[i + 1]) if i + 1 < len(kchunks) else None
                if i + 2 == len(kchunks) and qi_ + 1 < len(qblocks):
                    prep_q(qblocks[qi_ + 1])
                pv(kc, prev, i == 0, i == len(kchunks) - 1)
                prev = nxt
                DEF.tick()

            def tail_T1(qs=qs):
                a1, a2 = AT[0][:, 0:qs], AT[1][:, 0:qs]
                t1, t2 = P.rt("R2", "AT", 0), P.rt("R2", "AT", 1)
                di, dj = bt_ring.next(), bt_ring.next()
                dt, dtj = P.t("bt", di), P.t("bt", dj)
                dd, dj_ = btr[:, di, 0:qs], btr[:, dj, 0:qs]
                P.op("act", I("activation", out=a1, in_=banks[R[1]][:, 0:qs], func=AF.Copy), (bk(R[1]),), (t1,))
                P.op("act", I("activation", out=a2, in_=banks[R[0]][:, 0:qs], func=AF.Copy), (bk(R[0]),), (t2,))
                P.op("act", I("activation", out=dd, in_=banks[R[1]][:, 0:qs], func=AF.Copy), (bk(R[1]),), (dt,))
                P.op("act", I("activation", out=dj_, in_=banks[R[0]][:, 0:qs], func=AF.Copy), (bk(R[0]),), (dtj,))
                P.op("dve", I("tensor_tensor", out=a1, in0=banks[O[0]][:, 0:qs], in1=a1, op=ALU.mult), (bk(O[0]), t1), (t1,))
                P.op("dve", I("scalar_tensor_tensor", out=a2, in0=banks[O[1]][:, 0:qs], scalar=nlam, in1=a2,
                              op0=ALU.mult, op1=ALU.mult), (bk(O[1]), t2, P.t("lamt")), (t2,))
                P.op("pool", I("tensor_tensor", out=a1, in0=a1, in1=a2, op=ALU.add), (t1, t2), (t1,))
                P.op("pool", I("tensor_tensor", out=dd, in0=dd, in1=dj_, op=ALU.mult), (dt, dtj), (dt,))
                P.op("pool", I("tensor_tensor", out=dd, in0=dd, in1=dd, op=ALU.mult), (dt,), (dt,))
                return di

            def tail_fin(qs=qs, q0=q0, qtrk=qtrk):
                a1 = AT[0][:, 0:qs]
                P.op("dve", I("scalar_tensor_tensor", out=QA[:, hd, q0:q0 + qs], in0=a1, scalar=subcol, in1=banks[7][:, 0:qs],
                              op0=ALU.mult, op1=ALU.mult), (P.rt("R2", "AT", 0), bk(7), P.t("lamt")), (qtrk,))

            DEF.flush()
            di_ = tail_T1()
            a1_ = AT[0][:, 0:qs]
            DEF.add(("tail", hd, qb), rstd_stages(lambda c, a1_=a1_: a1_, [P.rt("R2", "AT", 0)], qs, 1, 1.0 / 128, d0=2,
                                                  extra=(di_, epsc[:, 0:1]), eps_add=0.0) + [(1, tail_fin)])
            bg_boundary()
        while bg:
            bg.pop(0)()

    PT_ring = Ring(range(4))

    def phase_A(l, n, last):
        for r in ("R0", "R1", "R2"):
            P.release(r)
        P.op("pool", I("memset", QP[0][64:128, :], 0.0), (), (P.rt("R2", "QP", 0),))
        P.op("pool", I("memset", QP[1][0:64, :], 0.0), (), (P.rt("R2", "QP", 1),))
        rtk = P.rt("R2", "ROPE")
        P.dma("pool", I("dma_start", out=ropeC, in_=rope_d[0]), (), (rtk,), "rope")
        P.dma("pool", I("dma_start", out=ropeS, in_=rope_d[1]), (), (rtk,), "rope")
        for f in proj_kv_items(l, 0, 0, bank_all):
            f()
        for f in proj_q_items(l, 0, bank_all, last):
            f()
        attention(l, n, 0, 0, last, proj_kv_items(l, 1, 1, bank_B))
        attention(l, n, 1, 1, last, proj_q_items(l, 1, bank_B, last) + proj_kv_items(l, 2, 0, bank_B))
        attention(l, n, 2, 0, last, proj_kv_items(l, 3, 1, bank_B))
        attention(l, n, 3, 1, last, [])

    def phase_B(l, n, blocks):
        P.release("R0")
        P.release("R2")
        P.dma("pool", I("dma_start", out=wsT[:], in_=sgw_d[l]), (), (P.t("wsT"),), "sgp")
        P.dma("sp", I("dma_start", out=sgbf[:, 0:512], in_=sgb_d[l:l + 1, :]), (), (P.t("sgbf"),), "sgp2")
        P.dma("sp", I("dma_start", out=sggt[:], in_=sgg_d[l].partition_broadcast(128)), (), (P.t("sggt"),), "sgp3")
        P.op("dve", I("tensor_copy", out=sgbr[:, 0:512], in_=sgbf[:, 0:512]), (P.t("sgbf"),), (P.t("sgbr"),))
        P.op("dve", I("tensor_tensor", out=sgbf[:, 512:1024], in0=sgbf[:, 0:512], in1=sgbr[:, 0:512],
                                              op=ALU.subtract), (P.t("sgbf"), P.t("sgbr")), (P.t("sgbf"),))
        P.op("dve", I("tensor_copy", out=sgbr[:, 512:1024], in_=sgbf[:, 512:1024]), (P.t("sgbf"),), (P.t("sgbr"),))
        tiles = [t for t in range(NTILE) if tile_blk(t) in blocks]
        SQK = math.sqrt(0.044715)

        def run_skew(items, sched):
            n_ = len(items)
            maxlag = max(lg for _, lg in sched)
            for step in range(n_ + maxlag):
                for fn_, lg in sched:
                    i_ = step - lg
                    if 0 <= i_ < n_:
                        fn_(items[i_])

        def g_s1(it):
            f1 = ft_ring.next()
            it["f1"] = f1
            X, T1 = banks[it["b"]][:, 0:it["sz"]], ftr[:, f1, 0:it["sz"]]
            P.op("act", I("activation", out=T1, in_=X, func=AF.Identity, scale=0.5), (bk(it["b"]),), (P.t("ft", f1),))
            P.op("act", I("activation", out=X, in_=X, func=AF.Square, scale=SQK), (bk(it["b"]),), (bk(it["b"]),))

        def g_s2a(it):
            X, T1 = banks[it["b"]][:, 0:it["sz"]], ftr[:, it["f1"], 0:it["sz"]]
            P.op("dve", I("scalar_tensor_tensor", out=X, in0=X, scalar=1.0, in1=T1, op0=ALU.add, op1=ALU.mult),
                 (bk(it["b"]), P.t("ft", it["f1"])), (bk(it["b"]),))

        def g_s2b(it):
            X = banks[it["b"]][:, 0:it["sz"]]
            P.op("act", I("activation", out=X, in_=X, func=AF.Tanh, scale=2.0 * GELU_C), (bk(it["b"]),), (bk(it["b"]),))

        uw = {}

        def u_s0(it):
            if it["half"] not in uw:
                wv, wt = w_next(2048)
                uw[it["half"]] = (wv.rearrange("p (k c) -> p k c", c=256), wt)
            wv3, wt = uw[it["half"]]
            t0, sz = BLK[it["bi"]]
            src_fn, src_trks = h_src(it["bi"])
            b = bank_all.next()
            it["b"], it["sz"] = b, sz
            gg = it["gg"]
            proj_fm(b, lambda kc: wv3[:, kc, gg * 128:(gg + 1) * 128], wt, src_fn, src_trks, sz, 8)

        def u_s3(it):
            t0, sz = BLK[it["bi"]]
            X, T1 = banks[it["b"]][:, 0:sz], ftr[:, it["f1"], 0:sz]
            P.op("dve", I("scalar_tensor_tensor", out=UB[:, it["g"], t0:t0 + sz], in0=X, scalar=1.0, in1=T1,
                          op0=ALU.add, op1=ALU.mult), (bk(it["b"]), P.t("ft", it["f1"])),
                 tuple(P.rt("R2", "UB", it["g"], t) for t in range(t0 // 128, (t0 + sz) // 128)))

        uitems = [dict(half=half, gg=gg, g=half * 2 + gg, bi=bi_) for half in range(2) for gg in range(2) for bi_ in blocks]
        run_skew(uitems, [(u_s3, 3), (g_s2a, 2), (g_s1, 1), (g_s2b, 2), (u_s0, 0)])

        sw = {}

        def v_s0(it):
            if "w" not in sw:
                wv0, wt0 = w_next(2048)
                wv1, wt1 = w_next(2048, hold=1)
                sw["w"] = ([wv0.rearrange("p (k c) -> p k c", c=256), wv1.rearrange("p (k c) -> p k c", c=256)], [wt0, wt1])
            wvs, wts = sw["w"]
            t = it["t"]
            DEF.ensure(("h", tile_blk(t)))
            b = bank_all.next()
            it["b"], it["sz"] = b, 512
            for half in range(2):
                P.op("pe", mmgroup(banks[b][:, half * 256:(half + 1) * 256],
                                   [(hT[:, kc, t * 128:(t + 1) * 128], wvs[half][:, kc, :]) for kc in range(8)]),
                     (wts[half],) + tuple(P.t("h", kc, tile_blk(t)) for kc in range(8)), (bk(b),))

        def v_s3(it):
            X, G = banks[it["b"]][:, :], ftr[:, it["f1"], :]
            si = stat_ring.next()
            it["si"] = si
            st = P.t("stat", si)
            gt = P.t("ft", it["f1"])
            P.op("dve", I("scalar_tensor_tensor", out=G, in0=X, scalar=1.0, in1=G, op0=ALU.add, op1=ALU.mult),
                 (bk(it["b"]), gt), (gt,))
            P.op("dve", I("bn_stats", out=stat[:, si, 0:6], in_=G), (gt,), (st,))
            P.op("dve", I("bn_aggr", out=stat[:, si, 8:10], in_=stat[:, si, 0:6]), (st,), (st,))
            P.op("dve", I("tensor_scalar", out=stat[:, si, 10:11], in0=stat[:, si, 9:10], scalar1=EPS, scalar2=None,
                          op0=ALU.add), (st,), (st,))
            P.op("pool", I("tensor_tensor", out=stat[:, si, 10:11], in0=stat[:, si, 10:11], in1=nhalf[:, 0:1], op=ALU.pow),
                 (st, P.t("nhalf")), (st,))

        def v_s4a(it):
            si = it["si"]
            st = P.t("stat", si)
            P.op("dve", I("scalar_tensor_tensor", out=stat[:, si, 11:12], in0=stat[:, si, 8:9], scalar=-1.0,
                          in1=stat[:, si, 10:11], op0=ALU.mult, op1=ALU.mult), (st,), (st,))

        def v_s4b(it):
            si = it["si"]
            G = ftr[:, it["f1"], :]
            gt = P.t("ft", it["f1"])
            P.op("act", I("activation", out=G, in_=G, func=AF.Identity, bias=stat[:, si, 11:12], scale=stat[:, si, 10:11]),
                 (gt, P.t("stat", si)), (gt,))

        def v_s4c(it):
            G = ftr[:, it["f1"], :]
            P.op("pool", I("tensor_tensor", out=SGV[:, it["t"], :], in0=G, in1=sggt[:], op=ALU.mult),
                 (P.t("ft", it["f1"]), P.t("sggt")), (P.rt("R0", "SGV", it["t"]),))

        vitems = [dict(t=t) for t in tiles]
        run_skew(vitems, [(v_s4a, 4), (v_s4b, 4), (v_s3, 3), (v_s4c, 4), (g_s2a, 2), (g_s1, 1), (g_s2b, 2), (v_s0, 0)])
        for t in tiles:
            b = bank_all.next()
            for g in range(4):
                def fn(e, g=g, t=t, b=b):
                    o = banks[b][:, g * 128:(g + 1) * 128]
                    e.matmul(o, lhsT=SGV[:, t, g * 128:(g + 1) * 128], rhs=wsT[:, g * 128:(g + 1) * 128],
                             start=True, stop=False)
                    e.matmul(o, lhsT=ones[0:1, :], rhs=sgbr[0:1, g * 128:(g + 1) * 128], start=False, stop=False)
                    return e.matmul(o, lhsT=ones[0:1, :], rhs=sgbr[0:1, 512 + g * 128:512 + (g + 1) * 128],
                                    start=False, stop=True)
                P.op("pe", fn, (P.rt("R0", "SGV", t), P.t("wsT"), P.t("ones"), P.t("sgbr")), (bk(b),))
            utr = [P.rt("R2", "UB", g, t) for g in range(4)]
            P.op("dve", I("tensor_tensor",
                out=UB[:, :, t * 128:(t + 1) * 128], in0=UB[:, :, t * 128:(t + 1) * 128],
                in1=banks[b][:, :].rearrange("p (g q) -> p g q", q=128), op=ALU.mult), tuple(utr) + (bk(b),), tuple(utr))

    def phase_C(l, n, blocks, on_final):
        P.release("R0")
        for jg in range(2):
            for jp in range(2):
                j0 = jg * 4 + jp * 2
                wva, wta = w_next(2048)
                was = wva.rearrange("p (j s k c) -> p j s k c", j=2, s=2, k=4)
                for jj in range(2):
                    j = j0 + jj
                    wvg, wtg = w_next(2048, hold=1 + jj)
                    wg = wvg.rearrange("p (s k c) -> p s k c", s=2, k=8)
                    mj = jp * 2 + jj
                    for bi_ in blocks:
                        t0, sz = BLK[bi_]
                        src_fn, src_trks = h_src(bi_)
                        tl = list(range(t0 // 128, (t0 + sz) // 128))
                        bga, ba, bgb, bs = bank_all.next(), bank_all.next(), bank_all.next(), bank_all.next()
                        proj_fm(bga, lambda kc: wg[:, 0, kc, :], wtg, src_fn, src_trks, sz, 8)
                        proj_fm(ba, lambda kc: was[:, jj, 0, kc, :], wta, lambda kc: QA[:, kc, t0:t0 + sz],
                                [P.rt("R1", "QA", kc, bi_) for kc in range(4)], sz, 4)
                        proj_fm(bgb, lambda kc: wg[:, 1, kc, :], wtg, src_fn, src_trks, sz, 8)
                        proj_fm(bs, lambda kc: was[:, jj, 1, kc, :], wta, lambda kc: UB[:, kc, t0:t0 + sz],
                                [P.rt("R2", "UB", kc, t) for kc in range(4) for t in tl], sz, 4)
                        f1, f2 = ft_ring.next(), ft_ring.next()
                        T1, T2 = ftr[:, f1, 0:sz], ftr[:, f2, 0:sz]
                        P.op("act", I("activation", out=T1, in_=banks[bga][:, 0:sz], func=AF.Tanh,
                                                                           scale=0.5), (bk(bga),), (P.t("ft", f1),))
                        P.op("act", I("activation", out=T2, in_=banks[bgb][:, 0:sz], func=AF.Tanh,
                                                                           scale=0.5), (bk(bgb),), (P.t("ft", f2),))
                        P.op("dve", I("scalar_tensor_tensor",
                            out=T1, in0=T1, scalar=1.0, in1=banks[ba][:, 0:sz], op0=ALU.add, op1=ALU.mult),
                             (P.t("ft", f1), bk(ba)), (P.t("ft", f1),))
                        P.op("dve", I("scalar_tensor_tensor",
                            out=T2, in0=T2, scalar=1.0, in1=banks[bs][:, 0:sz], op0=ALU.add, op1=ALU.mult),
                             (P.t("ft", f2), bk(bs)), (P.t("ft", f2),))
                        P.op("pool", I("tensor_tensor",
                            out=MG[:, mj, t0:t0 + sz], in0=T1, in1=T2, op=ALU.add),
                             (P.t("ft", f1), P.t("ft", f2)), (P.rt("R0", "MG", mj, bi_),))
            for half in range(2):
                wvo, wto = w_next(2048)
                wo = wvo.rearrange("p (k c) -> p k c", c=512)
                for bi_ in blocks:
                    t0, sz = BLK[bi_]
                    v = 2 if bi_ == 0 else n
                    for ff in range(4):
                        fo = half * 4 + ff
                        b = bank_all.next()
                        proj_fm(b, lambda kc: wo[:, kc, ff * 128:(ff + 1) * 128], wto, lambda kc: MG[:, kc, t0:t0 + sz],
                                [P.rt("R0", "MG", kc, bi_) for kc in range(4)], sz, 4)
                        P.op("dve", I("scalar_tensor_tensor",
                            out=xT[:, fo, t0:t0 + sz], in0=banks[b][:, 0:sz], scalar=mcol(l, 2, fo, v),
                            in1=xT[:, fo, t0:t0 + sz], op0=ALU.mult, op1=ALU.add),
                             (bk(b), P.t("mod", l), P.t("x", fo, bi_)), (P.t("x", fo, bi_),))
                    if jg == 1 and half == 1:
                        on_final(bi_)
            if jg == 0:
                P.release("R0")

    def phase_F(l, n, blocks, on_final, side=()):
        P.release("R0")
        P.release("R1")
        P.release("R2")
        side = list(side)

        def side_step():
            if side:
                side.pop(0)()
        for gi, (c0, ng) in enumerate(FFN_GROUPS):
            if gi > 0:
                P.release("R0")
                P.release("R1")
            for cc in range(ng):
                wv, wt = w_next(2048)
                wf = wv.rearrange("p (s k c) -> p s k c", s=2, k=8)
                reg = "R0" if cc < 4 else "R1"
                for bi_ in blocks:
                    t0, sz = BLK[bi_]
                    src_fn, src_trks = h_src(bi_)
                    ba, bb = bank_all.next(), bank_all.next()
                    proj_fm(ba, lambda kc: wf[:, 0, kc, :], wt, src_fn, src_trks, sz, 8)
                    proj_fm(bb, lambda kc: wf[:, 1, kc, :], wt, src_fn, src_trks, sz, 8)
                    f1 = ft_ring.next()
                    T1 = ftr[:, f1, 0:sz]
                    P.op("act", I("activation", out=T1, in_=banks[ba][:, 0:sz], func=AF.Tanh, scale=0.5),
                         (bk(ba),), (P.t("ft", f1),))
                    P.op("dve", I("scalar_tensor_tensor",
                        out=T1, in0=T1, scalar=1.0, in1=banks[ba][:, 0:sz], op0=ALU.add, op1=ALU.mult),
                         (P.t("ft", f1), bk(ba)), (P.t("ft", f1),))
                    P.op("dve", I("tensor_tensor",
                        out=ACTF[:, cc, t0:t0 + sz], in0=T1, in1=banks[bb][:, 0:sz], op=ALU.mult),
                         (P.t("ft", f1), bk(bb)), (P.rt(reg, "ACTF", cc, bi_),))
                side_step()
            for qd in range(4):
                wv, wt = w_next(ng * 256)
                wo = wv[:, 0:ng * 256].rearrange("p (k c) -> p k c", c=256)
                for bi_ in blocks:
                    t0, sz = BLK[bi_]
                    v = 2 if bi_ == 0 else n
                    for ff in range(2):
                        fo = qd * 2 + ff
                        b = bank_all.next()
                        proj_fm(b, lambda kc: wo[:, kc, ff * 128:(ff + 1) * 128], wt, lambda kc: ACTF[:, kc, t0:t0 + sz],
                                [P.rt("R0" if kc < 4 else "R1", "ACTF", kc, bi_) for kc in range(ng)], sz, ng)
                        P.op("dve", I("scalar_tensor_tensor",
                            out=xT[:, fo, t0:t0 + sz], in0=banks[b][:, 0:sz], scalar=mcol(l, 5, fo, v),
                            in1=xT[:, fo, t0:t0 + sz], op0=ALU.mult, op1=ALU.add),
                             (bk(b), P.t("mod", l), P.t("x", fo, bi_)), (P.t("x", fo, bi_),))
                    if gi == len(FFN_GROUPS) - 1 and qd == 3:
                        on_final(bi_)
                side_step()

    for n in range(nb):
        for fc in range(8):
            P.dma("sp", I("dma_start", out=xT[:, fc, :], in_=xin[n, :, fc, :]), (),
                  tuple(P.t("x", fc, bi_) for bi_ in range(5)), "xl%d" % fc)
        for bi_ in range(5):
            DEF.add(("h", bi_), norm_chain(0, n, 0, 1, bi_))
        for l in range(depth):
            last = (l == depth - 1)
            blocks = [1, 2, 3, 4] if last else [0, 1, 2, 3, 4]
            dbg = debug and n == 0 and l == depth - 1
            phase_A(l, n, last)
            DEF.flush()
            if dbg:
                dump("d_h1", hT[:], [128, 8, T], BF16, [P.t("h", kc, b_) for kc in range(8) for b_ in range(5)])
                dump("d_qa", arena[:, 9216:18432], [128, 9216], BF16, [P.rt("R1", "QA", h_, b_) for h_ in range(4) for b_ in blocks])
                dump("d_kv", arena[:, 0:9216], [128, 9216], BF16, [P.rt("R0", "KT", s_, b_) for s_ in range(2) for b_ in range(5)] + [P.rt("R0", "VV", s_, b_) for s_ in range(2) for b_ in range(5)])
            phase_B(l, n, blocks)
            if dbg:
                dump("d_ub", arena[:, 18432:27648], [128, 9216], BF16, [P.rt("R2", "UB", g_, t_) for g_ in range(4) for t_ in range(NTILE) if tile_blk(t_) in blocks])
                dump("d_sgv", arena[:, 0:9216], [128, 9216], BF16, [P.rt("R0", "SGV", t_) for t_ in range(NTILE) if tile_blk(t_) in blocks])
            phase_C(l, n, blocks, lambda b_, l=l, n=n: DEF.add(("h", b_), norm_chain(l, n, 3, 4, b_)))
            if dbg:
                DEF.flush()
                dump("d_x1", xT[:], [128, 8, T], F32, [P.t("x", fc, b_) for fc in range(8) for b_ in range(5)])
                dump("d_h2", hT[:], [128, 8, T], BF16, [P.t("h", kc, b_) for kc in range(8) for b_ in range(5)])
            if last:
                onf = lambda b_, n=n: DEF.add(("fin", b_), final_chain(n, b_))
            else:
                onf = lambda b_, l=l, n=n: DEF.add(("h", b_), norm_chain(l + 1, n, 0, 1, b_))
            if n == 0 and l + 1 < depth:
                bank_all.items = list(range(5))
                phase_F(l, n, blocks, onf, ada_items(l + 1, 5))
                bank_all.items = list(range(6))
            else:
                phase_F(l, n, blocks, onf)
            if dbg:
                dump("d_x2", xT[:], [128, 8, T], F32, [P.t("x", fc, b_) for fc in range(8) for b_ in range(5)])
                dump("d_mod", mod[:], [128, depth, 48, 3], F32, [P.t("mod", l_) for l_ in range(depth)])
                dump("d_lamt", lamt[:], [128, 16], F32, [P.t("lamt")])
        DEF.flush()
    P.wait_all("sp", [P.t("yout")])
    assert wstate["cur"] == len(TILES), (wstate["cur"], len(TILES))
    P.emit()
    print("sbuf bytes remaining/partition:", nc.sbuf_bytes_remaining, "ops:", {k: len(v) for k, v in P.q.items()})
    return nc


def prepare_shared(inp, depth):
    specs = layer_tile_specs()
    wst = []
    for l in range(depth):
        parts = []
        for (_, ps) in specs:
            for (mname, k0, nk, c0, ncols) in ps:
                parts.append(pack_part(inp[mname][l], k0, nk, c0, ncols))
        wst.append(np.concatenate(parts, axis=1))
    wst = np.ascontiguousarray(np.stack(wst, 0), dtype=np.float32)
    wada = np.stack([np.stack([pack_part(inp["ada_w"][l], 0, 8, jt * 256, 256) for jt in range(24)], 0)
                     for l in range(depth)], 0).astype(np.float32)
    lams = np.stack([inp["lambda_q1"][:depth], inp["lambda_k1"][:depth], inp["lambda_q2"][:depth],
                     inp["lambda_k2"][:depth]], 0)
    if depth < 4:
        lams = np.concatenate([lams, np.zeros((4, 4 - depth, 64), np.float32)], 1)
    lams = np.ascontiguousarray(lams.reshape(1024), dtype=np.float32)
    sgw = np.ascontiguousarray(inp["sg_w"][:depth].transpose(0, 3, 1, 2).reshape(depth, 128, 512), dtype=np.float32)
    sgb = np.ascontiguousarray(inp["sg_b"][:depth].reshape(depth, 512), dtype=np.float32)
    sgg = np.ascontiguousarray(inp["sg_norm_g"][:depth], dtype=np.float32)
    return dict(wst=wst, wada=np.ascontiguousarray(wada), lams=lams, sgw=sgw, sgb=sgb, sgg=sgg,
                rope=rope_tables(), perm=perm_matrix(), ident=np.eye(128, dtype=np.float32))


def cols(v):
    R = v.shape[0]
    k = v.shape[1] // 128
    return v.reshape(R, k, 128).transpose(2, 0, 1).reshape(128, R * k)


def prepare_core(inp, depth, bsel):
    x = inp["x"][bsel]
    ctx = inp["ctx"][bsel]
    nb = len(bsel)
    xc = np.concatenate([ctx, x], axis=1)
    xin = np.ascontiguousarray(xc.reshape(nb, T, 8, 128).transpose(0, 3, 2, 1), dtype=np.float32)
    cs = [inp["c"][b] for b in bsel]
    while len(cs) < 2:
        cs.append(np.zeros(D, np.float32))
    cmat = np.stack(cs + [inp["c_ctx"]], 0)

    def padl(a):
        a = a[:depth]
        if depth < 4:
            a = np.concatenate([a, np.zeros((4 - depth,) + a.shape[1:], a.dtype)], 0)
        return a

    sm = np.concatenate([cols(cmat), cols(padl(inp["norm1_g"])), cols(padl(inp["norm2_g"])),
                         cols(inp["final_g"][None, :]), cols(padl(inp["ada_b"])),
                         cols(padl(inp["subln_g"]))], axis=1)
    return dict(xin=xin, smalls=np.ascontiguousarray(sm, dtype=np.float32))


_PROG_CACHE = {}


def run(inp, depth=4, ncores=NCORES, nb=2, trace=False):
    inp = {k: np.asarray(v, dtype=np.float32) for k, v in inp.items()}
    key = (depth, nb)
    if key not in _PROG_CACHE:
        _PROG_CACHE[key] = build_program(depth, nb)
    nc = _PROG_CACHE[key]
    shared = prepare_shared(inp, depth)
    in_maps = []
    for c in range(ncores):
        m = dict(shared)
        m.update(prepare_core(inp, depth, list(range(c * nb, (c + 1) * nb))))
        in_maps.append(m)
    res = run_bass_kernel_spmd(nc, in_maps, core_ids=list(range(ncores)), trace=trace)
    outs = []
    for c in range(ncores):
        y = res.results[c]["yout"]
        outs.append(np.asarray(y).transpose(0, 3, 2, 1).reshape(nb, SEQ, D))
    return np.ascontiguousarray(np.concatenate(outs, 0), dtype=np.float32), res


def kernel(**inputs):
    out, _ = run(inputs, depth=4, ncores=NCORES, nb=2)
    return out
```

```python
import math
import numpy as np
import concourse.bass as bass
import concourse.mybir as mybir
from concourse.bass_utils import run_bass_kernel_spmd

F32 = mybir.dt.float32
BF16 = mybir.dt.bfloat16
AF = mybir.ActivationFunctionType
ALU = mybir.AluOpType
AX = mybir.AxisListType

D = 1024
SEQ = 2048
CTX = 256
T = SEQ + CTX
DFF = 2816
NCORES = 8
EPS = 1e-6
BLK = [(0, 256), (256, 512), (768, 512), (1280, 512), (1792, 512)]
NTILE = T // 128
GELU_C = math.sqrt(2.0 / math.pi)
FFN_GROUPS = [(0, 8), (8, 8), (16, 6)]
WSLOT = 2048
NWS = 4


def tile_blk(t):
    return 0 if t < 2 else 1 + (t - 2) // 4


def lam_init(i):
    return 0.8 - 0.6 * math.exp(-0.3 * i)


def layer_tile_specs():
    tl = []
    K_OFF, V_OFF, Q_OFF, U_OFF, SGV_OFF, GATE_OFF = 0, 512, 1024, 1536, 2048, 2560

    def kv(h):
        return ("KV%d" % h, [("w_in", 0, 8, K_OFF + h * 128, 128), ("w_in", 0, 8, V_OFF + h * 128, 128)])

    def q(p):
        return ("Q%d" % p, [("w_in", 0, 8, Q_OFF + p * 256, 256)])

    tl += [kv(0), q(0), kv(1), q(1), kv(2), kv(3)]
    tl += [("U0", [("w_in", 0, 8, U_OFF, 256)]), ("U1", [("w_in", 0, 8, U_OFF + 256, 256)]),
           ("SGV0", [("w_in", 0, 8, SGV_OFF, 256)]), ("SGV1", [("w_in", 0, 8, SGV_OFF + 256, 256)])]
    for jg in range(2):
        for jp in range(2):
            j0 = jg * 4 + jp * 2
            tl.append(("AS%d" % j0, [("w_branch_attn", 0, 4, j0 * 128, 128), ("w_branch_sg", 0, 4, j0 * 128, 128),
                                      ("w_branch_attn", 0, 4, (j0 + 1) * 128, 128),
                                      ("w_branch_sg", 0, 4, (j0 + 1) * 128, 128)]))
            for j in (j0, j0 + 1):
                tl.append(("G%d" % j, [("w_in", 0, 8, GATE_OFF + j * 128, 128),
                                       ("w_in", 0, 8, GATE_OFF + 1024 + j * 128, 128)]))
        for half in range(2):
            tl.append(("WO%d_%d" % (jg, half), [("w_out", jg * 4, 4, half * 512, 512)]))
    for (c0, ng) in FFN_GROUPS:
        for c in range(c0, c0 + ng):
            tl.append(("F%d" % c, [("w_ffn_in", 0, 8, c * 128, 128), ("w_ffn_in", 0, 8, DFF + c * 128, 128)]))
        for qd in range(4):
            tl.append(("FO%d_%d" % (c0, qd), [("w_ffn_out", c0, ng, qd * 256, 256)]))
    return tl


def pack_part(W, k0, nk, col0, ncols):
    blk = W[k0 * 128:(k0 + nk) * 128, col0:col0 + ncols]
    return blk.reshape(nk, 128, ncols).transpose(1, 0, 2).reshape(128, nk * ncols)


def rope_tables():
    t = np.arange(SEQ)
    row = (t // 64).astype(np.float32)
    col = (t % 64).astype(np.float32)
    half = 32
    inv = (1.0 / (np.float32(10000.0) ** (np.arange(0, half, 2, dtype=np.float32) / np.float32(half)))).astype(np.float32)
    ang_r = row[:, None] * inv[None, :]
    ang_c = col[:, None] * inv[None, :]
    C = np.zeros((64, SEQ), np.float32)
    S_ = np.zeros((64, SEQ), np.float32)
    for d in range(64):
        j = d % 16
        a = ang_r[:, j] if d < 32 else ang_c[:, j]
        C[d] = np.cos(a.astype(np.float32))
        S_[d] = np.sin(a.astype(np.float32))
    C = np.concatenate([C, C], 0)
    S_ = np.concatenate([S_, S_], 0)
    return np.stack([C, S_], 0).astype(np.float32)


def perm_matrix():
    P = np.zeros((128, 128), np.float32)
    for fp in range(128):
        d = fp % 32
        if d < 16:
            P[fp + 16, fp] = -1.0
        else:
            P[fp - 16, fp] = 1.0
    return P


class Trk:
    __slots__ = ("w", "r", "excl")

    def __init__(self, fence=None):
        self.w = None
        self.r = dict(fence) if fence else {}
        self.excl = False


class Ring:
    def __init__(self, items):
        self.items = list(items)
        self.i = 0

    def next(self):
        it = self.items[self.i % len(self.items)]
        self.i += 1
        return it


class Prog:
    ENG = ("pe", "act", "dve", "pool", "sp")

    def __init__(self, nc):
        self.nc = nc
        self.q = {e: [] for e in self.ENG}
        self.cnt = {e: 0 for e in self.ENG}
        self.seen = {e: {} for e in self.ENG}
        self.semh = {}
        self.dcnt = {}
        self.trk = {}
        self.region_trk = {}
        self.region_fence = {}

    def sem(self, name):
        if name not in self.semh:
            self.semh[name] = self.nc.alloc_semaphore("s_" + name)
        return self.semh[name]

    def t(self, *key):
        if key not in self.trk:
            self.trk[key] = Trk()
        return self.trk[key]

    def rt(self, region, *key):
        k = (region,) + key
        if k not in self.trk:
            self.trk[k] = Trk(self.region_fence.get(region))
            self.region_trk.setdefault(region, []).append(k)
        return self.trk[k]

    def release(self, region):
        fence = dict(self.region_fence.get(region, {}))
        for k in self.region_trk.get(region, []):
            tr = self.trk.pop(k)
            if tr.w is not None and fence.get(tr.w[0], 0) < tr.w[1]:
                fence[tr.w[0]] = tr.w[1]
            for s, v in tr.r.items():
                if fence.get(s, 0) < v:
                    fence[s] = v
        self.region_trk[region] = []
        self.region_fence[region] = fence

    def merge_fences(self, regions):
        u = {}
        for r in regions:
            for s, v in self.region_fence.get(r, {}).items():
                if u.get(s, 0) < v:
                    u[s] = v
        for r in regions:
            self.region_fence[r] = dict(u)

    def _deps(self, eng, reads, writes):
        need = {}

        def add(s, v):
            if need.get(s, 0) < v:
                need[s] = v

        for t in reads:
            if t.w is not None:
                add(*t.w)
            if t.excl:
                for s, v in t.r.items():
                    if s != eng:
                        add(s, v)
        for t in writes:
            if t.w is not None:
                add(*t.w)
            for s, v in t.r.items():
                add(s, v)
        waits = []
        for s, v in need.items():
            if s == eng and eng == "pe":
                continue
            if self.seen[eng].get(s, 0) >= v:
                continue
            self.seen[eng][s] = v
            waits.append((s, v))
        return waits

    def _mark(self, tok, reads, writes):
        s, v = tok
        for t in reads:
            if t.r.get(s, 0) < v:
                t.r[s] = v
        for t in writes:
            t.w = tok
            t.r = {}

    def op(self, eng, fn, reads=(), writes=()):
        waits = self._deps(eng, reads, writes)
        self.cnt[eng] += 1
        tok = (eng, self.cnt[eng])
        self.q[eng].append(("op", waits, fn, None))
        self._mark(tok, reads, writes)
        return tok

    def dma(self, eng, fn, reads, writes, sem):
        waits = self._deps(eng, reads, writes)
        self.dcnt[sem] = self.dcnt.get(sem, 0) + 16
        tok = (sem, self.dcnt[sem])
        self.q[eng].append(("dma", waits, fn, sem))
        self._mark(tok, reads, writes)
        return tok

    def wait_all(self, eng, trks):
        waits = self._deps(eng, trks, ())
        self.q[eng].append(("wait", waits, None, None))

    def emit(self):
        nc = self.nc
        for e in ("pe", "act", "dve", "pool"):
            self.sem(e)
        for s in list(self.dcnt.keys()):
            self.sem(s)
        with nc.Block() as block:
            engmap = {"pe": block.tensor, "act": block.scalar, "dve": block.vector,
                      "pool": block.gpsimd, "sp": block.sync}
            for name in self.ENG:
                items = self.q[name]
                if not items:
                    continue

                def body(e, items=items, name=name):
                    for kind, waits, fn, sem in items:
                        for (s, v) in waits:
                            e.wait_ge(self.semh[s], v)
                        if kind == "wait":
                            continue
                        if isinstance(fn, tuple):
                            ins = getattr(e, fn[0])(*fn[1], **fn[2])
                        else:
                            ins = fn(e)
                        if kind == "op":
                            ins.then_inc(self.semh[name], 1)
                        else:
                            ins.then_inc(self.semh[sem], 16)

                engmap[name](body)


class Deferred:
    def __init__(self):
        self.chains = []
        self.busy = False

    def add(self, key, stages):
        stages = list(stages)
        self.chains.append(dict(key=key, stages=stages, idx=0, wait=stages[0][0]))

    def _step(self, ch, only_a=False):
        if ch["wait"] > 0:
            ch["wait"] -= 1
            return False
        while True:
            st = ch["stages"][ch["idx"]]
            if only_a and not (len(st) > 2 and st[2] == "A"):
                return False
            st[1]()
            ch["idx"] += 1
            if ch["idx"] >= len(ch["stages"]):
                return True
            ch["wait"] = ch["stages"][ch["idx"]][0]
            if ch["wait"] > 0:
                return False

    def _in_b(self, ch):
        st = ch["stages"][ch["idx"]]
        return not (len(st) > 2 and st[2] == "A")

    def tick(self):
        if self.busy or not self.chains:
            return
        self.busy = True
        head = self.chains[0]
        overlap = len(self.chains) > 1 and self._in_b(head)
        if self._step(head):
            self.chains.pop(0)
        elif overlap:
            self._step(self.chains[1], only_a=True)
        self.busy = False

    def pending(self, key):
        return any(c["key"] == key for c in self.chains)

    def ensure(self, key):
        while self.pending(key):
            self.tick()

    def flush(self):
        while self.chains:
            self.tick()


def I(name, *a, **kw):
    return (name, a, kw)


def mmgroup(out_ap, pairs):
    pairs = list(pairs)

    def fn(e):
        n = len(pairs)
        ins = None
        for i, (l, r) in enumerate(pairs):
            ins = e.matmul(out_ap, lhsT=l, rhs=r, start=(i == 0), stop=(i == n - 1))
        return ins

    return fn


def build_program(depth=4, nb=2, debug=False):
    nc = bass.Bass("TRN2", target_bir_lowering=False)
    specs = layer_tile_specs()
    tile_sizes = [sum(nk * ncols for (_, _, nk, _, ncols) in parts) for (_, parts) in specs]
    tile_offs = np.concatenate([[0], np.cumsum(tile_sizes)]).astype(int)
    E = int(tile_offs[-1])
    NSM = 24 + 32 + 32 + 8 + 192 + 4
    SM_C, SM_N1, SM_N2, SM_FG, SM_AB, SM_SUB = 0, 24, 56, 88, 96, 288

    xin = nc.dram_tensor("xin", [nb, 128, 8, T], F32, kind="ExternalInput").ap()
    yout = nc.dram_tensor("yout", [nb, 128, 8, SEQ], F32, kind="ExternalOutput").ap()
    smalls_d = nc.dram_tensor("smalls", [128, NSM], F32, kind="ExternalInput").ap()
    lams_d = nc.dram_tensor("lams", [1024], F32, kind="ExternalInput").ap()
    sgw_d = nc.dram_tensor("sgw", [depth, 128, 512], F32, kind="ExternalInput").ap()
    sgb_d = nc.dram_tensor("sgb", [depth, 512], F32, kind="ExternalInput").ap()
    sgg_d = nc.dram_tensor("sgg", [depth, 512], F32, kind="ExternalInput").ap()
    rope_d = nc.dram_tensor("rope", [2, 128, SEQ], F32, kind="ExternalInput").ap()
    perm_d = nc.dram_tensor("perm", [128, 128], F32, kind="ExternalInput").ap()
    ident_d = nc.dram_tensor("ident", [128, 128], F32, kind="ExternalInput").ap()
    wada_d = nc.dram_tensor("wada", [depth, 24, 128, 2048], F32, kind="ExternalInput").ap()
    wst_d = nc.dram_tensor("wst", [depth, 128, E], F32, kind="ExternalInput").ap()

    xT = nc.alloc_sbuf_tensor("xT", [128, 8, T], F32)
    hT = nc.alloc_sbuf_tensor("hT", [128, 8, T], BF16)
    arena = nc.alloc_sbuf_tensor("arena", [128, 27648], BF16)
    wring = nc.alloc_sbuf_tensor("wring", [128, NWS, WSLOT], BF16)
    btr = nc.alloc_sbuf_tensor("btr", [128, 4, 512], BF16)
    ftr = nc.alloc_sbuf_tensor("ftr", [128, 4, 512], F32)
    smalls = nc.alloc_sbuf_tensor("smalls_sb", [128, NSM], F32)
    mod = nc.alloc_sbuf_tensor("mod", [128, depth, 48, 3], F32)
    lamt = nc.alloc_sbuf_tensor("lamt", [128, 16], F32)
    scT = nc.alloc_sbuf_tensor("scT", [128, 8, 3], BF16)
    wsT = nc.alloc_sbuf_tensor("wsT", [128, 512], BF16)
    sgbr = nc.alloc_sbuf_tensor("sgbr", [1, 1024], BF16)
    sgbf = nc.alloc_sbuf_tensor("sgbf", [1, 1024], F32)
    sggt = nc.alloc_sbuf_tensor("sggt", [128, 512], F32)
    perm = nc.alloc_sbuf_tensor("perm_sb", [128, 128], BF16)
    ones = nc.alloc_sbuf_tensor("ones_sb", [128, 128], BF16)
    nhalf = nc.alloc_sbuf_tensor("nhalf", [128, 512], BF16)
    stat = nc.alloc_sbuf_tensor("stat", [128, 4, 16], F32)
    identf = nc.alloc_sbuf_tensor("identf", [128, 128], F32)
    onesf = nc.alloc_sbuf_tensor("onesf", [128, 128], F32)
    diag = nc.alloc_sbuf_tensor("diag", [128, 4, 128], F32)
    rtiny = nc.alloc_sbuf_tensor("rtiny", [128, 24], F32)
    epsc = nc.alloc_sbuf_tensor("epsc", [128, 2], BF16)
    banks = [nc.alloc_psum_tensor("bank%d" % i, [128, 512], F32) for i in range(8)]

    P = Prog(nc)

    def KT(s):
        return arena[:, s * 4608: s * 4608 + 2304]

    def VV(s):
        return arena[:, s * 4608 + 2304:(s + 1) * 4608].rearrange("p (t d) -> p t d", d=128)

    QA = arena[:, 9216:18432].rearrange("p (h t) -> p h t", t=T)
    UB = arena[:, 18432:27648].rearrange("p (g t) -> p g t", t=T)
    ropeC = arena[:, 18432:18432 + 2048]
    ropeS = arena[:, 18432 + 2048:18432 + 4096]
    PT = [arena[:, 18432 + 4096 + i * 512:18432 + 4096 + (i + 1) * 512] for i in range(4)]
    AT = [arena[:, 18432 + 6144 + i * 1024:18432 + 6144 + (i + 1) * 1024].bitcast(F32) for i in range(2)]
    QP = [arena[:, 18432 + 6144 + 2048 + m * 512:18432 + 6144 + 2048 + (m + 1) * 512] for m in range(2)]
    SGV = arena[:, 0:9216].rearrange("p (t f) -> p t f", f=512)
    MG = arena[:, 0:9216].rearrange("p (j t) -> p j t", t=T)
    ACTF = arena[:, 0:18432].rearrange("p (c t) -> p c t", t=T)
    OST = arena[:, 18432:18432 + 8192].bitcast(F32).rearrange("p (c t) -> p c t", t=512)

    bt_ring = Ring(range(4))
    ft_ring = Ring(range(4))
    stat_ring = Ring(range(4))
    rs_ring = Ring(range(2))
    bank_all = Ring(range(6))
    DEF = Deferred()
    bank_S = Ring([4, 5])
    bank_B = Ring([6, 7])

    def bk(i):
        tr = P.t("bank", i)
        tr.excl = True
        return tr

    dbg_n = [0]

    def dump(name, ap, shape, dtype, trks):
        if not debug:
            return
        d = nc.dram_tensor(name, list(shape), dtype, kind="ExternalOutput").ap()
        dbg_n[0] += 1
        P.dma("sp", I("dma_start", out=d, in_=ap), tuple(trks), (P.t("dbg", name),), "dbg%d" % dbg_n[0])
        P.wait_all("sp", [P.t("dbg", name)])

    TILES = []
    NFFN = sum(ng + 4 for (_, ng) in FFN_GROUPS)
    NPRE = len(specs) - NFFN
    for jt in range(24):
        TILES.append((wada_d[0, jt], 2048))
    for n in range(nb):
        for l in range(depth):
            side = (n == 0 and l + 1 < depth)
            for ti in range(len(specs)):
                TILES.append((wst_d[l, :, int(tile_offs[ti]):int(tile_offs[ti + 1])], tile_sizes[ti]))
                fi_ = ti - NPRE
                if side and 0 <= fi_ < 24:
                    TILES.append((wada_d[l + 1, fi_], 2048))
    wstate = {"issued": 0, "cur": 0}

    def w_prefetch(upto):
        while wstate["issued"] < min(upto, len(TILES)):
            i = wstate["issued"]
            slot = i % NWS
            src, size = TILES[i]
            P.dma("pool", I("dma_start", out=wring[:, slot, 0:size], in_=src),
                  reads=(), writes=(P.t("W", slot),), sem="w%d" % slot)
            wstate["issued"] += 1

    def w_next(expect_size=None, hold=0):
        i = wstate["cur"]
        w_prefetch(i - hold + NWS)
        wstate["cur"] += 1
        slot = i % NWS
        if expect_size is not None:
            assert TILES[i][1] == expect_size, (i, TILES[i][1], expect_size)
        return wring[:, slot, :], P.t("W", slot)

    P.dma("sp", I("dma_start", out=smalls[:], in_=smalls_d), (), (P.t("smalls"),), "misc0")
    lam_raw = ftr[:, 0:2, :].rearrange("p a b -> p (a b)")
    P.dma("sp", I("dma_start", out=lam_raw, in_=lams_d.partition_broadcast(128)), (),
          (P.t("ft", 0), P.t("ft", 1)), "misc2")
    P.dma("pool", I("dma_start", out=perm[:], in_=perm_d), (), (P.t("perm"),), "misc3")
    P.dma("sp", I("dma_start", out=identf[:], in_=ident_d), (), (P.t("identf"),), "misc4")
    P.op("dve", I("memset", onesf[:], 1.0), (), (P.t("onesf"),))
    P.op("dve", I("memset", epsc[:], EPS), (), (P.t("epsc"),))
    P.op("dve", I("memset", ones[:], 1.0), (), (P.t("ones"),))
    P.op("dve", I("memset", nhalf[:], -0.5), (), (P.t("nhalf"),))
    w_prefetch(NWS)

    lr = lam_raw.rearrange("p (a l d) -> p a l d", a=4, l=4)
    P.op("dve", I("tensor_tensor", out=lr[:, 0], in0=lr[:, 0], in1=lr[:, 1], op=ALU.mult),
         (P.t("ft", 0), P.t("ft", 1)), (P.t("ft", 0),))
    P.op("dve", I("tensor_tensor", out=lr[:, 2], in0=lr[:, 2], in1=lr[:, 3], op=ALU.mult),
         (P.t("ft", 0), P.t("ft", 1)), (P.t("ft", 1),))
    P.op("dve", I("reduce_sum", out=lamt[:, 0:4], in_=lr[:, 0], axis=AX.X), (P.t("ft", 0),), (P.t("lamt"),))
    P.op("dve", I("reduce_sum", out=lamt[:, 4:8], in_=lr[:, 2], axis=AX.X), (P.t("ft", 1),), (P.t("lamt"),))
    P.op("act", I("activation", out=lamt[:, 0:8], in_=lamt[:, 0:8], func=AF.Exp), (P.t("lamt"),), (P.t("lamt"),))
    P.op("dve", I("tensor_tensor", out=lamt[:, 8:12], in0=lamt[:, 4:8], in1=lamt[:, 0:4], op=ALU.subtract),
         (P.t("lamt"),), (P.t("lamt"),))
    for l in range(depth):
        P.op("dve", I("tensor_scalar", out=lamt[:, 8 + l:9 + l], in0=lamt[:, 8 + l:9 + l],
                                                     scalar1=-lam_init(l), scalar2=None, op0=ALU.add),
             (P.t("lamt"),), (P.t("lamt"),))
        P.op("dve", I("tensor_scalar", out=lamt[:, 12 + l:13 + l], in0=smalls[:, SM_SUB + l:SM_SUB + l + 1],
                                                     scalar1=1.0 - lam_init(l), scalar2=None, op0=ALU.mult),
             (P.t("smalls"), P.t("lamt")), (P.t("lamt"),))

    cT = smalls[:, SM_C:SM_C + 24]
    sct = ftr[:, 2, 0:24]
    P.op("act", I("activation", out=sct, in_=cT, func=AF.Tanh, scale=0.5), (P.t("smalls"),), (P.t("ft", 2),))
    P.op("dve", I("tensor_scalar", out=sct, in0=sct, scalar1=0.5, scalar2=0.5, op0=ALU.mult, op1=ALU.add),
         (P.t("ft", 2),), (P.t("ft", 2),))
    for n in range(3):
        P.op("dve", I("tensor_tensor", out=scT[:, :, n], in0=sct[:, n * 8:(n + 1) * 8],
                                                     in1=cT[:, n * 8:(n + 1) * 8], op=ALU.mult),
             (P.t("ft", 2), P.t("smalls")), (P.t("scT"),))

    def ada_items(l, b):
        def tile_item(jt):
            def f():
                wv, wt = w_next(2048)
                wv3 = wv.rearrange("p (k c) -> p k c", c=256)
                for jj in range(2):
                    j = jt * 2 + jj
                    P.op("pe", mmgroup(banks[b][:, j * 3:(j + 1) * 3],
                                       [(wv3[:, kc, jj * 128:(jj + 1) * 128], scT[:, kc, :]) for kc in range(8)]),
                         (wt, P.t("scT")), (bk(b),))
                if jt == 23:
                    post()
            return f

        def post():
            psv = banks[b][:, 0:144].rearrange("p (j n) -> p j n", n=3)
            for n in range(3):
                P.op("dve", I("tensor_tensor", out=mod[:, l, :, n], in0=psv[:, :, n],
                              in1=smalls[:, SM_AB + l * 48:SM_AB + (l + 1) * 48], op=ALU.add),
                     (bk(b), P.t("smalls")), (P.t("mod", l),))
            for n in range(3):
                for (seg, gsrc) in ((1, SM_N1), (4, SM_N2)):
                    P.op("dve", I("scalar_tensor_tensor", out=mod[:, l, seg * 8:(seg + 1) * 8, n],
                                  in0=mod[:, l, seg * 8:(seg + 1) * 8, n], scalar=1.0,
                                  in1=smalls[:, gsrc + l * 8:gsrc + (l + 1) * 8], op0=ALU.add, op1=ALU.mult),
                         (P.t("mod", l), P.t("smalls")), (P.t("mod", l),))
                for seg in (2, 5):
                    P.op("dve", I("tensor_scalar", out=mod[:, l, seg * 8:(seg + 1) * 8, n],
                                  in0=mod[:, l, seg * 8:(seg + 1) * 8, n], scalar1=0.5, scalar2=None, op0=ALU.mult),
                         (P.t("mod", l),), (P.t("mod", l),))
        return [tile_item(jt) for jt in range(24)]

    for f in ada_items(0, bank_all.next()):
        f()

    def mcol(l, seg, fc, v):
        return mod[:, l, seg * 8 + fc, v:v + 1]

    def rstd_stages(src_fn, src_trks, sz, nchunks, inv_n, d0=1, extra=None, eps_add=EPS, co=0):
        nt = sz // 128
        b1, b2 = 6, 7
        groups = [list(range(c, min(c + 4, nchunks))) for c in range(0, nchunks, 4)]
        slots = {}

        def sq(grp):
            for c in grp:
                bi = bt_ring.next()
                slots[c] = bi
                P.op("act", I("activation", out=btr[:, bi, 0:sz], in_=src_fn(c), func=AF.Square),
                     (src_trks[c],), (P.t("bt", bi),))

        def mm(grp):
            for c in grp:
                bi = slots[c]

                def fn(e, c=c, bi=bi):
                    ins = None
                    for tt in range(nt):
                        ins = e.matmul(banks[b1][:, co + tt:co + tt + 1], lhsT=btr[:, bi, tt * 128:(tt + 1) * 128], rhs=ones[:, 0:1],
                                       start=(c == 0 and tt == 0),
                                       stop=(c == nchunks - 1 and tt == nt - 1 and extra is None),
                                       skip_group_check=True)
                    return ins
                P.op("pe", fn, (P.t("bt", bi), P.t("ones")), (bk(b1),))
            if extra is not None and grp[-1] == nchunks - 1:
                ei, erhs = extra

                def fne(e):
                    ins = None
                    for tt in range(nt):
                        ins = e.matmul(banks[b1][:, co + tt:co + tt + 1], lhsT=btr[:, ei, tt * 128:(tt + 1) * 128], rhs=erhs,
                                       start=False, stop=(tt == nt - 1), skip_group_check=True)
                    return ins
                P.op("pe", fne, (P.t("bt", ei), P.t("epsc")), (bk(b1),))

        def fin():
            P.op("dve", I("tensor_scalar", out=rtiny[:, co:co + nt], in0=banks[b1][:, co:co + nt], scalar1=inv_n,
                          scalar2=eps_add, op0=ALU.mult, op1=ALU.add), (bk(b1),), (P.t("rtiny", co),))
            P.op("pool", I("tensor_tensor", out=rtiny[:, co:co + nt], in0=rtiny[:, co:co + nt], in1=nhalf[:, 0:nt], op=ALU.pow),
                 (P.t("rtiny", co), P.t("nhalf")), (P.t("rtiny", co),))

        def dg():
            for tt in range(nt):
                P.op("dve", I("tensor_scalar", out=diag[:, tt, :], in0=identf[:], scalar1=rtiny[:, co + tt:co + tt + 1],
                              scalar2=None, op0=ALU.mult), (P.t("rtiny", co), P.t("identf")), (P.t("diag", tt),))

        def bc():
            def fn2(e):
                ins = None
                for tt in range(nt):
                    ins = e.matmul(banks[b2][:, tt * 128:(tt + 1) * 128], lhsT=onesf[:], rhs=diag[:, tt, :],
                                   start=True, stop=True)
                return ins
            P.op("pe", fn2, tuple(P.t("diag", tt) for tt in range(nt)) + (P.t("onesf"),), (bk(b2),))

        stages = [(d0, lambda: sq(groups[0]), "A")]
        for gi in range(len(groups)):
            last_g = gi == len(groups) - 1

            def st(gi=gi, last_g=last_g):
                mm(groups[gi])
                if not last_g:
                    sq(groups[gi + 1])
                else:
                    fin()
            stages.append((2, st, "A"))
        stages.append((2, dg))
        stages.append((1, bc))
        return stages

    def norm_chain(l, n, seg_sh, seg_a, bi_):
        t0, sz = BLK[bi_]
        v = 2 if bi_ == 0 else n
        stages = rstd_stages(lambda c: xT[:, c, t0:t0 + sz], [P.t("x", c, bi_) for c in range(8)], sz, 8, 1.0 / D, d0=0,
                             co=4 * bi_)

        def apply():
            for fc in range(8):
                f2 = ft_ring.next()
                P.op("dve", I("tensor_tensor", out=ftr[:, f2, 0:sz], in0=xT[:, fc, t0:t0 + sz], in1=banks[7][:, 0:sz],
                              op=ALU.mult), (P.t("x", fc, bi_), bk(7)), (P.t("ft", f2),))
                P.op("act", I("activation", out=hT[:, fc, t0:t0 + sz], in_=ftr[:, f2, 0:sz], func=AF.Identity,
                              bias=mcol(l, seg_sh, fc, v), scale=mcol(l, seg_a, fc, v)),
                     (P.t("ft", f2), P.t("mod", l)), (P.t("h", fc, bi_),))
        stages.append((1, apply))
        return stages

    def final_chain(n, bi_):
        t0, sz = BLK[bi_]
        stages = rstd_stages(lambda c: xT[:, c, t0:t0 + sz], [P.t("x", c, bi_) for c in range(8)], sz, 8, 1.0 / D, d0=0,
                             co=4 * bi_)

        def apply():
            otr = P.rt("R2", "OST")
            for fc in range(8):
                P.op("dve", I("scalar_tensor_tensor", out=OST[:, fc, :], in0=xT[:, fc, t0:t0 + sz],
                              scalar=smalls[:, SM_FG + fc:SM_FG + fc + 1], in1=banks[7][:, 0:sz], op0=ALU.mult, op1=ALU.mult),
                     (P.t("x", fc, bi_), P.t("smalls"), bk(7)), (otr,))
            P.dma("sp", I("dma_start", out=yout[n, :, :, t0 - CTX:t0 - CTX + sz], in_=OST), (otr,), (P.t("yout"),), "out")
        stages.append((1, apply))
        return stages

    def proj_fm(b, wcols, wt, src_fn, src_trks, sz, nk):
        P.op("pe", mmgroup(banks[b][:, 0:sz], [(wcols(kc), src_fn(kc)) for kc in range(nk)]),
             (wt,) + tuple(src_trks), (bk(b),))
        DEF.tick()

    def h_src(bi_):
        DEF.ensure(("h", bi_))
        t0, sz = BLK[bi_]
        return (lambda kc: hT[:, kc, t0:t0 + sz]), [P.t("h", kc, bi_) for kc in range(8)]

    def gelu_chain(b, sz, out_ap, out_trks):
        X = banks[b][:, 0:sz]
        f1 = ft_ring.next()
        T1 = ftr[:, f1, 0:sz]
        P.op("act", I("activation", out=T1, in_=X, func=AF.Identity, scale=0.5), (bk(b),), (P.t("ft", f1),))
        P.op("act", I("activation", out=X, in_=X, func=AF.Square), (bk(b),), (bk(b),))
        P.op("dve", I("tensor_scalar", out=X, in0=X, scalar1=0.044715, scalar2=1.0, op0=ALU.mult, op1=ALU.add),
             (bk(b),), (bk(b),))
        P.op("dve", I("tensor_tensor", out=X, in0=X, in1=T1, op=ALU.mult), (bk(b), P.t("ft", f1)), (bk(b),))
        P.op("act", I("activation", out=X, in_=X, func=AF.Tanh, scale=2.0 * GELU_C), (bk(b),), (bk(b),))
        P.op("dve", I("scalar_tensor_tensor", out=out_ap, in0=X, scalar=1.0, in1=T1, op0=ALU.add, op1=ALU.mult),
             (bk(b), P.t("ft", f1)), tuple(out_trks))

    def proj_kv_items(l, hd, slot, ring):
        items = []
        holder = {}

        def get_w():
            if "w" not in holder:
                wv, wt = w_next(2048)
                holder["w"] = (wv.rearrange("p (s k c) -> p s k c", s=2, k=8), wt)
            return holder["w"]

        def k_item(bi_):
            def f():
                wv3, wt = get_w()
                t0, sz = BLK[bi_]
                src_fn, src_trks = h_src(bi_)
                b = ring.next()
                proj_fm(b, lambda kc: wv3[:, 0, kc, :], wt, src_fn, src_trks, sz, 8)
                dst = KT(slot)[:, t0:t0 + sz]
                dtrk = P.rt("R0", "KT", slot, bi_)
                if bi_ == 0:
                    P.op("act", I("activation", out=dst, in_=banks[b][:, 0:sz], func=AF.Copy), (bk(b),), (dtrk,))
                else:
                    rope_evac(b, sz, t0 - CTX, dst, dtrk, ring)
            return f

        def v_item(tiles):
            def f():
                wv3, wt = get_w()
                for t in tiles:
                    DEF.ensure(("h", tile_blk(t)))
                b = ring.next()
                for i, t in enumerate(tiles):
                    P.op("pe", mmgroup(banks[b][:, i * 128:(i + 1) * 128],
                                       [(hT[:, kc, t * 128:(t + 1) * 128], wv3[:, 1, kc, :]) for kc in range(8)]),
                         (wt,) + tuple(P.t("h", kc, tile_blk(t)) for kc in range(8)), (bk(b),))
                nt = len(tiles)
                dst = VV(slot)[:, tiles[0]:tiles[0] + nt, :]
                src = banks[b][:, 0:nt * 128].rearrange("p (t d) -> p t d", d=128)
                P.op("dve", I("tensor_copy", out=dst, in_=src), (bk(b),),
                     (P.rt("R0", "VV", slot, tile_blk(tiles[0])),))
            return f

        for bi_ in range(5):
            items.append(k_item(bi_))
        items.append(v_item([0, 1]))
        for g in range(4):
            items.append(v_item([2 + 4 * g + i for i in range(4)]))
        return items

    def rope_evac(b, sz, lt0, dst, dtrk, ring):
        X = banks[b][:, 0:sz]
        qi = PT_ring.next()
        qtk = P.rt("R2", "PT", qi)
        P.op("act", I("activation", out=PT[qi][:, 0:sz], in_=X, func=AF.Copy), (bk(b),), (qtk,))
        b2 = ring.next()
        P.op("pe", I("matmul", banks[b2][:, 0:sz], lhsT=perm[:], rhs=PT[qi][:, 0:sz], start=True, stop=True),
             (qtk, P.t("perm")), (bk(b2),))
        rtk = P.rt("R2", "ROPE")
        P.op("dve", I("tensor_tensor", out=X, in0=X, in1=ropeC[:, lt0:lt0 + sz], op=ALU.mult), (bk(b), rtk), (bk(b),))
        f1 = ft_ring.next()
        P.op("dve", I("tensor_tensor", out=ftr[:, f1, 0:sz], in0=banks[b2][:, 0:sz], in1=ropeS[:, lt0:lt0 + sz],
                                              op=ALU.mult), (bk(b2), rtk), (P.t("ft", f1),))
        P.op("dve", I("tensor_tensor", out=dst, in0=ftr[:, f1, 0:sz], in1=X, op=ALU.add),
             (P.t("ft", f1), bk(b)), (dtrk,))

    def proj_q_items(l, pair, ring, last):
        items = []
        holder = {}

        def get_w():
            if "w" not in holder:
                wv, wt = w_next(2048)
                holder["w"] = (wv.rearrange("p (k c) -> p k c", c=256), wt)
            return holder["w"]

        def q_item(hh, bi_):
            def f():
                wv3, wt = get_w()
                hd = pair * 2 + hh
                t0, sz = BLK[bi_]
                src_fn, src_trks = h_src(bi_)
                b = ring.next()
                proj_fm(b, lambda kc: wv3[:, kc, hh * 128:(hh + 1) * 128], wt, src_fn, src_trks, sz, 8)
                dst = QA[:, hd, t0:t0 + sz]
                dtrk = P.rt("R1", "QA", hd, bi_)
                if bi_ == 0:
                    P.op("act", I("activation", out=dst, in_=banks[b][:, 0:sz], func=AF.Copy), (bk(b),), (dtrk,))
                else:
                    rope_evac(b, sz, t0 - CTX, dst, dtrk, ring)
            return f

        for hh in range(2):
            for bi_ in range(5):
                if bi_ == 0 and last:
                    continue
                items.append(q_item(hh, bi_))
        return items

    def attention(l, n, hd, slot, last, background):
        qblocks = [1, 2, 3, 4] if last else [0, 1, 2, 3, 4]
        nlam = lamt[:, 8 + l:9 + l]
        subcol = lamt[:, 12 + l:13 + l]
        O = [0, 1]
        R = [2, 3]
        bg = list(background)
        nbound = [len(qblocks)]

        def bg_boundary():
            k = -(-len(bg) // max(1, nbound[0]))
            nbound[0] -= 1
            for _ in range(min(k, len(bg))):
                bg.pop(0)()

        def prep_q(qb_):
            q0_, qs_ = BLK[qb_]
            tr = P.rt("R1", "QA", hd, qb_)
            P.op("pool", I("tensor_copy", out=QP[0][0:64, 0:qs_], in_=QA[0:64, hd, q0_:q0_ + qs_]), (tr,), (P.rt("R2", "QP", 0),))
            P.op("pool", I("tensor_copy", out=QP[1][64:128, 0:qs_], in_=QA[64:128, hd, q0_:q0_ + qs_]), (tr,), (P.rt("R2", "QP", 1),))

        prep_q(qblocks[0])
        for qi_, qb in enumerate(qblocks):
            q0, qs = BLK[qb]
            kchunks = [0, 1] if qb == 0 else list(range(NTILE))
            qtrk = P.rt("R1", "QA", hd, qb)

            def scores(kc):
                res = []
                for m in range(2):
                    b = bank_S.next()
                    P.op("pe", I("matmul", banks[b][:, 0:qs], lhsT=KT(slot)[:, kc * 128:(kc + 1) * 128],
                                 rhs=QP[m][:, 0:qs], start=True, stop=True),
                         (P.rt("R0", "KT", slot, tile_blk(kc)), P.rt("R2", "QP", m)), (bk(b),))
                    pi = PT_ring.next()
                    P.op("act", I("activation", out=PT[pi][:, 0:qs], in_=banks[b][:, 0:qs], func=AF.Exp, scale=0.125, bias=-8.0),
                         (bk(b),), (P.rt("R2", "PT", pi),))
                    res.append(pi)
                return res

            def pv(kc, pis, first, lastk):
                vtrk = P.rt("R0", "VV", slot, tile_blk(kc))
                for m in range(2):
                    pi = pis[m]
                    P.op("pe", I("matmul", banks[O[m]][:, 0:qs], lhsT=VV(slot)[:, kc, :],
                                                               rhs=PT[pi][:, 0:qs], start=first, stop=lastk),
                         (vtrk, P.rt("R2", "PT", pi)), (bk(O[m]),))
                    P.op("pe", I("matmul", banks[R[m]][:, 0:qs], lhsT=ones[:],
                                                               rhs=PT[pi][:, 0:qs], start=first, stop=lastk),
                         (P.t("ones"), P.rt("R2", "PT", pi)), (bk(R[m]),))

            prev = scores(kchunks[0])
            for i, kc in enumerate(kchunks):
                nxt = scores(kchunks[i + 1]) if i + 1 < len(kchunks) else None
                if i + 2 == len(kchunks) and qi_ + 1 < len(qblocks):
                    prep_q(qblocks[qi_ + 1])
                pv(kc, prev, i == 0, i == len(kchunks) - 1)
                prev = nxt
                DEF.tick()

            def tail_T1(qs=qs):
                a1, a2 = AT[0][:, 0:qs], AT[1][:, 0:qs]
                t1, t2 = P.rt("R2", "AT", 0), P.rt("R2", "AT", 1)
                di, dj = bt_ring.next(), bt_ring.next()
                dt, dtj = P.t("bt", di), P.t("bt", dj)
                dd, dj_ = btr[:, di, 0:qs], btr[:, dj, 0:qs]
                P.op("act", I("activation", out=a1, in_=banks[R[1]][:, 0:qs], func=AF.Copy), (bk(R[1]),), (t1,))
                P.op("act", I("activation", out=a2, in_=banks[R[0]][:, 0:qs], func=AF.Copy), (bk(R[0]),), (t2,))
                P.op("act", I("activation", out=dd, in_=banks[R[1]][:, 0:qs], func=AF.Copy), (bk(R[1]),), (dt,))
                P.op("act", I("activation", out=dj_, in_=banks[R[0]][:, 0:qs], func=AF.Copy), (bk(R[0]),), (dtj,))
                P.op("dve", I("tensor_tensor", out=a1, in0=banks[O[0]][:, 0:qs], in1=a1, op=ALU.mult), (bk(O[0]), t1), (t1,))
                P.op("dve", I("scalar_tensor_tensor", out=a2, in0=banks[O[1]][:, 0:qs], scalar=nlam, in1=a2,
                              op0=ALU.mult, op1=ALU.mult), (bk(O[1]), t2, P.t("lamt")), (t2,))
                P.op("pool", I("tensor_tensor", out=a1, in0=a1, in1=a2, op=ALU.add), (t1, t2), (t1,))
                P.op("pool", I("tensor_tensor", out=dd, in0=dd, in1=dj_, op=ALU.mult), (dt, dtj), (dt,))
                P.op("pool", I("tensor_tensor", out=dd, in0=dd, in1=dd, op=ALU.mult), (dt,), (dt,))
                return di

            def tail_fin(qs=qs, q0=q0, qtrk=qtrk):
                a1 = AT[0][:, 0:qs]
                P.op("dve", I("scalar_tensor_tensor", out=QA[:, hd, q0:q0 + qs], in0=a1, scalar=subcol, in1=banks[7][:, 0:qs],
                              op0=ALU.mult, op1=ALU.mult), (P.rt("R2", "AT", 0), bk(7), P.t("lamt")), (qtrk,))

            DEF.flush()
            di_ = tail_T1()
            a1_ = AT[0][:, 0:qs]
            DEF.add(("tail", hd, qb), rstd_stages(lambda c, a1_=a1_: a1_, [P.rt("R2", "AT", 0)], qs, 1, 1.0 / 128, d0=2,
                                                  extra=(di_, epsc[:, 0:1]), eps_add=0.0) + [(1, tail_fin)])
            bg_boundary()
        while bg:
            bg.pop(0)()

    PT_ring = Ring(range(4))

    def phase_A(l, n, last):
        for r in ("R0", "R1", "R2"):
            P.release(r)
        P.op("pool", I("memset", QP[0][64:128, :], 0.0), (), (P.rt("R2", "QP", 0),))
        P.op("pool", I("memset", QP[1][0:64, :], 0.0), (), (P.rt("R2", "QP", 1),))
        rtk = P.rt("R2", "ROPE")
        P.dma("pool", I("dma_start", out=ropeC, in_=rope_d[0]), (), (rtk,), "rope")
        P.dma("pool", I("dma_start", out=ropeS, in_=rope_d[1]), (), (rtk,), "rope")
        for f in proj_kv_items(l, 0, 0, bank_all):
            f()
        for f in proj_q_items(l, 0, bank_all, last):
            f()
        attention(l, n, 0, 0, last, proj_kv_items(l, 1, 1, bank_B))
        attention(l, n, 1, 1, last, proj_q_items(l, 1, bank_B, last) + proj_kv_items(l, 2, 0, bank_B))
        attention(l, n, 2, 0, last, proj_kv_items(l, 3, 1, bank_B))
        attention(l, n, 3, 1, last, [])

    def phase_B(l, n, blocks):
        P.release("R0")
        P.release("R2")
        P.dma("pool", I("dma_start", out=wsT[:], in_=sgw_d[l]), (), (P.t("wsT"),), "sgp")
        P.dma("sp", I("dma_start", out=sgbf[:, 0:512], in_=sgb_d[l:l + 1, :]), (), (P.t("sgbf"),), "sgp2")
        P.dma("sp", I("dma_start", out=sggt[:], in_=sgg_d[l].partition_broadcast(128)), (), (P.t("sggt"),), "sgp3")
        P.op("dve", I("tensor_copy", out=sgbr[:, 0:512], in_=sgbf[:, 0:512]), (P.t("sgbf"),), (P.t("sgbr"),))
        P.op("dve", I("tensor_tensor", out=sgbf[:, 512:1024], in0=sgbf[:, 0:512], in1=sgbr[:, 0:512],
                                              op=ALU.subtract), (P.t("sgbf"), P.t("sgbr")), (P.t("sgbf"),))
        P.op("dve", I("tensor_copy", out=sgbr[:, 512:1024], in_=sgbf[:, 512:1024]), (P.t("sgbf"),), (P.t("sgbr"),))
        tiles = [t for t in range(NTILE) if tile_blk(t) in blocks]
        SQK = math.sqrt(0.044715)

        def run_skew(items, sched):
            n_ = len(items)
            maxlag = max(lg for _, lg in sched)
            for step in range(n_ + maxlag):
                for fn_, lg in sched:
                    i_ = step - lg
                    if 0 <= i_ < n_:
                        fn_(items[i_])

        def g_s1(it):
            f1 = ft_ring.next()
            it["f1"] = f1
            X, T1 = banks[it["b"]][:, 0:it["sz"]], ftr[:, f1, 0:it["sz"]]
            P.op("act", I("activation", out=T1, in_=X, func=AF.Identity, scale=0.5), (bk(it["b"]),), (P.t("ft", f1),))
            P.op("act", I("activation", out=X, in_=X, func=AF.Square, scale=SQK), (bk(it["b"]),), (bk(it["b"]),))

        def g_s2a(it):
            X, T1 = banks[it["b"]][:, 0:it["sz"]], ftr[:, it["f1"], 0:it["sz"]]
            P.op("dve", I("scalar_tensor_tensor", out=X, in0=X, scalar=1.0, in1=T1, op0=ALU.add, op1=ALU.mult),
                 (bk(it["b"]), P.t("ft", it["f1"])), (bk(it["b"]),))

        def g_s2b(it):
            X = banks[it["b"]][:, 0:it["sz"]]
            P.op("act", I("activation", out=X, in_=X, func=AF.Tanh, scale=2.0 * GELU_C), (bk(it["b"]),), (bk(it["b"]),))

        uw = {}

        def u_s0(it):
            if it["half"] not in uw:
                wv, wt = w_next(2048)
                uw[it["half"]] = (wv.rearrange("p (k c) -> p k c", c=256), wt)
            wv3, wt = uw[it["half"]]
            t0, sz = BLK[it["bi"]]
            src_fn, src_trks = h_src(it["bi"])
            b = bank_all.next()
            it["b"], it["sz"] = b, sz
            gg = it["gg"]
            proj_fm(b, lambda kc: wv3[:, kc, gg * 128:(gg + 1) * 128], wt, src_fn, src_trks, sz, 8)

        def u_s3(it):
            t0, sz = BLK[it["bi"]]
            X, T1 = banks[it["b"]][:, 0:sz], ftr[:, it["f1"], 0:sz]
            P.op("dve", I("scalar_tensor_tensor", out=UB[:, it["g"], t0:t0 + sz], in0=X, scalar=1.0, in1=T1,
                          op0=ALU.add, op1=ALU.mult), (bk(it["b"]), P.t("ft", it["f1"])),
                 tuple(P.rt("R2", "UB", it["g"], t) for t in range(t0 // 128, (t0 + sz) // 128)))

        uitems = [dict(half=half, gg=gg, g=half * 2 + gg, bi=bi_) for half in range(2) for gg in range(2) for bi_ in blocks]
        run_skew(uitems, [(u_s3, 3), (g_s2a, 2), (g_s1, 1), (g_s2b, 2), (u_s0, 0)])

        sw = {}

        def v_s0(it):
            if "w" not in sw:
                wv0, wt0 = w_next(2048)
                wv1, wt1 = w_next(2048, hold=1)
                sw["w"] = ([wv0.rearrange("p (k c) -> p k c", c=256), wv1.rearrange("p (k c) -> p k c", c=256)], [wt0, wt1])
            wvs, wts = sw["w"]
            t = it["t"]
            DEF.ensure(("h", tile_blk(t)))
            b = bank_all.next()
            it["b"], it["sz"] = b, 512
            for half in range(2):
                P.op("pe", mmgroup(banks[b][:, half * 256:(half + 1) * 256],
                                   [(hT[:, kc, t * 128:(t + 1) * 128], wvs[half][:, kc, :]) for kc in range(8)]),
                     (wts[half],) + tuple(P.t("h", kc, tile_blk(t)) for kc in range(8)), (bk(b),))

        def v_s3(it):
            X, G = banks[it["b"]][:, :], ftr[:, it["f1"], :]
            si = stat_ring.next()
            it["si"] = si
            st = P.t("stat", si)
            gt = P.t("ft", it["f1"])
            P.op("dve", I("scalar_tensor_tensor", out=G, in0=X, scalar=1.0, in1=G, op0=ALU.add, op1=ALU.mult),
                 (bk(it["b"]), gt), (gt,))
            P.op("dve", I("bn_stats", out=stat[:, si, 0:6], in_=G), (gt,), (st,))
            P.op("dve", I("bn_aggr", out=stat[:, si, 8:10], in_=stat[:, si, 0:6]), (st,), (st,))
            P.op("dve", I("tensor_scalar", out=stat[:, si, 10:11], in0=stat[:, si, 9:10], scalar1=EPS, scalar2=None,
                          op0=ALU.add), (st,), (st,))
            P.op("pool", I("tensor_tensor", out=stat[:, si, 10:11], in0=stat[:, si, 10:11], in1=nhalf[:, 0:1], op=ALU.pow),
                 (st, P.t("nhalf")), (st,))

        def v_s4a(it):
            si = it["si"]
            st = P.t("stat", si)
            P.op("dve", I("scalar_tensor_tensor", out=stat[:, si, 11:12], in0=stat[:, si, 8:9], scalar=-1.0,
                          in1=stat[:, si, 10:11], op0=ALU.mult, op1=ALU.mult), (st,), (st,))

        def v_s4b(it):
            si = it["si"]
            G = ftr[:, it["f1"], :]
            gt = P.t("ft", it["f1"])
            P.op("act", I("activation", out=G, in_=G, func=AF.Identity, bias=stat[:, si, 11:12], scale=stat[:, si, 10:11]),
                 (gt, P.t("stat", si)), (gt,))

        def v_s4c(it):
            G = ftr[:, it["f1"], :]
            P.op("pool", I("tensor_tensor", out=SGV[:, it["t"], :], in0=G, in1=sggt[:], op=ALU.mult),
                 (P.t("ft", it["f1"]), P.t("sggt")), (P.rt("R0", "SGV", it["t"]),))

        vitems = [dict(t=t) for t in tiles]
        run_skew(vitems, [(v_s4a, 4), (v_s4b, 4), (v_s3, 3), (v_s4c, 4), (g_s2a, 2), (g_s1, 1), (g_s2b, 2), (v_s0, 0)])
        for t in tiles:
            b = bank_all.next()
            for g in range(4):
                def fn(e, g=g, t=t, b=b):
                    o = banks[b][:, g * 128:(g + 1) * 128]
                    e.matmul(o, lhsT=SGV[:, t, g * 128:(g + 1) * 128], rhs=wsT[:, g * 128:(g + 1) * 128],
                             start=True, stop=False)
                    e.matmul(o, lhsT=ones[0:1, :], rhs=sgbr[0:1, g * 128:(g + 1) * 128], start=False, stop=False)
                    return e.matmul(o, lhsT=ones[0:1, :], rhs=sgbr[0:1, 512 + g * 128:512 + (g + 1) * 128],
                                    start=False, stop=True)
                P.op("pe", fn, (P.rt("R0", "SGV", t), P.t("wsT"), P.t("ones"), P.t("sgbr")), (bk(b),))
            utr = [P.rt("R2", "UB", g, t) for g in range(4)]
            P.op("dve", I("tensor_tensor",
                out=UB[:, :, t * 128:(t + 1) * 128], in0=UB[:, :, t * 128:(t + 1) * 128],
                in1=banks[b][:, :].rearrange("p (g q) -> p g q", q=128), op=ALU.mult), tuple(utr) + (bk(b),), tuple(utr))

    def phase_C(l, n, blocks, on_final):
        P.release("R0")
        for jg in range(2):
            for jp in range(2):
                j0 = jg * 4 + jp * 2
                wva, wta = w_next(2048)
                was = wva.rearrange("p (j s k c) -> p j s k c", j=2, s=2, k=4)
                for jj in range(2):
                    j = j0 + jj
                    wvg, wtg = w_next(2048, hold=1 + jj)
                    wg = wvg.rearrange("p (s k c) -> p s k c", s=2, k=8)
                    mj = jp * 2 + jj
                    for bi_ in blocks:
                        t0, sz = BLK[bi_]
                        src_fn, src_trks = h_src(bi_)
                        tl = list(range(t0 // 128, (t0 + sz) // 128))
                        bga, ba, bgb, bs = bank_all.next(), bank_all.next(), bank_all.next(), bank_all.next()
                        proj_fm(bga, lambda kc: wg[:, 0, kc, :], wtg, src_fn, src_trks, sz, 8)
                        proj_fm(ba, lambda kc: was[:, jj, 0, kc, :], wta, lambda kc: QA[:, kc, t0:t0 + sz],
                                [P.rt("R1", "QA", kc, bi_) for kc in range(4)], sz, 4)
                        proj_fm(bgb, lambda kc: wg[:, 1, kc, :], wtg, src_fn, src_trks, sz, 8)
                        proj_fm(bs, lambda kc: was[:, jj, 1, kc, :], wta, lambda kc: UB[:, kc, t0:t0 + sz],
                                [P.rt("R2", "UB", kc, t) for kc in range(4) for t in tl], sz, 4)
                        f1, f2 = ft_ring.next(), ft_ring.next()
                        T1, T2 = ftr[:, f1, 0:sz], ftr[:, f2, 0:sz]
                        P.op("act", I("activation", out=T1, in_=banks[bga][:, 0:sz], func=AF.Tanh,
                                                                           scale=0.5), (bk(bga),), (P.t("ft", f1),))
                        P.op("act", I("activation", out=T2, in_=banks[bgb][:, 0:sz], func=AF.Tanh,
                                                                           scale=0.5), (bk(bgb),), (P.t("ft", f2),))
                        P.op("dve", I("scalar_tensor_tensor",
                            out=T1, in0=T1, scalar=1.0, in1=banks[ba][:, 0:sz], op0=ALU.add, op1=ALU.mult),
                             (P.t("ft", f1), bk(ba)), (P.t("ft", f1),))
                        P.op("dve", I("scalar_tensor_tensor",
                            out=T2, in0=T2, scalar=1.0, in1=banks[bs][:, 0:sz], op0=ALU.add, op1=ALU.mult),
                             (P.t("ft", f2), bk(bs)), (P.t("ft", f2),))
                        P.op("pool", I("tensor_tensor",
                            out=MG[:, mj, t0:t0 + sz], in0=T1, in1=T2, op=ALU.add),
                             (P.t("ft", f1), P.t("ft", f2)), (P.rt("R0", "MG", mj, bi_),))
            def wo_update(wo, wto, half, bi_):
                t0, sz = BLK[bi_]
                v = 2 if bi_ == 0 else n
                for ff in range(4):
                    fo = half * 4 + ff
                    b = bank_all.next()
                    proj_fm(b, lambda kc: wo[:, kc, ff * 128:(ff + 1) * 128], wto, lambda kc: MG[:, kc, t0:t0 + sz],
                            [P.rt("R0", "MG", kc, bi_) for kc in range(4)], sz, 4)
                    P.op("dve", I("scalar_tensor_tensor", out=xT[:, fo, t0:t0 + sz], in0=banks[b][:, 0:sz],
                                  scalar=mcol(l, 2, fo, v), in1=xT[:, fo, t0:t0 + sz], op0=ALU.mult, op1=ALU.add),
                         (bk(b), P.t("mod", l), P.t("x", fo, bi_)), (P.t("x", fo, bi_),))

            if jg == 0:
                for half in range(2):
                    wvo, wto = w_next(2048)
                    wo = wvo.rearrange("p (k c) -> p k c", c=512)
                    for bi_ in blocks:
                        wo_update(wo, wto, half, bi_)
            else:
                wts_ = []
                for half in range(2):
                    wvo, wto = w_next(2048, hold=half)
                    wts_.append((wvo.rearrange("p (k c) -> p k c", c=512), wto))
                for bi_ in blocks:
                    for half in range(2):
                        wo_update(wts_[half][0], wts_[half][1], half, bi_)
                    on_final(bi_)
            if jg == 0:
                P.release("R0")

    def phase_F(l, n, blocks, on_final, side=()):
        P.release("R0")
        P.release("R1")
        P.release("R2")
        side = list(side)

        def side_step():
            if side:
                side.pop(0)()
        for gi, (c0, ng) in enumerate(FFN_GROUPS):
            if gi > 0:
                P.release("R0")
                P.release("R1")
            for cc in range(ng):
                wv, wt = w_next(2048)
                wf = wv.rearrange("p (s k c) -> p s k c", s=2, k=8)
                reg = "R0" if cc < 4 else "R1"
                for bi_ in blocks:
                    t0, sz = BLK[bi_]
                    src_fn, src_trks = h_src(bi_)
                    ba, bb = bank_all.next(), bank_all.next()
                    proj_fm(ba, lambda kc: wf[:, 0, kc, :], wt, src_fn, src_trks, sz, 8)
                    proj_fm(bb, lambda kc: wf[:, 1, kc, :], wt, src_fn, src_trks, sz, 8)
                    f1 = ft_ring.next()
                    T1 = ftr[:, f1, 0:sz]
                    P.op("act", I("activation", out=T1, in_=banks[ba][:, 0:sz], func=AF.Tanh, scale=0.5),
                         (bk(ba),), (P.t("ft", f1),))
                    P.op("dve", I("scalar_tensor_tensor",
                        out=T1, in0=T1, scalar=1.0, in1=banks[ba][:, 0:sz], op0=ALU.add, op1=ALU.mult),
                         (P.t("ft", f1), bk(ba)), (P.t("ft", f1),))
                    P.op("dve", I("tensor_tensor",
                        out=ACTF[:, cc, t0:t0 + sz], in0=T1, in1=banks[bb][:, 0:sz], op=ALU.mult),
                         (P.t("ft", f1), bk(bb)), (P.rt(reg, "ACTF", cc, bi_),))
                side_step()
            def fo_update(wo, wt, qd, bi_):
                t0, sz = BLK[bi_]
                v = 2 if bi_ == 0 else n
                for ff in range(2):
                    fo = qd * 2 + ff
                    b = bank_all.next()
                    proj_fm(b, lambda kc: wo[:, kc, ff * 128:(ff + 1) * 128], wt, lambda kc: ACTF[:, kc, t0:t0 + sz],
                            [P.rt("R0" if kc < 4 else "R1", "ACTF", kc, bi_) for kc in range(ng)], sz, ng)
                    P.op("dve", I("scalar_tensor_tensor", out=xT[:, fo, t0:t0 + sz], in0=banks[b][:, 0:sz],
                                  scalar=mcol(l, 5, fo, v), in1=xT[:, fo, t0:t0 + sz], op0=ALU.mult, op1=ALU.add),
                         (bk(b), P.t("mod", l), P.t("x", fo, bi_)), (P.t("x", fo, bi_),))

            if gi < len(FFN_GROUPS) - 1:
                for qd in range(4):
                    wv, wt = w_next(ng * 256)
                    wo = wv[:, 0:ng * 256].rearrange("p (k c) -> p k c", c=256)
                    for bi_ in blocks:
                        fo_update(wo, wt, qd, bi_)
                    side_step()
            else:
                wl = []
                for qd in range(4):
                    wv, wt = w_next(ng * 256, hold=qd)
                    wl.append((wv[:, 0:ng * 256].rearrange("p (k c) -> p k c", c=256), wt))
                for bi_ in blocks:
                    for qd in range(4):
                        fo_update(wl[qd][0], wl[qd][1], qd, bi_)
                    on_final(bi_)
                for qd in range(4):
                    side_step()

    for n in range(nb):
        for fc in range(8):
            P.dma("sp", I("dma_start", out=xT[:, fc, :], in_=xin[n, :, fc, :]), (),
                  tuple(P.t("x", fc, bi_) for bi_ in range(5)), "xl%d" % fc)
        for bi_ in range(5):
            DEF.add(("h", bi_), norm_chain(0, n, 0, 1, bi_))
        for l in range(depth):
            last = (l == depth - 1)
            blocks = [1, 2, 3, 4] if last else [0, 1, 2, 3, 4]
            dbg = debug and n == 0 and l == depth - 1
            phase_A(l, n, last)
            DEF.flush()
            if dbg:
                dump("d_h1", hT[:], [128, 8, T], BF16, [P.t("h", kc, b_) for kc in range(8) for b_ in range(5)])
                dump("d_qa", arena[:, 9216:18432], [128, 9216], BF16, [P.rt("R1", "QA", h_, b_) for h_ in range(4) for b_ in blocks])
                dump("d_kv", arena[:, 0:9216], [128, 9216], BF16, [P.rt("R0", "KT", s_, b_) for s_ in range(2) for b_ in range(5)] + [P.rt("R0", "VV", s_, b_) for s_ in range(2) for b_ in range(5)])
            phase_B(l, n, blocks)
            if dbg:
                dump("d_ub", arena[:, 18432:27648], [128, 9216], BF16, [P.rt("R2", "UB", g_, t_) for g_ in range(4) for t_ in range(NTILE) if tile_blk(t_) in blocks])
                dump("d_sgv", arena[:, 0:9216], [128, 9216], BF16, [P.rt("R0", "SGV", t_) for t_ in range(NTILE) if tile_blk(t_) in blocks])
            phase_C(l, n, blocks, lambda b_, l=l, n=n: DEF.add(("h", b_), norm_chain(l, n, 3, 4, b_)))
            if dbg:
                DEF.flush()
                dump("d_x1", xT[:], [128, 8, T], F32, [P.t("x", fc, b_) for fc in range(8) for b_ in range(5)])
                dump("d_h2", hT[:], [128, 8, T], BF16, [P.t("h", kc, b_) for kc in range(8) for b_ in range(5)])
            if last:
                onf = lambda b_, n=n: DEF.add(("fin", b_), final_chain(n, b_))
            else:
                onf = lambda b_, l=l, n=n: DEF.add(("h", b_), norm_chain(l + 1, n, 0, 1, b_))
            if n == 0 and l + 1 < depth:
                bank_all.items = list(range(5))
                phase_F(l, n, blocks, onf, ada_items(l + 1, 5))
                bank_all.items = list(range(6))
            else:
                phase_F(l, n, blocks, onf)
            if dbg:
                dump("d_x2", xT[:], [128, 8, T], F32, [P.t("x", fc, b_) for fc in range(8) for b_ in range(5)])
                dump("d_mod", mod[:], [128, depth, 48, 3], F32, [P.t("mod", l_) for l_ in range(depth)])
                dump("d_lamt", lamt[:], [128, 16], F32, [P.t("lamt")])
        DEF.flush()
    P.wait_all("sp", [P.t("yout")])
    assert wstate["cur"] == len(TILES), (wstate["cur"], len(TILES))
    P.emit()
    print("sbuf bytes remaining/partition:", nc.sbuf_bytes_remaining, "ops:", {k: len(v) for k, v in P.q.items()})
    return nc


def prepare_shared(inp, depth):
    specs = layer_tile_specs()
    wst = []
    for l in range(depth):
        parts = []
        for (_, ps) in specs:
            for (mname, k0, nk, c0, ncols) in ps:
                parts.append(pack_part(inp[mname][l], k0, nk, c0, ncols))
        wst.append(np.concatenate(parts, axis=1))
    wst = np.ascontiguousarray(np.stack(wst, 0), dtype=np.float32)
    wada = np.stack([np.stack([pack_part(inp["ada_w"][l], 0, 8, jt * 256, 256) for jt in range(24)], 0)
                     for l in range(depth)], 0).astype(np.float32)
    lams = np.stack([inp["lambda_q1"][:depth], inp["lambda_k1"][:depth], inp["lambda_q2"][:depth],
                     inp["lambda_k2"][:depth]], 0)
    if depth < 4:
        lams = np.concatenate([lams, np.zeros((4, 4 - depth, 64), np.float32)], 1)
    lams = np.ascontiguousarray(lams.reshape(1024), dtype=np.float32)
    sgw = np.ascontiguousarray(inp["sg_w"][:depth].transpose(0, 3, 1, 2).reshape(depth, 128, 512), dtype=np.float32)
    sgb = np.ascontiguousarray(inp["sg_b"][:depth].reshape(depth, 512), dtype=np.float32)
    sgg = np.ascontiguousarray(inp["sg_norm_g"][:depth], dtype=np.float32)
    return dict(wst=wst, wada=np.ascontiguousarray(wada), lams=lams, sgw=sgw, sgb=sgb, sgg=sgg,
                rope=rope_tables(), perm=perm_matrix(), ident=np.eye(128, dtype=np.float32))


def cols(v):
    R = v.shape[0]
    k = v.shape[1] // 128
    return v.reshape(R, k, 128).transpose(2, 0, 1).reshape(128, R * k)


def prepare_core(inp, depth, bsel):
    x = inp["x"][bsel]
    ctx = inp["ctx"][bsel]
    nb = len(bsel)
    xc = np.concatenate([ctx, x], axis=1)
    xin = np.ascontiguousarray(xc.reshape(nb, T, 8, 128).transpose(0, 3, 2, 1), dtype=np.float32)
    cs = [inp["c"][b] for b in bsel]
    while len(cs) < 2:
        cs.append(np.zeros(D, np.float32))
    cmat = np.stack(cs + [inp["c_ctx"]], 0)

    def padl(a):
        a = a[:depth]
        if depth < 4:
            a = np.concatenate([a, np.zeros((4 - depth,) + a.shape[1:], a.dtype)], 0)
        return a

    sm = np.concatenate([cols(cmat), cols(padl(inp["norm1_g"])), cols(padl(inp["norm2_g"])),
                         cols(inp["final_g"][None, :]), cols(padl(inp["ada_b"])),
                         cols(padl(inp["subln_g"]))], axis=1)
    return dict(xin=xin, smalls=np.ascontiguousarray(sm, dtype=np.float32))


_PROG_CACHE = {}


def run(inp, depth=4, ncores=NCORES, nb=2, trace=False):
    inp = {k: np.asarray(v, dtype=np.float32) for k, v in inp.items()}
    key = (depth, nb)
    if key not in _PROG_CACHE:
        _PROG_CACHE[key] = build_program(depth, nb)
    nc = _PROG_CACHE[key]
    shared = prepare_shared(inp, depth)
    in_maps = []
    for c in range(ncores):
        m = dict(shared)
        m.update(prepare_core(inp, depth, list(range(c * nb, (c + 1) * nb))))
        in_maps.append(m)
    res = run_bass_kernel_spmd(nc, in_maps, core_ids=list(range(ncores)), trace=trace)
    outs = []
    for c in range(ncores):
        y = res.results[c]["yout"]
        outs.append(np.asarray(y).transpose(0, 3, 2, 1).reshape(nb, SEQ, D))
    return np.ascontiguousarray(np.concatenate(outs, 0), dtype=np.float32), res


def kernel(**inputs):
    out, _ = run(inputs, depth=4, ncores=NCORES, nb=2)
    return out
```
